# Optimizing a Trainium2 kernel written in Bass

```python
import math
import jax, jax.numpy as jnp
from jax import lax
import numpy as np

D_MODEL = 2048
BATCH = 4
SEQ = 4096
DEPTH = 4

N_MIXERS = 3
D_FF = 4 * D_MODEL
D_PLE = 256
NORM_EPS = 1e-6
LN_EPS = 1e-5
NEG = -1e30

NSA_HEADS = 16
NSA_KV_HEADS = 4
NSA_HPG = NSA_HEADS // NSA_KV_HEADS
NSA_HEAD_DIM = D_MODEL // NSA_HEADS
NSA_CMP_LEN = 32
NSA_CMP_STRIDE = 16
NSA_CMP_HIDDEN = NSA_HEAD_DIM
NSA_SEL_LEN = 64
NSA_SEL_TOPK = 16
NSA_SEL_LOCAL = 2
NSA_WINDOW = 512
NSA_Q_BLOCK = 32
NSA_FORCE = 1e9
NSA_IN_WIDTH = NSA_HEADS * NSA_HEAD_DIM + 6 * NSA_KV_HEADS * NSA_HEAD_DIM + 3 * NSA_HEADS

CONV_WIDTH = 31
CONV_INNER = D_MODEL

GLA_HEADS = 4
GLA_DK = D_MODEL // 2 // GLA_HEADS
GLA_DV = D_MODEL // GLA_HEADS
GLA_GATE_RANK = 16
GLA_GATE_NORM = 16.0
GLA_CHUNK = 64
GLA_IN_WIDTH = 2 * GLA_HEADS * GLA_DK + 2 * GLA_HEADS * GLA_DV + GLA_GATE_RANK

kernel_name = "hybrid_nsa_conformer_gla_trunk"


def rms_norm(x, gain):
    xf = x.astype(jnp.float32)
    y = xf * lax.rsqrt(jnp.mean(xf * xf, axis=-1, keepdims=True) + NORM_EPS)
    return (y * gain.astype(jnp.float32)).astype(x.dtype)


def layer_norm(x, gain, bias):
    xf = x.astype(jnp.float32)
    mu = jnp.mean(xf, axis=-1, keepdims=True)
    xc = xf - mu
    y = xc * lax.rsqrt(jnp.mean(xc * xc, axis=-1, keepdims=True) + LN_EPS)
    return (y * gain.astype(jnp.float32) + bias.astype(jnp.float32)).astype(x.dtype)


def masked_softmax(s, valid):
    s = jnp.where(valid, s.astype(jnp.float32), NEG)
    return jax.nn.softmax(s, axis=-1) * valid


def nsa_mixer(h, w_in, w_out, pos_k, w1_k, w2_k, pos_v, w1_v, w2_v):
    B, S, _ = h.shape
    G, HPG, Dh = NSA_KV_HEADS, NSA_HPG, NSA_HEAD_DIM
    kvw = G * Dh
    proj = h @ w_in
    cuts = [NSA_HEADS * Dh + i * kvw for i in range(7)]
    q, kc, vc, ks, vs, kw, vw, gates = jnp.split(proj, cuts, axis=-1)
    q = q.reshape(B, S, G, HPG, Dh) * (Dh ** -0.5)
    kc, vc, ks, vs, kw, vw = (a.reshape(B, S, G, Dh) for a in (kc, vc, ks, vs, kw, vw))
    gates = jax.nn.sigmoid(gates.reshape(B, S, G, HPG, 3))

    n_cmp = (S - NSA_CMP_LEN) // NSA_CMP_STRIDE + 1
    cmp_idx = np.arange(n_cmp)[:, None] * NSA_CMP_STRIDE + np.arange(NSA_CMP_LEN)[None, :]
    cmp_end = cmp_idx[:, -1]

    def compress(kv, pos, w1, w2):
        blk = kv[:, cmp_idx] + pos[None, None, :, None, :]
        blk = blk.transpose(0, 1, 3, 2, 4).reshape(B, n_cmp, G, NSA_CMP_LEN * Dh)
        return jax.nn.silu(blk @ w1) @ w2

    k_cmp = compress(kc, pos_k, w1_k, w2_k)
    v_cmp = compress(vc, pos_v, w1_v, w2_v)

    n_sel = S // NSA_SEL_LEN
    n_top = min(NSA_SEL_TOPK, n_sel)
    cmp_start = np.arange(n_cmp) * NSA_CMP_STRIDE
    sel_start = np.arange(n_sel) * NSA_SEL_LEN
    overlap = jnp.asarray(((cmp_start[:, None] <= sel_start[None, :] + NSA_SEL_LEN - 1)
                           & (cmp_start[:, None] + NSA_CMP_LEN - 1 >= sel_start[None, :])).astype(np.float32))
    ks_blk = ks.reshape(B, n_sel, NSA_SEL_LEN, G, Dh).transpose(0, 3, 1, 2, 4)
    vs_blk = vs.reshape(B, n_sel, NSA_SEL_LEN, G, Dh).transpose(0, 3, 1, 2, 4)
    b_ix = jnp.arange(B)[:, None, None, None]
    g_ix = jnp.arange(G)[None, :, None, None]
    blk_ids = jnp.arange(n_sel)
    tok_off = jnp.arange(NSA_SEL_LEN)

    pad = ((0, 0), (NSA_WINDOW, 0), (0, 0), (0, 0))
    kw_pad = jnp.pad(kw, pad)
    vw_pad = jnp.pad(vw, pad)
    QB = NSA_Q_BLOCK

    def block(start):
        t = start + jnp.arange(QB)
        qb = lax.dynamic_slice_in_dim(q, start, QB, axis=1)
        gb = lax.dynamic_slice_in_dim(gates, start, QB, axis=1)

        valid_c = cmp_end[None, :] <= t[:, None]
        p_c = masked_softmax(jnp.einsum('bqghd,bngd->bghqn', qb, k_cmp), valid_c)
        o_c = jnp.einsum('bghqn,bngd->bqghd', p_c.astype(v_cmp.dtype), v_cmp)

        imp = jnp.einsum('bghqn,nm->bgqm', p_c, overlap)
        diff = (t // NSA_SEL_LEN)[:, None] - blk_ids[None, :]
        forced = (blk_ids[None, :] == 0) | ((diff >= 0) & (diff < NSA_SEL_LOCAL))
        imp = jnp.where(forced, NSA_FORCE, imp)
        imp = jnp.where(diff >= 0, imp, -NSA_FORCE)
        _, sel = lax.top_k(imp, n_top)
        k_sel = ks_blk[b_ix, g_ix, sel]
        v_sel = vs_blk[b_ix, g_ix, sel]
        tok = sel[..., None] * NSA_SEL_LEN + tok_off
        valid_s = (tok <= t[None, None, :, None, None]).reshape(B, G, 1, QB, n_top * NSA_SEL_LEN)
        s_s = jnp.einsum('bqghd,bgqnld->bghqnl', qb, k_sel).reshape(B, G, HPG, QB, n_top * NSA_SEL_LEN)
        p_s = masked_softmax(s_s, valid_s)
        o_s = jnp.einsum('bghqk,bgqkd->bqghd', p_s.astype(v_sel.dtype),
                         v_sel.reshape(B, G, QB, n_top * NSA_SEL_LEN, Dh))

        kwb = lax.dynamic_slice_in_dim(kw_pad, start, QB + NSA_WINDOW, axis=1)
        vwb = lax.dynamic_slice_in_dim(vw_pad, start, QB + NSA_WINDOW, axis=1)
        kpos = start - NSA_WINDOW + jnp.arange(QB + NSA_WINDOW)
        valid_w = ((kpos[None, :] <= t[:, None]) & (kpos[None, :] > t[:, None] - NSA_WINDOW)
                   & (kpos[None, :] >= 0))
        p_w = masked_softmax(jnp.einsum('bqghd,bkgd->bghqk', qb, kwb), valid_w)
        o_w = jnp.einsum('bghqk,bkgd->bqghd', p_w.astype(vwb.dtype), vwb)

        return gb[..., 0, None] * o_c + gb[..., 1, None] * o_s + gb[..., 2, None] * o_w

    out = lax.map(block, jnp.arange(S // QB) * QB)
    out = out.transpose(1, 0, 2, 3, 4, 5).reshape(B, S, NSA_HEADS * Dh)
    return out @ w_out


def conv_mixer(h, w_in, dw, db, ln_g, ln_b, w_out):
    a, g = jnp.split(h @ w_in, 2, axis=-1)
    u = a * jax.nn.sigmoid(g)
    y = lax.conv_general_dilated(
        u, dw[:, None, :], window_strides=(1,), padding=[(CONV_WIDTH - 1, 0)],
        dimension_numbers=('NWC', 'WIO', 'NWC'), feature_group_count=CONV_INNER) + db
    y = jax.nn.silu(layer_norm(y, ln_g, ln_b))
    return y @ w_out


def gla_mixer(h, w_in, w_gate_up, b_gate, norm_g, w_out):
    B, S, _ = h.shape
    H, dk, dv, C = GLA_HEADS, GLA_DK, GLA_DV, GLA_CHUNK
    nC = S // C
    cuts = [H * dk, 2 * H * dk, 2 * H * dk + H * dv, 2 * H * dk + 2 * H * dv]
    q, k, v, r, gz = jnp.split(h @ w_in, cuts, axis=-1)
    glog = jax.nn.log_sigmoid((gz @ w_gate_up + b_gate).astype(jnp.float32)) / GLA_GATE_NORM
    f32 = jnp.float32
    q = q.astype(f32).reshape(B, nC, C, H, dk) * (dk ** -0.5)
    k = k.astype(f32).reshape(B, nC, C, H, dk)
    v = v.astype(f32).reshape(B, nC, C, H, dv)
    bcum = jnp.cumsum(glog.reshape(B, nC, C, H, dk), axis=2)
    blast = bcum[:, :, -1]
    q_dec = q * jnp.exp(bcum)
    k_intra = k * jnp.exp(-bcum)
    k_state = k * jnp.exp(blast[:, :, None] - bcum)
    causal = jnp.asarray(np.tril(np.ones((C, C), dtype=bool)))
    A = jnp.where(causal, jnp.einsum('bncha,bnsha->bnhcs', q_dec, k_intra), 0.0)
    o_intra = jnp.einsum('bnhcs,bnshv->bnchv', A, v)

    def step(state, xs):
        qd, kst, vv, dl = xs
        o = jnp.einsum('bcha,bhav->bchv', qd, state)
        state = state * jnp.exp(dl)[..., None] + jnp.einsum('bcha,bchv->bhav', kst, vv)
        return state, o

    xs = tuple(jnp.moveaxis(a, 1, 0) for a in (q_dec, k_state, v, blast))
    _, o_inter = lax.scan(step, jnp.zeros((B, H, dk, dv), f32), xs)
    o = (o_intra + jnp.moveaxis(o_inter, 0, 1)).reshape(B, S, H, dv)
    o = o * lax.rsqrt(jnp.mean(o * o, axis=-1, keepdims=True) + NORM_EPS) * norm_g.astype(f32)
    o = o.reshape(B, S, H * dv).astype(h.dtype) * jax.nn.silu(r)
    return o @ w_out


def setup_inputs(seed: int = 0) -> dict:
    key = jax.random.key(seed)
    keys = iter(jax.random.split(key, 64))

    def nrm(shape, scale):
        return jax.random.normal(next(keys), shape, jnp.float32) * scale

    def gain(shape):
        return 1.0 + nrm(shape, 0.02)

    n_nsa = len(range(0, DEPTH, N_MIXERS))
    n_conv = len(range(1, DEPTH, N_MIXERS))
    n_gla = len(range(2, DEPTH, N_MIXERS))
    D, Dh, L = D_MODEL, NSA_HEAD_DIM, NSA_CMP_LEN
    return {
        "x": nrm((BATCH, SEQ, D), 1.0),
        "p": nrm((DEPTH, BATCH, SEQ, D_PLE), 1.0),
        "norm_mix": gain((DEPTH, D)),
        "norm_ffn": gain((DEPTH, D)),
        "norm_ple": gain((DEPTH, D)),
        "norm_final": gain((D,)),
        "ffn_w1": nrm((DEPTH, D, D_FF), D ** -0.5),
        "ffn_w2": nrm((DEPTH, D_FF, D), D_FF ** -0.5),
        "ple_w_proj": nrm((DEPTH, D_PLE, D), D_PLE ** -0.5),
        "ple_w_gate": nrm((DEPTH, D, D), D ** -0.5),
        "nsa_w_in": nrm((n_nsa, D, NSA_IN_WIDTH), D ** -0.5),
        "nsa_w_out": nrm((n_nsa, NSA_HEADS * Dh, D), (NSA_HEADS * Dh) ** -0.5),
        "nsa_cmp_pos_k": nrm((n_nsa, L, Dh), 0.02),
        "nsa_cmp_w1_k": nrm((n_nsa, L * Dh, NSA_CMP_HIDDEN), (L * Dh) ** -0.5),
        "nsa_cmp_w2_k": nrm((n_nsa, NSA_CMP_HIDDEN, Dh), NSA_CMP_HIDDEN ** -0.5),
        "nsa_cmp_pos_v": nrm((n_nsa, L, Dh), 0.02),
        "nsa_cmp_w1_v": nrm((n_nsa, L * Dh, NSA_CMP_HIDDEN), (L * Dh) ** -0.5),
        "nsa_cmp_w2_v": nrm((n_nsa, NSA_CMP_HIDDEN, Dh), NSA_CMP_HIDDEN ** -0.5),
        "conv_w_in": nrm((n_conv, D, 2 * CONV_INNER), D ** -0.5),
        "conv_dw": nrm((n_conv, CONV_WIDTH, CONV_INNER), CONV_WIDTH ** -0.5),
        "conv_db": nrm((n_conv, CONV_INNER), 0.01),
        "conv_ln_g": gain((n_conv, CONV_INNER)),
        "conv_ln_b": nrm((n_conv, CONV_INNER), 0.01),
        "conv_w_out": nrm((n_conv, CONV_INNER, D), CONV_INNER ** -0.5),
        "gla_w_in": nrm((n_gla, D, GLA_IN_WIDTH), D ** -0.5),
        "gla_w_gate_up": nrm((n_gla, GLA_GATE_RANK, GLA_HEADS * GLA_DK), GLA_GATE_RANK ** -0.5),
        "gla_b_gate": nrm((n_gla, GLA_HEADS * GLA_DK), 0.01),
        "gla_norm_g": gain((n_gla, GLA_DV)),
        "gla_w_out": nrm((n_gla, GLA_HEADS * GLA_DV, D), (GLA_HEADS * GLA_DV) ** -0.5),
    }


def reference(x, p, norm_mix, norm_ffn, norm_ple, norm_final, ffn_w1, ffn_w2,
              ple_w_proj, ple_w_gate, nsa_w_in, nsa_w_out, nsa_cmp_pos_k, nsa_cmp_w1_k,
              nsa_cmp_w2_k, nsa_cmp_pos_v, nsa_cmp_w1_v, nsa_cmp_w2_v, conv_w_in, conv_dw,
              conv_db, conv_ln_g, conv_ln_b, conv_w_out, gla_w_in, gla_w_gate_up,
              gla_b_gate, gla_norm_g, gla_w_out):
    h = x
    for i in range(DEPTH):
        m, j = i % N_MIXERS, i // N_MIXERS
        hn = rms_norm(h, norm_mix[i])
        if m == 0:
            y = nsa_mixer(hn, nsa_w_in[j], nsa_w_out[j], nsa_cmp_pos_k[j], nsa_cmp_w1_k[j],
                          nsa_cmp_w2_k[j], nsa_cmp_pos_v[j], nsa_cmp_w1_v[j], nsa_cmp_w2_v[j])
        elif m == 1:
            y = conv_mixer(hn, conv_w_in[j], conv_dw[j], conv_db[j], conv_ln_g[j],
                           conv_ln_b[j], conv_w_out[j])
        else:
            y = gla_mixer(hn, gla_w_in[j], gla_w_gate_up[j], gla_b_gate[j], gla_norm_g[j],
                          gla_w_out[j])
        h = h + y
        hn = rms_norm(h, norm_ffn[i])
        h = h + jnp.square(jax.nn.relu(hn @ ffn_w1[i])) @ ffn_w2[i]
        gate = jax.nn.sigmoid(rms_norm(h, norm_ple[i]) @ ple_w_gate[i])
        h = h + (p[i] @ ple_w_proj[i]) * gate
    return rms_norm(h, norm_final)
```

```python
import numpy as np
from contextlib import ExitStack
import concourse.bass as bass
import concourse.mybir as mybir
from concourse.bass_utils import run_bass_kernel_spmd

F32 = mybir.dt.float32
BF16 = mybir.dt.bfloat16
AF = mybir.ActivationFunctionType
ALU = mybir.AluOpType
AX = mybir.AxisListType


class Ev:
    __slots__ = ("sem", "val")

    def __init__(self, sem, val):
        self.sem = sem
        self.val = val


class Buf:
    __slots__ = ("name", "w", "r", "excl")

    def __init__(self, name="", excl=False):
        self.name = name
        self.w = None
        self.r = {}
        self.excl = excl


def PBuf():
    return Buf(excl=True)


def _flat(x, out=None):
    if out is None:
        out = []
    if x is None:
        return out
    if isinstance(x, Buf):
        out.append(x)
    else:
        for y in x:
            _flat(y, out)
    return out


class _Rec:
    def __getattr__(self, name):
        def f(*a, **k):
            return (name, a, k)
        return f


R = _Rec()


def _call(fn):
    if callable(fn):
        return fn
    name, a, k = fn

    def call(e):
        try:
            return getattr(e, name)(*a, **k)
        except Exception:
            print("FAILED OP:", name, [str(x)[:200] for x in a], {kk: str(v)[:200] for kk, v in k.items()}, flush=True)
            raise
    return call


class Prog:
    def __init__(self, nc, stack, ndma=8):
        self.nc = nc
        self._stack = stack
        self.streams = {e: [] for e in ("sync", "act", "pool", "dve", "pe")}
        self.esem = {e: stack.enter_context(nc.semaphore("sem_" + e))
                     for e in ("act", "pool", "dve", "pe")}
        self.ecnt = {e: 0 for e in self.esem}
        self.nd = ndma
        self.dsem = {q: [stack.enter_context(nc.semaphore("dma_%s_%d" % (q, i)))
                         for i in range(ndma)] for q in ("sync", "pool")}
        self.dcnt = {q: [0] * ndma for q in self.dsem}
        self.dnext = {q: 0 for q in self.dsem}
        self.waited = {}
        self.pe_pending = []
        self.ninst = 0

    def _wait(self, eng, ev):
        if ev is None:
            return
        assert ev.val is not None, "wait on unresolved PE event"
        key = (eng, ev.sem.num)
        if self.waited.get(key, 0) >= ev.val:
            return
        self.waited[key] = ev.val
        sem, val = ev.sem, ev.val
        self.streams[eng].append(lambda e: e.wait_ge(sem, val))

    def _deps(self, eng, R, W):
        pesem = self.esem["pe"]
        for b in R:
            ev = b.w
            if ev is not None and not (eng == "pe" and ev.sem is pesem):
                self._wait(eng, ev)
        for b in W:
            ev = b.w
            if ev is not None and not (eng == "pe" and ev.sem is pesem):
                self._wait(eng, ev)
            for ev in b.r.values():
                if not (eng == "pe" and ev.sem is pesem):
                    self._wait(eng, ev)

    def _post(self, ev, R, W):
        k = ev.sem.num
        for b in R:
            b.r[k] = ev
        for b in W:
            b.w = ev
            b.r = {}

    def sync_to(self, eng, bufs):
        self._deps(eng, [], _flat(bufs))

    def op(self, eng, fn, reads=(), writes=(), inc=True):
        Rd = _flat(reads)
        W = _flat(writes)
        if any(b.excl for b in Rd):
            W = W + [b for b in Rd if b.excl]
            Rd = [b for b in Rd if not b.excl]
        fn = _call(fn)
        self._deps(eng, Rd, W)
        sem = self.esem[eng]
        self.ninst += 1
        if inc:
            self.ecnt[eng] += 1
            ev = Ev(sem, self.ecnt[eng])
            self.streams[eng].append(lambda e: fn(e).then_inc(sem, 1))
            if eng == "pe":
                for p in self.pe_pending:
                    p.val = ev.val
                self.pe_pending = []
        else:
            assert eng == "pe"
            ev = Ev(sem, None)
            self.pe_pending.append(ev)
            self.streams[eng].append(lambda e: fn(e))
        self._post(ev, Rd, W)
        return ev

    def dma(self, q, out, in_, reads=(), writes=(), **kw):
        Rd = _flat(reads)
        W = _flat(writes)
        i = self.dnext[q]
        self.dnext[q] = (i + 1) % self.nd
        sem = self.dsem[q][i]
        prev = self.dcnt[q][i]
        if prev > 0:
            self._wait(q, Ev(sem, 16 * prev))
        self._deps(q, Rd, W)
        self.dcnt[q][i] += 1
        ev = Ev(sem, 16 * self.dcnt[q][i])
        self.ninst += 1
        self.streams[q].append(
            lambda e: e.dma_start(out=out, in_=in_, **kw).then_inc(sem, 16))
        self._post(ev, Rd, W)
        return ev

    def finish(self):
        for q in self.dsem:
            for i in range(self.nd):
                if self.dcnt[q][i] > 0:
                    self._wait("sync", Ev(self.dsem[q][i], 16 * self.dcnt[q][i]))
        for e in self.esem:
            if self.ecnt[e] > 0:
                self._wait("sync", Ev(self.esem[e], self.ecnt[e]))
        if hasattr(self, "ccsem") and self.cccnt > 0:
            self._wait("sync", Ev(self.ccsem, self.cccnt))

    def emit(self):
        self.finish()
        st = self.streams
        with self.nc.Block() as block:
            @block.sync
            def _(e):
                for f in st["sync"]:
                    f(e)

            @block.scalar
            def _(e):
                for f in st["act"]:
                    f(e)

            @block.gpsimd
            def _(e):
                for f in st["pool"]:
                    f(e)

            @block.vector
            def _(e):
                for f in st["dve"]:
                    f(e)

            @block.tensor
            def _(e):
                for f in st["pe"]:
                    f(e)


def _prog_cc(self, kind, ins, outs, groups, reads=(), writes=(), stack=None):
    if not hasattr(self, "ccsem"):
        self.ccsem = self._stack.enter_context(self.nc.semaphore("cc_sem"))
        self.cccnt = 0
    Rd = _flat(reads)
    W = _flat(writes)
    sem = self.ccsem
    if self.cccnt > 0:
        self._wait("pool", Ev(sem, self.cccnt))
    self._deps("pool", Rd, W)
    self.cccnt += self.CC_INC
    ev = Ev(sem, self.cccnt)
    inc = self.CC_INC
    self.streams["pool"].append(lambda e: e.collective_compute(
        kind, ALU.bypass, replica_groups=groups, ins=ins, outs=outs).then_inc(sem, inc))
    self._post(ev, Rd, W)
    return ev


Prog.cc = _prog_cc
Prog.CC_INC = 1


def _prog_barrier(self):
    evs = []
    for e in self.esem:
        if self.ecnt[e] > 0:
            evs.append(Ev(self.esem[e], self.ecnt[e]))
    for q in self.dsem:
        for i in range(self.nd):
            if self.dcnt[q][i] > 0:
                evs.append(Ev(self.dsem[q][i], 16 * self.dcnt[q][i]))
    assert not self.pe_pending
    for eng in self.streams:
        for ev in evs:
            self._wait(eng, ev)


Prog.barrier = _prog_barrier


def _prog_wait_all(self, ev):
    for eng in self.streams:
        self._wait(eng, ev)


Prog.wait_all = _prog_wait_all


_uid = [0]

D = 2048
DFF = 8192
T = 1024
TT = T // 512
NTOK = 2048
NG = NTOK // T
CONV_W = 31
HALO = 30


class Ctx:
    pass


def _dram(nc, name, shape, dt, kind):
    return nc.dram_tensor(name, list(shape), dt, kind=kind).ap()


def emit_T(P, nc, io, pre, tail, post, F=0, F32TAIL=0):
    hT_in = io["hT"]
    if pre in ("linear", "gla"):
        mixX = io["mixX"]; w_out = io["w_out"]
        if pre == "gla":
            rT = io["rT"]
    elif pre == "conv":
        uT = io["uT"]; haloX = io["haloX"]; dwT = io["dwT"]; cvec = io["cvec"]; w_out = io["w_out"]
    if tail:
        w1 = io["w1"]; w2 = io["w2"]; wg = io["wg"]; wp = io["wp"]; pT = io["pT"]
    gains = io["gains"]
    if post in ("proj", "glu"):
        w_in = io["w_in"]; projT = io["projT"]; hT_out = io["hT_out"]
        if F32TAIL:
            tail32 = io["tail32"]
    else:
        outT = io["outT"]

    _uid[0] += 1
    st = ExitStack()
    with st:
        def A(name, shape, dt):
            return st.enter_context(nc.sbuf_tensor("T_%d_" % _uid[0] + name, shape, dt))

        def PSA(name, shape, dt=F32):
            return st.enter_context(nc.psum_tensor("T_%d_" % _uid[0] + name, shape, dt))
        h32 = A("h32", [128, 16, T], F32); hB = [Buf() for _ in range(16)]
        xb = A("xb", [128, 16, T], BF16); xbB = [Buf() for _ in range(16)]
        ab = A("ab", [128, 16, T], BF16); abB = [Buf() for _ in range(16)]
        NW = 2
        wt32 = [A("wt32_%d" % i, [128, 16, 256], F32) for i in range(NW)]; wt32B = [Buf() for _ in range(NW)]
        wtb = [A("wtb_%d" % i, [128, 16, 256], BF16) for i in range(NW)]; wtbB = [Buf() for _ in range(NW)]
        scr = [wt32[i][:].rearrange("p a b -> p (a b)")[:, 0:T] for i in range(NW)]; scrB = wt32B
        rstd = A("rstd", [128, T], F32); rstdB = Buf()
        tmpa = A("tmpa", [128, 4, 512], F32); tmp = [tmpa[:, i, :] for i in range(4)]; tmpB = [Buf() for _ in range(4)]
        obf = [A("obf%d" % i, [128, T], BF16) for i in range(2)]; obfB = [Buf(), Buf()]
        gn = A("gn", [128, 3, 16], F32); gnB = Buf()
        sel = A("sel", [128, 2], F32); selB = Buf()
        ones = A("ones", [128, 128], F32); onesB = Buf()
        NPS = 2
        ps_mm = [[PSA("psmm%d_%d" % (i, t), [128, 512]) for t in range(TT)] for i in range(NPS)]
        ps_mmB = [[PBuf() for t in range(TT)] for i in range(NPS)]
        ps_st = [PSA("psst%d" % t, [128, 512]) for t in range(2 * TT)]
        ps_stB = [PBuf() for t in range(2 * TT)]
        if tail:
            pb = A("pb", [128, 2, T], BF16); pbB = [Buf(), Buf()]
        if pre == "conv":
            identb = A("identb", [128, 128], BF16); identB = Buf()
            ident32 = A("ident32", [128, 128], F32)
            dw_sb = A("dw_sb", [128, 16, CONV_W], F32); dwB = Buf()
            cv_sb = A("cv_sb", [128, 3, 16], F32); cvB = Buf()
            abf = ab[:].rearrange("p a b -> p (a b)")
            NDJ = CONV_W * 128
            Dj = [abf[:, i * 8192: i * 8192 + NDJ].rearrange("p (j m) -> p j m", j=CONV_W) for i in range(2)]
            ub = [abf[:, i * 8192 + NDJ: i * 8192 + NDJ + HALO + T] for i in range(2)]
            DjB = [abB[0:8], abB[8:16]]
            DjJ = [[Buf() for _ in range(CONV_W)] for _ in range(2)]; ubB = [Buf(), Buf()]
            mean = tmpa[:, 0:2, :].rearrange("p a t -> p (a t)"); meanB = tmpB[0:2]

        C = Ctx(); C.wi = 0; C.pi = 0; C.ci = 0; C.ti = 0; C.oi = 0

        P.dma("sync", gn[:], gains, writes=gnB)
        P.dma("sync", sel[:], io["sel"], writes=selB)
        P.op("pool", R.memset(ones[:], 1.0), writes=onesB)
        if pre == "conv":
            P.dma("sync", dw_sb[:], dwT, writes=dwB)
            P.dma("sync", cv_sb[:], cvec, writes=cvB)
            P.op("pool", R.memset(ident32[:], 1.0), writes=identB)
            P.op("pool", R.affine_select(out=ident32[:], in_=ident32[:], pattern=[[-1, 128]],
                                         compare_op=ALU.is_equal, fill=0.0, base=0, channel_multiplier=1),
                 reads=identB, writes=identB)
            P.op("dve", R.tensor_copy(out=identb[:], in_=ident32[:]), reads=identB, writes=identB)

        cast_rot = ["act", "dve", "pool", "dve"]

        def cast(eng, out, in_, reads, writes):
            if eng == "act":
                P.op("act", R.copy(out=out, in_=in_), reads=reads, writes=writes)
            else:
                P.op(eng, R.tensor_copy(out=out, in_=in_), reads=reads, writes=writes)

        def linear(xt, xB, KC, Wd, r0, c0, ncols, epi, WB=None):
            nblk = (ncols + 255) // 256
            for blk in range(nblk):
                cw = min(256, ncols - blk * 256)
                wi = C.wi % NW; C.wi += 1
                w32, wb_ = wt32[wi], wtb[wi]
                src = Wd[r0:r0 + KC * 128, c0 + blk * 256: c0 + blk * 256 + cw].rearrange("(kc p) c -> p kc c", p=128)
                P.dma("sync", w32[:, 0:KC, 0:cw], src, reads=WB, writes=wt32B[wi])
                ce = cast_rot[C.ci % len(cast_rot)]; C.ci += 1
                cast(ce, wb_[:, 0:KC, 0:cw], w32[:, 0:KC, 0:cw], wt32B[wi], wtbB[wi])
                for j in range((cw + 127) // 128):
                    m = min(128, cw - j * 128)
                    pi = C.pi % NPS; C.pi += 1
                    for kc in range(KC):
                        for tt in range(TT):
                            P.op("pe", R.matmul(ps_mm[pi][tt][0:m, :], lhsT=wb_[:, kc, j * 128:j * 128 + m],
                                                rhs=xt[:, kc, tt * 512:(tt + 1) * 512], start=(kc == 0), stop=(kc == KC - 1)),
                                 reads=[wtbB[wi], xB[kc]], writes=ps_mmB[pi][tt], inc=(kc == KC - 1))
                    epi(blk * 2 + j, m, ps_mm[pi], ps_mmB[pi])

        def sl_(tt):
            return slice(tt * 512, (tt + 1) * 512)

        def norm(gi, out_bf=True, eps=1e-6):
            for kc in range(16):
                s = kc % 2
                P.op("act", R.activation(out=scr[s], in_=h32[:, kc, :], func=AF.Square), reads=hB[kc], writes=scrB[s])
                for tt in range(TT):
                    P.op("pe", R.matmul(ps_st[tt][:], lhsT=ones[:], rhs=scr[s][:, sl_(tt)], start=(kc == 0), stop=(kc == 15)),
                         reads=[onesB, scrB[s]], writes=ps_stB[tt])
            for tt in range(TT):
                P.op("dve", R.tensor_scalar(out=rstd[:, sl_(tt)], in0=ps_st[tt][:], scalar1=1.0 / D, scalar2=eps,
                                            op0=ALU.mult, op1=ALU.add), reads=ps_stB[tt], writes=rstdB)
            P.op("act", R.activation(out=rstd[:], in_=rstd[:], func=AF.Sqrt), reads=rstdB, writes=rstdB)
            P.op("dve", R.reciprocal(out=rstd[:], in_=rstd[:]), reads=rstdB, writes=rstdB)
            if out_bf:
                for kc in range(16):
                    P.op("dve", R.scalar_tensor_tensor(out=xb[:, kc, :], in0=h32[:, kc, :], scalar=gn[:, gi, kc:kc + 1],
                                                       in1=rstd[:], op0=ALU.mult, op1=ALU.mult),
                         reads=[hB[kc], gnB, rstdB], writes=xbB[kc])

        def epi_add_h(j, m, ps, psB):
            for tt in range(TT):
                P.op("dve", R.tensor_tensor(out=h32[:, j, sl_(tt)], in0=h32[:, j, sl_(tt)], in1=ps[tt][:], op=ALU.add),
                     reads=[hB[j], psB[tt]], writes=hB[j])

        for g in range(NG):
            tok = slice(g * T, (g + 1) * T)
            hview = hT_in.rearrange("(c p) t -> p c t", p=128)

            if pre == "conv":
                for cc in range(16):
                    di = cc % 2
                    for en in ("dve", "act", "sync"):
                        P.sync_to(en, DjB[di])
                    for j in range(CONV_W):
                        if j % 2 == 0:
                            P.op("dve", R.tensor_scalar(out=Dj[di][:, j, :], in0=identb[:], scalar1=dw_sb[:, cc, j:j + 1],
                                                        scalar2=None, op0=ALU.mult), reads=[identB, dwB], writes=DjJ[di][j])
                        else:
                            P.op("act", R.activation(out=Dj[di][:, j, :], in_=identb[:], func=AF.Identity,
                                                     scale=dw_sb[:, cc, j:j + 1]), reads=[identB, dwB], writes=DjJ[di][j])
                    if g == 0:
                        P.dma("sync", ub[di][:, 0:HALO], haloX[0, cc * 128:(cc + 1) * 128, 2:2 + HALO], writes=ubB[di])
                        P.dma("sync", ub[di][:, HALO:HALO + T], uT[cc * 128:(cc + 1) * 128, 0:T], writes=ubB[di])
                        P.op("dve", R.tensor_scalar(out=ub[di][:, 0:HALO], in0=ub[di][:, 0:HALO], scalar1=sel[:, 1:2], scalar2=None, op0=ALU.mult),
                             reads=[ubB[di], selB], writes=ubB[di])
                    else:
                        P.dma("sync", ub[di], uT[cc * 128:(cc + 1) * 128, g * T - HALO: g * T + T], writes=ubB[di])
                    pi = C.pi % NPS; C.pi += 1
                    for j in range(CONV_W):
                        for tt in range(TT):
                            lastj = (j == CONV_W - 1)
                            P.op("pe", R.matmul(ps_mm[pi][tt][:], lhsT=Dj[di][:, j, :],
                                                rhs=ub[di][:, j + tt * 512: j + tt * 512 + 512],
                                                start=(j == 0), stop=lastj),
                                 reads=[DjJ[di][j], ubB[di]] + (DjB[di] if lastj else []), writes=ps_mmB[pi][tt], inc=lastj)
                    for tt in range(TT):
                        P.op("act", R.activation(out=h32[:, cc, sl_(tt)], in_=ps_mm[pi][tt][:], func=AF.Identity,
                                                 bias=cv_sb[:, 0, cc:cc + 1]), reads=[ps_mmB[pi][tt], cvB], writes=hB[cc])
                for kc in range(16):
                    s = kc % 2
                    P.op("act", R.activation(out=scr[s], in_=h32[:, kc, :], func=AF.Square), reads=hB[kc], writes=scrB[s])
                    for tt in range(TT):
                        P.op("pe", R.matmul(ps_st[tt][:], lhsT=ones[:], rhs=scr[s][:, sl_(tt)], start=(kc == 0), stop=(kc == 15)),
                             reads=[onesB, scrB[s]], writes=ps_stB[tt])
                        P.op("pe", R.matmul(ps_st[TT + tt][:], lhsT=ones[:], rhs=h32[:, kc, sl_(tt)], start=(kc == 0), stop=(kc == 15)),
                             reads=[onesB, hB[kc]], writes=ps_stB[TT + tt])
                for tt in range(TT):
                    P.op("dve", R.tensor_scalar(out=mean[:, sl_(tt)], in0=ps_st[TT + tt][:], scalar1=1.0 / D, scalar2=None, op0=ALU.mult),
                         reads=ps_stB[TT + tt], writes=meanB)
                    P.op("dve", R.tensor_tensor(out=scr[0][:, sl_(tt)], in0=mean[:, sl_(tt)], in1=mean[:, sl_(tt)], op=ALU.mult), reads=meanB, writes=scrB[0])
                    P.op("dve", R.scalar_tensor_tensor(out=rstd[:, sl_(tt)], in0=ps_st[tt][:], scalar=1.0 / D, in1=scr[0][:, sl_(tt)],
                                                       op0=ALU.mult, op1=ALU.subtract), reads=[ps_stB[tt], scrB[0]], writes=rstdB)
                P.op("dve", R.tensor_scalar(out=rstd[:], in0=rstd[:], scalar1=1e-5, scalar2=None, op0=ALU.add), reads=rstdB, writes=rstdB)
                P.op("act", R.activation(out=rstd[:], in_=rstd[:], func=AF.Sqrt), reads=rstdB, writes=rstdB)
                P.op("dve", R.reciprocal(out=rstd[:], in_=rstd[:]), reads=rstdB, writes=rstdB)
                for kc in range(16):
                    s = kc % 2
                    P.op("dve", R.tensor_tensor(out=scr[s], in0=h32[:, kc, :], in1=mean, op=ALU.subtract),
                         reads=[hB[kc], meanB], writes=scrB[s])
                    P.op("dve", R.tensor_tensor(out=scr[s], in0=scr[s], in1=rstd[:], op=ALU.mult), reads=[scrB[s], rstdB], writes=scrB[s])
                    P.op("act", R.activation(out=xb[:, kc, :], in_=scr[s], func=AF.Silu, scale=cv_sb[:, 1, kc:kc + 1],
                                             bias=cv_sb[:, 2, kc:kc + 1]), reads=[scrB[s], cvB], writes=xbB[kc])
            elif pre in ("linear", "gla"):
                for c4 in range(4):
                    rk, f0 = divmod(c4 * 512, 1024)
                    for cand, (dst, dstB) in enumerate(((xb, xbB), (ab, abB))):
                        src = mixX[rk, f0:f0 + 512, cand * NTOK + g * T: cand * NTOK + (g + 1) * T].rearrange("(c p) t -> p c t", p=128)
                        P.dma("sync", dst[:, c4 * 4:(c4 + 1) * 4, :], src, writes=dstB[c4 * 4:(c4 + 1) * 4])
                for kc in range(16):
                    P.op("dve", R.tensor_scalar(out=xb[:, kc, :], in0=xb[:, kc, :], scalar1=sel[:, 0:1], scalar2=None, op0=ALU.mult),
                         reads=[xbB[kc], selB], writes=xbB[kc])
                    P.op("dve", R.scalar_tensor_tensor(out=xb[:, kc, :], in0=ab[:, kc, :], scalar=sel[:, 1:2], in1=xb[:, kc, :], op0=ALU.mult, op1=ALU.add),
                         reads=[abB[kc], xbB[kc], selB], writes=xbB[kc])
                if pre == "gla":
                    rview = rT.rearrange("(c p) t -> p c t", p=128)
                    for c4 in range(4):
                        P.dma("sync", ab[:, c4 * 4:(c4 + 1) * 4, :], rview[:, c4 * 4:(c4 + 1) * 4, tok], writes=abB[c4 * 4:(c4 + 1) * 4])
                    for kc in range(16):
                        P.op("act", R.activation(out=ab[:, kc, :], in_=ab[:, kc, :], func=AF.Silu), reads=abB[kc], writes=abB[kc])
                        P.op("dve", R.tensor_tensor(out=xb[:, kc, :], in0=xb[:, kc, :], in1=ab[:, kc, :], op=ALU.mult),
                             reads=[xbB[kc], abB[kc]], writes=xbB[kc])

            for c4 in range(4):
                P.dma("sync", h32[:, c4 * 4:(c4 + 1) * 4, :], hview[:, c4 * 4:(c4 + 1) * 4, tok], writes=hB[c4 * 4:(c4 + 1) * 4])

            if pre is not None:
                linear(xb, xbB, 16, w_out, 0, 0, D, epi_add_h, io.get('w_outB'))

            if tail:
                norm(0)
                for qd in range(4):
                    def epi_ffn1(j, m, ps, psB):
                        for tt in range(TT):
                            ti = C.ti % 4; C.ti += 1
                            P.op("act", R.activation(out=tmp[ti], in_=ps[tt][:], func=AF.Relu), reads=psB[tt], writes=tmpB[ti])
                            P.op("act", R.activation(out=ab[:, j, sl_(tt)], in_=tmp[ti], func=AF.Square), reads=tmpB[ti], writes=abB[j])
                    linear(xb, xbB, 16, w1, 0, qd * 2048, 2048, epi_ffn1, io.get('w1B'))
                    linear(ab, abB, 16, w2, qd * 2048, 0, D, epi_add_h, io.get('w2B'))
                norm(1)

                def epi_gate(j, m, ps, psB):
                    for tt in range(TT):
                        P.op("act", R.activation(out=ab[:, j, sl_(tt)], in_=ps[tt][:], func=AF.Sigmoid), reads=psB[tt], writes=abB[j])
                linear(xb, xbB, 16, wg, 0, 0, D, epi_gate, io.get('wgB'))
                wi = C.wi % NW; C.wi += 1
                p32 = wt32[wi][:].rearrange("p a b -> p (a b)")[:, 0:2 * T].rearrange("p (c t) -> p c t", c=2)
                P.dma("sync", p32, pT.rearrange("(c p) t -> p c t", p=128)[:, :, tok], writes=wt32B[wi])
                for c in range(2):
                    P.op("pool", R.tensor_copy(out=pb[:, c, :], in_=p32[:, c, :]), reads=wt32B[wi], writes=pbB[c])

                def epi_ple(j, m, ps, psB):
                    for tt in range(TT):
                        ti = C.ti % 4; C.ti += 1
                        P.op("dve", R.tensor_tensor(out=tmp[ti], in0=ps[tt][:], in1=ab[:, j, sl_(tt)], op=ALU.mult),
                             reads=[psB[tt], abB[j]], writes=tmpB[ti])
                        P.op("dve", R.tensor_tensor(out=h32[:, j, sl_(tt)], in0=h32[:, j, sl_(tt)], in1=tmp[ti], op=ALU.add),
                             reads=[hB[j], tmpB[ti]], writes=hB[j])
                linear(pb, pbB, 2, wp, 0, 0, D, epi_ple, io.get('wpB'))

            if post in ("proj", "glu"):
                oview = hT_out.rearrange("(c p) t -> p c t", p=128)
                for c4 in range(4):
                    P.dma("pool", oview[:, c4 * 4:(c4 + 1) * 4, tok], h32[:, c4 * 4:(c4 + 1) * 4, :], reads=hB[c4 * 4:(c4 + 1) * 4])
                norm(2)
                if post == "proj":
                    nfull = F // 128

                    def epi_store(j, m, ps, psB):
                        if j == nfull:
                            wi = C.wi % NW; C.wi += 1
                            dst_t = scr[wi]; dB = scrB[wi]
                        else:
                            oi = C.oi % 2; C.oi += 1
                            dst_t = obf[oi][:]; dB = obfB[oi]
                        for tt in range(TT):
                            if tt % 2 == 0:
                                P.op("act", R.copy(out=dst_t[0:m, sl_(tt)], in_=ps[tt][0:m, :]), reads=psB[tt], writes=dB)
                            else:
                                P.op("dve", R.tensor_copy(out=dst_t[0:m, sl_(tt)], in_=ps[tt][0:m, :]), reads=psB[tt], writes=dB)
                        if j == nfull:
                            P.dma("pool", tail32[0:m, tok], dst_t[0:m, :], reads=dB)
                        else:
                            P.dma("pool", projT[j * 128:j * 128 + m, tok], dst_t[0:m, :], reads=dB)
                    linear(xb, xbB, 16, w_in, 0, 0, F, epi_store, io.get('w_inB'))
                else:
                    def epi_glu(j, m, ps, psB):
                        if j % 2 == 0:
                            C.glu_t = [(C.ti + k) % 4 for k in range(TT)]; C.ti += TT
                            for tt in range(TT):
                                ti = C.glu_t[tt]
                                P.op("act", R.activation(out=tmp[ti], in_=ps[tt][:], func=AF.Sigmoid), reads=psB[tt], writes=tmpB[ti])
                        else:
                            oi = C.oi % 2; C.oi += 1
                            for tt in range(TT):
                                ti = C.glu_t[tt]
                                P.op("dve", R.tensor_tensor(out=obf[oi][:, sl_(tt)], in0=ps[tt][:], in1=tmp[ti], op=ALU.mult),
                                     reads=[psB[tt], tmpB[ti]], writes=obfB[oi])
                            fc = j // 2
                            P.dma("pool", projT[fc * 128:(fc + 1) * 128, tok], obf[oi][:], reads=obfB[oi])
                    linear(xb, xbB, 16, w_in, 0, 0, F, epi_glu, io.get('w_inB'))
            else:
                norm(2, out_bf=False)
                oview = outT.rearrange("(c p) t -> p c t", p=128)
                for kc in range(16):
                    wi = C.wi % NW; C.wi += 1
                    P.op("dve", R.scalar_tensor_tensor(out=scr[wi], in0=h32[:, kc, :], scalar=gn[:, 2, kc:kc + 1], in1=rstd[:],
                                                       op0=ALU.mult, op1=ALU.mult), reads=[hB[kc], gnB, rstdB], writes=scrB[wi])
                    P.dma("pool", oview[:, kc, tok], scr[wi], reads=scrB[wi])
    P.barrier()


_uid = [0]

S = 4096
NQT = 32
SCALE = 128 ** -0.5
NEGB = 30000.0


def emit_nsa(P, nc, projX, gateX, cw, attnT, selD):
    _uid[0] += 1
    st = ExitStack()
    with st:
        def A(name, shape, dt):
            return st.enter_context(nc.sbuf_tensor("nsa_%d_" % _uid[0] + name, shape, dt))

        def PS(name, shape, dt=F32):
            return st.enter_context(nc.psum_tensor("nsa_%d_" % _uid[0] + name, shape, dt))

        q_sb = A("q", [128, 4, S], BF16); qB = Buf()
        kc_sb = A("kc", [128, S], BF16); kcB = Buf()
        vc_sb = A("vc", [128, S], BF16); vcB = Buf()
        ks_sb = A("ks", [128, S], BF16); ksB = Buf()
        kw_sb = A("kw", [128, S], BF16); kwB = Buf()
        vT_sb = A("vT", [128, S], BF16); vTB = Buf()
        vs_e = A("vse", [128, NQT, 129], BF16); vsB = Buf()
        vw_e = A("vwe", [128, NQT, 129], BF16); vwB = Buf()
        gT_sb = A("gT", [12, S], F32); gTB = Buf()
        gstg = A("gstg", [12, S], F32); gstgB = Buf()
        stg = A("stg", [128, S], BF16); stgB = Buf()
        sel = A("sel", [128, 2], F32); selB = Buf()
        g_sb = A("g", [128, NQT, 12], F32); gB = Buf()
        w1s = A("w1s", [128, 32, 128], F32); w1sB = Buf()
        w1b = [A("w1b%d" % i, [128, 32, 128], BF16) for i in range(2)]; w1bB = [Buf(), Buf()]
        w2s = A("w2s", [128, 128], F32); w2sB = Buf()
        w2b = [A("w2b%d" % i, [128, 128], BF16) for i in range(2)]; w2bB = [Buf(), Buf()]
        pos_s = A("poss", [128, 32], F32); possB = Buf()
        posb = [A("posb%d" % i, [128, 32], BF16) for i in range(2)]; posbB = [Buf(), Buf()]
        cb = [A("cb%d" % i, [128, 1], F32) for i in range(2)]; cbB = [Buf(), Buf()]
        hid = A("hid", [128, 256], BF16); hidB = Buf()
        kcmpT = A("kcmpT", [128, 256], BF16); kcmpB = Buf()
        vcmp = A("vcmp", [128, 2, 193], BF16); vcmpB = Buf()
        ident32 = A("ident32", [128, 128], F32); idB = Buf()
        identb = A("identb", [128, 128], BF16)
        E32 = A("E32", [64, S], F32)
        Eb = A("Eb", [64, S], BF16); EB = Buf()
        ov32 = A("ov32", [128, 2, 64], F32); ovB = Buf()
        pt = [A("pt%d" % i, [128, 512], BF16) for i in range(3)]; ptB = [Buf() for _ in range(3)]
        imp = A("imp", [128, 64], F32); impB = Buf()
        wk = A("wk", [128, 64], F32); wkB = Buf()
        m8 = A("m8", [128, 8], F32); m8B = Buf()
        negm = A("negm", [128, 64], F32); negmB = Buf()
        ind = A("ind", [128, 3, 64], F32); indB = [Buf(), Buf(), Buf()]
        addm = A("addm", [128, 64], F32); addB = Buf()
        ones64 = A("ones64", [128, 64], F32); onesB = Buf()
        negT4 = A("negT4", [64, 4, 128], BF16); negTB = Buf()
        den = A("den", [128, 3, 4], F32); denB = [Buf(), Buf(), Buf()]
        coef = A("coef", [128, 3, 4], F32); coefB = [Buf(), Buf(), Buf()]
        o32 = A("o32", [128, 4, 128], F32); o32B = Buf()
        oT = [A("oT%d" % i, [128, 4, 128], BF16) for i in range(2)]; oTB = [Buf(), Buf()]

        ps_s = [PS("s%d" % i, [128, 512]) for i in range(2)]; ps_sB = [PBuf(), PBuf()]
        ps_c = [PS("c%d" % i, [128, 512]) for i in range(2)]; ps_cB = [PBuf(), PBuf()]
        ps_e = [PS("e%d" % i, [128, 512]) for i in range(2)]; ps_eB = [PBuf(), PBuf()]
        ps_w = [PS("w%d" % i, [128, 512]) for i in range(2)]; ps_wB = [PBuf(), PBuf()]

        C = type("C", (), {})(); C.si = 0; C.pi = 0; C.oi = 0

        P.dma("sync", sel[:], selD, writes=selB)
        P.op("pool", R.memset(ident32[:], 1.0), writes=idB)
        P.op("pool", R.affine_select(out=ident32[:], in_=ident32[:], pattern=[[-1, 128]], compare_op=ALU.is_equal,
                                     fill=0.0, base=0, channel_multiplier=1), reads=idB, writes=idB)
        P.op("dve", R.tensor_copy(out=identb[:], in_=ident32[:]), reads=idB, writes=idB)
        P.op("pool", R.memset(ones64[:], 1.0), writes=onesB)
        P.op("pool", R.memset(E32[:], 1.0), writes=EB)
        P.op("pool", R.affine_select(out=E32[:], in_=E32[:], pattern=[[1, S]], compare_op=ALU.is_ge, fill=0.0,
                                     base=0, channel_multiplier=-64), reads=EB, writes=EB)
        P.op("pool", R.affine_select(out=E32[:], in_=E32[:], pattern=[[-1, S]], compare_op=ALU.is_ge, fill=0.0,
                                     base=63, channel_multiplier=64), reads=EB, writes=EB)
        P.op("dve", R.tensor_copy(out=Eb[:], in_=E32[:]), reads=EB, writes=EB)
        P.op("pool", R.memset(ov32[:], 1.0), writes=ovB)
        for c in range(2):
            P.op("pool", R.affine_select(out=ov32[:, c, :], in_=ov32[:, c, :], pattern=[[-4, 64]], compare_op=ALU.is_ge, fill=0.0,
                                         base=128 * c + 1, channel_multiplier=1), reads=ovB, writes=ovB)
            P.op("pool", R.affine_select(out=ov32[:, c, :], in_=ov32[:, c, :], pattern=[[4, 64]], compare_op=ALU.is_ge, fill=0.0,
                                         base=3 - 128 * c, channel_multiplier=-1), reads=ovB, writes=ovB)
        P.op("pool", R.memset(vcmp[:], 0.0), writes=vcmpB)
        P.op("dve", R.tensor_copy(out=vcmp[:, :, 129:193], in_=ov32[:]), reads=ovB, writes=vcmpB)
        P.op("pool", R.memset(vcmp[:, :, 128:129], 1.0), writes=vcmpB)
        P.op("pool", R.memset(vs_e[:, :, 128:129], 1.0), writes=vsB)
        P.op("pool", R.memset(vw_e[:, :, 128:129], 1.0), writes=vwB)
        P.op("pool", R.memset(hid[:], 0.0), writes=hidB)
        P.op("pool", R.memset(kcmpT[:], 0.0), writes=kcmpB)
        for i, (w1n, w2n, pn) in enumerate((("w1k", "w2k", "posTk"), ("w1v", "w2v", "posTv"))):
            P.dma("sync", w1s[:], cw[w1n].rearrange("(l d) j -> d l j", d=128), reads=cw.get(w1n + "B"), writes=w1sB)
            P.op("dve", R.tensor_copy(out=w1b[i][:], in_=w1s[:]), reads=w1sB, writes=w1bB[i])
            P.dma("sync", w2s[:], cw[w2n], writes=w2sB)
            P.op("dve", R.tensor_copy(out=w2b[i][:], in_=w2s[:]), reads=w2sB, writes=w2bB[i])
            P.dma("sync", pos_s[:], cw[pn], writes=possB)
            P.op("dve", R.tensor_copy(out=posb[i][:], in_=pos_s[:]), reads=possB, writes=posbB[i])
            for l in range(32):
                P.op("pe", R.matmul(ps_s[0][:, 0:1], lhsT=w1b[i][:, l, :], rhs=posb[i][:, l:l + 1], start=(l == 0), stop=(l == 31)),
                     reads=[w1bB[i], posbB[i]], writes=ps_sB[0], inc=(l == 31))
            P.op("dve", R.tensor_copy(out=cb[i][:], in_=ps_s[0][:, 0:1]), reads=ps_sB[0], writes=cbB[i])

        def sview(t):
            return t.rearrange("p (r t) -> p r t", r=2)

        def load_sel(dst, dstB, src, base, np_=128, stage=None, stageB=None):
            stage = stg if stage is None else stage
            stageB = stgB if stageB is None else stageB
            for cand, (tgt, tgtB) in enumerate(((dst, dstB), (stage[0:np_, :], stageB))):
                if isinstance(src, list):
                    g_ = cand * 2560 + base
                    r0_, r1_, ap_ = [p_ for p_ in src if p_[0] <= g_ < p_[1]][0]
                    sap = ap_[:, g_ - r0_:g_ - r0_ + np_, :]
                else:
                    sap = src[:, cand * 24 + base:cand * 24 + base + np_, :]
                P.dma("sync", sview(tgt), sap.rearrange("r p t -> p r t"), writes=tgtB)
            P.op("dve", R.tensor_scalar(out=dst, in0=dst, scalar1=sel[0:np_, 0:1], scalar2=None, op0=ALU.mult), reads=[dstB, selB], writes=dstB)
            P.op("dve", R.scalar_tensor_tensor(out=dst, in0=stage[0:np_, :], scalar=sel[0:np_, 1:2], in1=dst, op0=ALU.mult, op1=ALU.add),
                 reads=[stageB, dstB, selB], writes=dstB)

        for gl in range(2):
            for hh in range(4):
                load_sel(q_sb[:, hh, :], qB, projX, (gl * 4 + hh) * 128)
            load_sel(kc_sb[:], kcB, projX, 1024 + gl * 128)
            load_sel(vc_sb[:], vcB, projX, 1280 + gl * 128)
            load_sel(ks_sb[:], ksB, projX, 1536 + gl * 128)
            load_sel(kw_sb[:], kwB, projX, 2048 + gl * 128)
            load_sel(gT_sb[:], gTB, gateX, gl * 12, np_=12, stage=gstg, stageB=gstgB)
            for (ve, veB, base) in ((vs_e, vsB, 1792), (vw_e, vwB, 2304)):
                load_sel(vT_sb[:], vTB, projX, base + gl * 128)
                for k4 in range(NQT // 4):
                    pst = ps_c[k4 % 2]
                    pv = pst[:].bitcast(BF16)[:, 0:512].rearrange("p (a d) -> p a d", a=4)
                    for a in range(4):
                        kt = k4 * 4 + a
                        P.op("pe", R.transpose(pv[:, a, :], vT_sb[:, kt * 128:(kt + 1) * 128], identb[:]),
                             reads=[vTB, idB], writes=ps_cB[k4 % 2], inc=(a == 3))
                    P.op("dve" if k4 % 2 == 0 else "act",
                         (R.tensor_copy if k4 % 2 == 0 else R.copy)(out=ve[:, k4 * 4:(k4 + 1) * 4, 0:128], in_=pv),
                         reads=ps_cB[k4 % 2], writes=veB)
            for k4 in range(NQT // 4):
                pst = ps_e[k4 % 2]
                for a in range(4):
                    kt = k4 * 4 + a
                    P.op("pe", R.transpose(pst[:, a * 12:(a + 1) * 12], gT_sb[0:12, kt * 128:(kt + 1) * 128], ident32[0:12, 0:12]),
                         reads=[gTB, idB], writes=ps_eB[k4 % 2], inc=(a == 3))
                P.op("act", R.activation(out=g_sb[:, k4 * 4:(k4 + 1) * 4, :], in_=pst[:, 0:48].rearrange("p (a c) -> p a c", a=4), func=AF.Sigmoid),
                     reads=ps_eB[k4 % 2], writes=gB)
            for i, (src, srcB) in enumerate(((kc_sb, kcB), (vc_sb, vcB))):
                v3 = src[:].rearrange("p (n s) -> p n s", s=16)
                for l in range(32):
                    rhs = v3[:, 0:255, l] if l < 16 else v3[:, 1:256, l - 16]
                    P.op("pe", R.matmul(ps_s[0][:, 0:255], lhsT=w1b[i][:, l, :], rhs=rhs, start=(l == 0), stop=(l == 31)),
                         reads=[w1bB[i], srcB], writes=ps_sB[0], inc=(l == 31))
                P.op("act", R.activation(out=hid[:, 0:255], in_=ps_s[0][:, 0:255], func=AF.Silu, bias=cb[i][:, 0:1]),
                     reads=[ps_sB[0], cbB[i]], writes=hidB)
                if i == 0:
                    P.op("pe", R.matmul(ps_s[1][:, 0:255], lhsT=w2b[0][:], rhs=hid[:, 0:255], start=True, stop=True),
                         reads=[w2bB[0], hidB], writes=ps_sB[1])
                    P.op("dve", R.tensor_copy(out=kcmpT[:, 0:255], in_=ps_s[1][:, 0:255]), reads=ps_sB[1], writes=kcmpB)
                else:
                    for c in range(2):
                        nk = 128 if c == 0 else 127
                        P.op("pe", R.matmul(ps_s[1][0:nk, c * 128:(c + 1) * 128], lhsT=hid[:, c * 128:c * 128 + nk], rhs=w2b[1][:], start=True, stop=True),
                             reads=[w2bB[1], hidB], writes=ps_sB[1])
                        P.op("dve", R.tensor_copy(out=vcmp[0:nk, c, 0:128], in_=ps_s[1][0:nk, c * 128:(c + 1) * 128]), reads=ps_sB[1], writes=vcmpB)

            def att_tile(qt, kT, kTB, nk, vrhs, vB, pso, psoB, width, first, last, mask, bias_kt=None):
                si = C.si % 2; C.si += 1
                pi = C.pi % 3; C.pi += 1
                qrhs = q_sb[:, :, qt * 128:(qt + 1) * 128]
                P.op("pe", R.matmul(ps_s[si][0:nk, :], lhsT=kT, rhs=qrhs, start=True, stop=(bias_kt is None)),
                     reads=[kTB, qB], writes=ps_sB[si], inc=(bias_kt is None))
                if bias_kt is not None:
                    P.op("pe", R.matmul(ps_s[si][0:nk, :], lhsT=Eb[0:64, bias_kt * 128:(bias_kt + 1) * 128], rhs=negT4[:].rearrange("p a t -> p (a t)"),
                                        start=False, stop=True), reads=[EB, negTB], writes=ps_sB[si])
                P.op("act", R.activation(out=pt[pi][0:nk, :], in_=ps_s[si][0:nk, :], func=AF.Exp, scale=SCALE), reads=ps_sB[si], writes=ptB[pi])
                if mask is not None:
                    pat, base, cm = mask
                    p3 = pt[pi][0:nk, :].rearrange("p (a t) -> p a t", a=4)
                    P.op("pool", R.affine_select(out=p3, in_=p3, pattern=pat, compare_op=ALU.is_ge, fill=0.0, base=base, channel_multiplier=cm),
                         reads=ptB[pi], writes=ptB[pi])
                for hh in range(4):
                    P.op("pe", R.matmul(pso[hh // 2][:, (hh % 2) * width:(hh % 2 + 1) * width], lhsT=pt[pi][0:nk, hh * 128:(hh + 1) * 128], rhs=vrhs,
                                        start=(first and hh % 2 == 0), stop=last, skip_group_check=True), reads=[ptB[pi], vB], writes=psoB[hh // 2], inc=(hh == 3))

            def dens(x, pso, psoB, width):
                for b2 in range(2):
                    P.op("dve", R.tensor_scalar(out=den[:, x, b2 * 2:(b2 + 1) * 2], in0=pso[b2][:, 0:2 * width].rearrange("p (a w) -> p a w", a=2)[:, :, 128],
                                                scalar1=1e-30, scalar2=None, op0=ALU.max), reads=psoB[b2], writes=denB[x])
                P.op("dve", R.reciprocal(out=den[:, x, :], in_=den[:, x, :]), reads=denB[x], writes=denB[x])

            for qt in range(NQT):
                t0 = qt * 128
                chunks = [0] + ([1] if qt >= 16 else [])
                for ci, c in enumerate(chunks):
                    nk = 128 if c == 0 else 127
                    full = (c == 0 and qt >= 17)
                    mask = None if full else ([[0, 4], [1, 128]], t0 - 31 - 2048 * c, -16)
                    att_tile(qt, kcmpT[:, c * 128:c * 128 + nk], kcmpB, nk, vcmp[0:nk, c, :], vcmpB, ps_c, ps_cB, 193,
                             ci == 0, ci == len(chunks) - 1, mask)
                dens(0, ps_c, ps_cB, 193)
                gq = g_sb[:, qt, :].rearrange("p (h x) -> p h x", x=3)
                P.op("dve", R.tensor_tensor(out=coef[:, 0, :], in0=den[:, 0, :], in1=gq[:, :, 0], op=ALU.mult), reads=[denB[0], gB], writes=coefB[0])
                for hh in range(4):
                    P.op("dve", R.tensor_scalar(out=o32[:, hh, :], in0=ps_c[hh // 2][:, (hh % 2) * 193:(hh % 2) * 193 + 128], scalar1=coef[:, 0, hh:hh + 1],
                                                scalar2=None, op0=ALU.mult), reads=[ps_cB[hh // 2], coefB[0]], writes=o32B)
                need_sel = qt >= 8
                if need_sel:
                    for hh in range(4):
                        src = ps_c[hh // 2][:, (hh % 2) * 193 + 129:(hh % 2) * 193 + 193]
                        if hh == 0:
                            P.op("dve", R.tensor_scalar(out=imp[:], in0=src, scalar1=den[:, 0, 0:1], scalar2=None, op0=ALU.mult),
                                 reads=[ps_cB[0], denB[0]], writes=impB)
                        else:
                            P.op("dve", R.scalar_tensor_tensor(out=imp[:], in0=src, scalar=den[:, 0, hh:hh + 1], in1=imp[:], op0=ALU.mult, op1=ALU.add),
                                 reads=[ps_cB[hh // 2], denB[0], impB], writes=impB)
                    for k, off in enumerate((128, 64, 0)):
                        P.op("pool", R.affine_select(out=ind[:, k, :], in_=ones64[:], pattern=[[-64, 64]], compare_op=ALU.is_ge, fill=0.0,
                                                     base=t0 - off, channel_multiplier=1), reads=onesB, writes=indB[k])
                    P.op("dve", R.tensor_tensor(out=addm[:], in0=ind[:, 0, :], in1=ind[:, 1, :], op=ALU.add), reads=[indB[0], indB[1]], writes=addB)
                    P.op("dve", R.tensor_scalar(out=addm[:], in0=addm[:], scalar1=-1e9, scalar2=-1e9, op0=ALU.mult, op1=ALU.add), reads=addB, writes=addB)
                    P.op("dve", R.scalar_tensor_tensor(out=addm[:], in0=ind[:, 2, :], scalar=3e9, in1=addm[:], op0=ALU.mult, op1=ALU.add),
                         reads=[indB[2], addB], writes=addB)
                    P.op("dve", R.tensor_tensor(out=imp[:], in0=imp[:], in1=ind[:, 0, :], op=ALU.mult), reads=[impB, indB[0]], writes=impB)
                    P.op("dve", R.tensor_tensor(out=imp[:], in0=imp[:], in1=addm[:], op=ALU.add), reads=[impB, addB], writes=impB)
                    P.op("dve", R.tensor_scalar(out=imp[:, 0:1], in0=imp[:, 0:1], scalar1=0.0, scalar2=3e9, op0=ALU.mult, op1=ALU.add), reads=impB, writes=impB)
                    P.op("dve", R.max(out=m8[:], in_=imp[:]), reads=impB, writes=m8B)
                    P.op("dve", R.match_replace(out=wk[:], in_to_replace=m8[:], in_values=imp[:], imm_value=-3e9), reads=[m8B, impB], writes=wkB)
                    P.op("dve", R.max(out=m8[:], in_=wk[:]), reads=wkB, writes=m8B)
                    P.op("dve", R.tensor_scalar(out=negm[:], in0=imp[:], scalar1=m8[:, 7:8], scalar2=None, op0=ALU.is_ge), reads=[impB, m8B], writes=negmB)
                    P.op("dve", R.tensor_scalar(out=negm[:], in0=negm[:], scalar1=-1.0, scalar2=NEGB, op0=ALU.add, op1=ALU.mult), reads=negmB, writes=negmB)
                    P.op("pe", R.transpose(ps_c[0][0:64, 0:128], negm[:], ident32[:]), reads=[negmB, idB], writes=ps_cB[0])
                    for a in range(4):
                        P.op("dve", R.tensor_copy(out=negT4[:, a, :], in_=ps_c[0][0:64, 0:128]), reads=ps_cB[0], writes=negTB)
                kts = list(range(max(0, qt - 4), qt + 1))
                for i, kt in enumerate(kts):
                    if kt == qt:
                        mask = ([[0, 4], [1, 128]], 0, -1)
                    elif kt == qt - 4:
                        mask = ([[0, 4], [-1, 128]], -1, 1)
                    else:
                        mask = None
                    att_tile(qt, kw_sb[:, kt * 128:(kt + 1) * 128], kwB, 128, vw_e[:, kt, :], vwB, ps_w, ps_wB, 129, i == 0, i == len(kts) - 1, mask)
                for kt in range(qt + 1):
                    mask = ([[0, 4], [1, 128]], 0, -1) if kt == qt else None
                    bias_kt = kt if (need_sel and kt < qt) else None
                    att_tile(qt, ks_sb[:, kt * 128:(kt + 1) * 128], ksB, 128, vs_e[:, kt, :], vsB, ps_e, ps_eB, 129, kt == 0, kt == qt, mask, bias_kt)
                for x, (pso, psoB) in ((1, (ps_e, ps_eB)), (2, (ps_w, ps_wB))):
                    dens(x, pso, psoB, 129)
                    P.op("dve", R.tensor_tensor(out=coef[:, x, :], in0=den[:, x, :], in1=gq[:, :, x], op=ALU.mult), reads=[denB[x], gB], writes=coefB[x])
                    for hh in range(4):
                        P.op("dve", R.scalar_tensor_tensor(out=o32[:, hh, :], in0=pso[hh // 2][:, (hh % 2) * 129:(hh % 2) * 129 + 128],
                                                           scalar=coef[:, x, hh:hh + 1], in1=o32[:, hh, :], op0=ALU.mult, op1=ALU.add),
                             reads=[psoB[hh // 2], coefB[x], o32B], writes=o32B)
                oi = C.oi % 2; C.oi += 1
                for hh in range(4):
                    P.op("pe", R.transpose(ps_c[1][:, hh * 128:(hh + 1) * 128], o32[:, hh, :], ident32[:]), reads=[o32B, idB], writes=ps_cB[1], inc=(hh == 3))
                P.op("act", R.copy(out=oT[oi][:].rearrange("p a t -> p (a t)"), in_=ps_c[1][:]), reads=ps_cB[1], writes=oTB[oi])
                dst = attnT[gl * 512:(gl + 1) * 512, qt * 128:(qt + 1) * 128].rearrange("(a d) t -> d a t", a=4)
                P.dma("sync", dst, oT[oi][:], reads=oTB[oi])
    P.barrier()


_uid = [0]

S = 4096
DK = 256
DV = 512
CH = 64
BLK = 512


def emit_gla(P, nc, projX, gzX, wg, ng, onT, selD, dbg_chunks=None, dbg_stage=9):
    _uid[0] += 1
    st = ExitStack()
    with st:
        def A(name, shape, dt):
            return st.enter_context(nc.sbuf_tensor("gla_%d_" % _uid[0] + name, shape, dt))

        def PS(name, shape, dt=F32):
            return st.enter_context(nc.psum_tensor("gla_%d_" % _uid[0] + name, shape, dt))

        ident32 = A("ident32", [128, 128], F32); idB = Buf()
        identb = A("identb", [128, 128], BF16)
        Um32 = A("Um32", [64, 64], F32); UmB = Buf()
        Ucs = A("Ucs", [64, 64], F32)
        onec = A("onec", [128, 1], F32); onecB = Buf()
        wg_sb = A("wg", [16, 512], F32); wgB = Buf()
        bgbc = A("bgbc", [64, 512], F32); bgB = Buf()
        ngbc = A("ngbc", [64, 512], F32); ngbcB = Buf()
        zb = [A("zb%d" % i, [64, DK], F32) for i in range(2)]; zbB = [Buf(), Buf()]
        qT = [A("qT%d" % i, [128, 2, BLK], BF16) for i in range(2)]; qTB = [Buf(), Buf()]
        kT = [A("kT%d" % i, [128, 2, BLK], BF16) for i in range(2)]; kTB = [Buf(), Buf()]
        vT = [A("vT%d" % i, [128, 4, BLK], BF16) for i in range(2)]; vTB = [Buf(), Buf()]
        gz = [A("gz%d" % i, [16, BLK], F32) for i in range(2)]; gzB = [Buf(), Buf()]
        qS = A("qS", [128, 2, BLK], BF16); kS = A("kS", [128, 2, BLK], BF16); vS = A("vS", [128, 4, BLK], BF16); stgB = [Buf(), Buf(), Buf()]
        sel = A("sel", [128, 2], F32); selB = Buf()
        lsp = [A("lsp%d" % i, [64, DK], F32) for i in range(2)]; lspB = [Buf(), Buf()]
        emt = [A("emt%d" % i, [64, DK], F32) for i in range(2)]; emtB = [Buf(), Buf()]
        kit = [A("kit%d" % i, [64, DK], BF16) for i in range(2)]; kitB = [Buf(), Buf()]
        vtok = [A("vtok%d" % i, [64, DV], BF16) for i in range(2)]; vtokB = [Buf(), Buf()]
        epT = [A("epT%d" % i, [128, 2, CH], F32) for i in range(2)]; epTB = [Buf(), Buf()]
        emT = [A("emT%d" % i, [128, 2, CH], F32) for i in range(2)]; emTB = [Buf(), Buf()]
        qd = [A("qd%d" % i, [128, 2, CH], BF16) for i in range(2)]; qdB = [Buf(), Buf()]
        kiT = [A("kiT%d" % i, [128, 2, CH], BF16) for i in range(2)]; kiTB = [Buf(), Buf()]
        Abf = [A("Abf%d" % i, [64, CH], BF16) for i in range(2)]; AbfB = [Buf(), Buf()]
        S32 = A("S32", [128, 2, DV], F32); S32B = [Buf(), Buf()]
        S16 = A("S16", [128, 2, DV], BF16); S16B = [Buf(), Buf()]
        tS = [A("tS%d" % i, [128, DV], F32) for i in range(2)]; tSB = [Buf(), Buf()]
        junk = A("junk", [64, DV], F32); junkB = Buf()
        ssq = [A("ssq%d" % i, [64, 1], F32) for i in range(2)]; ssqB = [Buf(), Buf()]
        on = [A("on%d" % i, [64, DV], BF16) for i in range(2)]; onB = [Buf(), Buf()]
        onst = [A("onst%d" % i, [128, 4, BLK], BF16) for i in range(2)]; onstB = [Buf(), Buf()]

        ps_z = PS("z", [128, 512]); ps_zB = PBuf()
        ps_bT = PS("bT", [128, 512]); ps_bTB = PBuf()
        ps_tr = PS("tr", [128, 512]); ps_trB = PBuf()
        ps_A = PS("A", [128, 512]); ps_AB = PBuf()
        ps_o = PS("o", [128, 512]); ps_oB = PBuf()
        ps_S = [PS("S%d" % i, [128, 512]) for i in range(2)]; ps_SB = [PBuf(), PBuf()]
        ps_ot = PS("ot", [128, 512]); ps_otB = PBuf()

        P.op("pool", R.memset(ident32[:], 1.0), writes=idB)
        P.op("pool", R.affine_select(out=ident32[:], in_=ident32[:], pattern=[[-1, 128]], compare_op=ALU.is_equal,
                                     fill=0.0, base=0, channel_multiplier=1), reads=idB, writes=idB)
        P.op("dve", R.tensor_copy(out=identb[:], in_=ident32[:]), reads=idB, writes=idB)
        P.op("pool", R.memset(Um32[:], 1.0), writes=UmB)
        P.op("pool", R.affine_select(out=Um32[:], in_=Um32[:], pattern=[[1, 64]], compare_op=ALU.is_ge, fill=0.0,
                                     base=0, channel_multiplier=-1), reads=UmB, writes=UmB)
        P.op("dve", R.tensor_scalar(out=Ucs[:], in0=Um32[:], scalar1=-1.0 / 16.0, scalar2=None, op0=ALU.mult), reads=UmB, writes=UmB)
        P.op("pool", R.memset(onec[:], 1.0), writes=onecB)
        P.dma("sync", sel[:], selD, writes=selB)
        P.dma("sync", wg_sb[:], wg[0:16, :], writes=wgB)
        P.dma("sync", bgbc[:], wg[16:80, :], writes=bgB)
        P.dma("sync", ngbc[:], ng, writes=ngbcB)

        def rows(base, n, blk, cand):
            r, o = divmod(blk * BLK, 2048)
            return projX[cand][r, base:base + n, o:o + BLK]

        def blend(dst, dstB, stage, stageB):
            P.op("dve", R.tensor_scalar(out=dst, in0=dst, scalar1=sel[:, 0:1], scalar2=None, op0=ALU.mult), reads=[dstB, selB], writes=dstB)
            P.op("dve", R.scalar_tensor_tensor(out=dst, in0=stage, scalar=sel[:, 1:2], in1=dst, op0=ALU.mult, op1=ALU.add),
                 reads=[stageB, dstB, selB], writes=dstB)

        cnt = 0
        for hl in range(2):
            for ac in range(2):
                P.op("pool", R.memset(S32[:, ac, :], 0.0), writes=S32B[ac])
                P.op("pool", R.memset(S16[:, ac, :], 0.0), writes=S16B[ac])
            for blk in range(S // BLK):
                bi = (hl * (S // BLK) + blk) % 2
                for ac in range(2):
                    P.dma("sync", qT[bi][:, ac, :], rows(hl * 256 + ac * 128, 128, blk, 0), writes=qTB[bi])
                    P.dma("sync", qS[:, ac, :], rows(hl * 256 + ac * 128, 128, blk, 1), writes=stgB[0])
                    P.dma("sync", kT[bi][:, ac, :], rows(512 + hl * 256 + ac * 128, 128, blk, 0), writes=kTB[bi])
                    P.dma("sync", kS[:, ac, :], rows(512 + hl * 256 + ac * 128, 128, blk, 1), writes=stgB[1])
                for dc in range(4):
                    P.dma("sync", vT[bi][:, dc, :], rows(1024 + hl * 512 + dc * 128, 128, blk, 0), writes=vTB[bi])
                    P.dma("sync", vS[:, dc, :], rows(1024 + hl * 512 + dc * 128, 128, blk, 1), writes=stgB[2])
                blend(qT[bi][:], qTB[bi], qS[:], stgB[0])
                blend(kT[bi][:], kTB[bi], kS[:], stgB[1])
                blend(vT[bi][:], vTB[bi], vS[:], stgB[2])
                r_, o_ = divmod(blk * BLK, 2048)
                P.dma("sync", gz[bi][0:16, :], gzX[r_, :, o_:o_ + BLK], writes=gzB[bi])
                for cc in range(BLK // CH):
                    if dbg_chunks is not None and cnt >= dbg_chunks:
                        continue
                    i2 = cnt % 2; cnt += 1
                    ts = slice(cc * CH, (cc + 1) * CH)
                    P.op("pe", R.matmul(ps_z[0:64, 0:DK], lhsT=gz[bi][:, ts], rhs=wg_sb[:, hl * DK:(hl + 1) * DK], start=True, stop=True),
                         reads=[gzB[bi], wgB], writes=ps_zB)
                    P.op("dve", R.tensor_tensor(out=zb[i2][:], in0=ps_z[0:64, 0:DK], in1=bgbc[:, hl * DK:(hl + 1) * DK], op=ALU.add), reads=[ps_zB, bgB], writes=zbB[i2])
                    P.op("act", R.activation(out=lsp[i2][:], in_=zb[i2][:], func=AF.Exp, scale=-1.0), reads=zbB[i2], writes=lspB[i2])
                    P.op("act", R.activation(out=lsp[i2][:], in_=lsp[i2][:], func=AF.Ln, bias=onec[0:64, :]), reads=[lspB[i2], onecB], writes=lspB[i2])
                    if dbg_stage < 2:
                        continue
                    P.op("pe", R.matmul(ps_z[0:64, 0:DK], lhsT=Ucs[:], rhs=lsp[i2][:], start=True, stop=True), reads=[UmB, lspB[i2]], writes=ps_zB)
                    for ac in range(2):
                        P.op("pe", R.matmul(ps_bT[:, ac * CH:(ac + 1) * CH], lhsT=lsp[i2][:, ac * 128:(ac + 1) * 128], rhs=Ucs[:], start=True, stop=True),
                             reads=[UmB, lspB[i2]], writes=ps_bTB)
                    P.op("act", R.activation(out=emt[i2][:], in_=ps_z[0:64, 0:DK], func=AF.Exp, scale=-1.0), reads=ps_zB, writes=emtB[i2])
                    bT3 = ps_bT[:, 0:2 * CH].rearrange("p (a c) -> p a c", a=2)
                    P.op("act", R.activation(out=epT[i2][:], in_=bT3, func=AF.Exp), reads=ps_bTB, writes=epTB[i2])
                    P.op("act", R.activation(out=emT[i2][:], in_=bT3, func=AF.Exp, scale=-1.0), reads=ps_bTB, writes=emTB[i2])
                    if dbg_stage < 3:
                        continue
                    trb = ps_tr[:].bitcast(BF16)
                    for ac in range(2):
                        P.op("pe", R.transpose(trb[0:64, ac * 128:(ac + 1) * 128], kT[bi][:, ac, ts], identb[:]), reads=[kTB[bi], idB], writes=ps_trB, inc=False)
                    for dc in range(4):
                        P.op("pe", R.transpose(trb[0:64, 256 + dc * 128:256 + (dc + 1) * 128], vT[bi][:, dc, ts], identb[:]), reads=[vTB[bi], idB], writes=ps_trB,
                             inc=(dc == 3))
                    P.op("dve", R.tensor_tensor(out=kit[i2][:], in0=trb[0:64, 0:256], in1=emt[i2][:], op=ALU.mult), reads=[ps_trB, emtB[i2]], writes=kitB[i2])
                    P.op("act", R.copy(out=vtok[i2][:], in_=trb[0:64, 256:768]), reads=ps_trB, writes=vtokB[i2])
                    if dbg_stage < 4:
                        continue
                    P.op("dve", R.scalar_tensor_tensor(out=qd[i2][:], in0=epT[i2][:], scalar=DK ** -0.5, in1=qT[bi][:, :, ts], op0=ALU.mult, op1=ALU.mult),
                         reads=[epTB[i2], qTB[bi]], writes=qdB[i2])
                    P.op("dve", R.tensor_tensor(out=kiT[i2][:], in0=emT[i2][:], in1=kT[bi][:, :, ts], op=ALU.mult), reads=[emTB[i2], kTB[bi]], writes=kiTB[i2])
                    for ac in range(2):
                        P.op("pe", R.matmul(ps_A[0:64, 0:CH], lhsT=kiT[i2][:, ac, :], rhs=qd[i2][:, ac, :], start=(ac == 0), stop=(ac == 1)),
                             reads=[kiTB[i2], qdB[i2]], writes=ps_AB, inc=(ac == 1))
                    P.op("dve", R.tensor_tensor(out=Abf[i2][:], in0=ps_A[0:64, 0:CH], in1=Um32[:], op=ALU.mult), reads=[ps_AB, UmB], writes=AbfB[i2])
                    if dbg_stage < 5:
                        continue
                    P.op("pe", R.matmul(ps_o[0:64, :], lhsT=Abf[i2][:], rhs=vtok[i2][:], start=True, stop=False), reads=[AbfB[i2], vtokB[i2]], writes=ps_oB, inc=False)
                    for ac in range(2):
                        P.op("pe", R.matmul(ps_o[0:64, :], lhsT=qd[i2][:, ac, :], rhs=S16[:, ac, :], start=False, stop=(ac == 1)),
                             reads=[qdB[i2], S16B[ac]], writes=ps_oB, inc=(ac == 1))
                    for ac in range(2):
                        P.op("pe", R.matmul(ps_S[ac][:], lhsT=kit[i2][:, ac * 128:(ac + 1) * 128], rhs=vtok[i2][:], start=True, stop=True),
                             reads=[kitB[i2], vtokB[i2]], writes=ps_SB[ac])
                        P.op("dve", R.tensor_tensor(out=tS[ac][:], in0=S32[:, ac, :], in1=ps_S[ac][:], op=ALU.add), reads=[S32B[ac], ps_SB[ac]], writes=tSB[ac])
                        eb = epT[i2][:, ac, CH - 1:CH]
                        P.op("act", R.activation(out=S32[:, ac, :], in_=tS[ac][:], func=AF.Identity, scale=eb), reads=[tSB[ac], epTB[i2]], writes=S32B[ac])
                        P.op("dve", R.tensor_scalar(out=S16[:, ac, :], in0=tS[ac][:], scalar1=eb, scalar2=None, op0=ALU.mult), reads=[tSB[ac], epTB[i2]], writes=S16B[ac])
                    if dbg_stage < 6:
                        continue
                    P.op("act", R.activation(out=junk[:], in_=ps_o[0:64, :], func=AF.Square), reads=ps_oB, writes=junkB)
                    P.op("dve", R.reduce_sum(out=ssq[i2][:], in_=junk[:], axis=AX.X), reads=junkB, writes=ssqB[i2])
                    P.op("dve", R.tensor_scalar(out=ssq[i2][:], in0=ssq[i2][:], scalar1=1.0 / DV, scalar2=1e-6, op0=ALU.mult, op1=ALU.add), reads=ssqB[i2], writes=ssqB[i2])
                    P.op("act", R.activation(out=ssq[i2][:], in_=ssq[i2][:], func=AF.Sqrt), reads=ssqB[i2], writes=ssqB[i2])
                    P.op("dve", R.reciprocal(out=ssq[i2][:], in_=ssq[i2][:]), reads=ssqB[i2], writes=ssqB[i2])
                    P.op("dve", R.scalar_tensor_tensor(out=on[i2][:], in0=ps_o[0:64, :], scalar=ssq[i2][:, 0:1], in1=ngbc[:], op0=ALU.mult, op1=ALU.mult),
                         reads=[ps_oB, ssqB[i2], ngbcB], writes=onB[i2])
                    otb = ps_ot[:].bitcast(BF16)
                    for dc in range(4):
                        P.op("pe", R.transpose(otb[:, dc * CH:(dc + 1) * CH], on[i2][:, dc * 128:(dc + 1) * 128], identb[0:64, 0:64]), reads=[onB[i2], idB], writes=ps_otB,
                             inc=(dc == 3))
                    P.op("dve", R.tensor_copy(out=onst[bi][:, :, ts], in_=otb[:, 0:4 * CH].rearrange("p (a c) -> p a c", a=4)), reads=ps_otB, writes=onstB[bi])
                r, o = divmod(blk * BLK, 2048)
                dst = onT[hl * 512:(hl + 1) * 512, blk * BLK:(blk + 1) * BLK].rearrange("(a p) t -> p a t", p=128)
                P.dma("sync", dst, onst[bi][:], reads=onstB[bi])
    P.barrier()

import numpy as np
import ml_dtypes


BF = ml_dtypes.bfloat16
NCORES = 8
PAIRS = [[0, 1], [2, 3], [4, 5], [6, 7]]
ALL8 = [list(range(8))]
NSA_F = 5168
GLA_F = 6160
NSA_FP = 5376
GLA_FP = 6400
HST = 32

def weight_specs():
    sp = []
    for j in range(2):
        sp += [("nsa_w_in%d" % j, 2048, NSA_FP), ("nsa_w_out%d" % j, 2048, 2048),
               ("cmp_w1k%d" % j, 4096, 128), ("cmp_w1v%d" % j, 4096, 128)]
    sp += [("conv_w_in", 2048, 4096), ("conv_w_out", 2048, 2048), ("gla_w_in", 2048, GLA_FP), ("gla_w_out", 2048, 2048)]
    for l in range(4):
        sp += [("ffn_w1_%d" % l, 2048, 8192), ("ffn_w2_%d" % l, 8192, 2048), ("ple_wg_%d" % l, 2048, 2048), ("ple_wp_%d" % l, 256, 2048)]
    return sp


def build_fused(dumps=False, stop=99):
    nc = bass.Bass("TRN2", target_bir_lowering=False)
    I = lambda n, s, dt=F32: nc.dram_tensor(n, list(s), dt, kind="ExternalInput").ap()
    O = lambda n, s, dt=F32: nc.dram_tensor(n, list(s), dt, kind="ExternalOutput").ap()
    N = lambda n, s, dt=F32: nc.dram_tensor(n, list(s), dt).ap()
    xT = I("xT", [D, NTOK])
    pT = I("pT", [4, 256, NTOK])
    selD = I("sel", [128, 2])
    gainsD = I("gains", [5, 128, 3, 16])
    small = {}
    for j in range(2):
        small["cmp_w2k%d" % j] = I("cmp_w2k%d" % j, [128, 128]); small["cmp_w2v%d" % j] = I("cmp_w2v%d" % j, [128, 128])
        small["posTk%d" % j] = I("posTk%d" % j, [128, 32]); small["posTv%d" % j] = I("posTv%d" % j, [128, 32])
    dwT = I("dwT", [128, 16, 31]); cvec = I("cvec", [128, 3, 16])
    gla_wg = I("gla_wg", [80, 512]); gla_ng = I("gla_ng", [64, 512])
    outT = O("outT", [D, NTOK])
    specs = weight_specs()
    shard = {}
    spec_d = {n: (k, f) for n, k, f in specs}
    bounce = {}
    full = {}
    WB = {n: Buf() for n, _, _ in specs}
    hA = N("hA", [D, NTOK]); hBt = N("hBt", [D, NTOK])
    proj_nsa = [N("proj_nsa%d" % j, [5120, NTOK], BF16) for j in range(2)]
    gate_nsa = [N("gate_nsa%d" % j, [64, NTOK]) for j in range(2)]
    NSA_PIECES = [(0, 2048), (2048, 4096), (4096, 5120)]
    projX_nsa = [[N("projX_nsa%d_%d" % (j, k), [2 * (r1 - r0), NTOK], BF16) for k, (r0, r1) in enumerate(NSA_PIECES)] for j in range(2)]
    gateX_nsa = [N("gateX_nsa%d" % j, [2 * 64, NTOK]) for j in range(2)]
    mix_loc = [N("mix_loc%d" % j, [1024, 4096], BF16) for j in range(3)]
    mixX = [N("mixX%d" % j, [2 * 1024, 4096], BF16) for j in range(3)]
    uT = N("uT", [D, NTOK], BF16)
    halo_loc = N("halo_loc", [D, HST], BF16); haloX = N("haloX", [2 * D, HST], BF16)
    proj_gla = N("proj_gla", [6144, NTOK], BF16); gz_loc = N("gz_loc", [16, NTOK])
    projX_gla = [N("projX_gla%d" % k, [2 * 2048, NTOK], BF16) for k in range(2)]; gzX = N("gzX", [2 * 16, NTOK])
    dump = {}
    if dumps:
        dump["hA"] = O("dump_hA", [D, 256]); dump["hB"] = O("dump_hB", [D, 256])
        dump["att0"] = O("dump_att0", [1024, 1024], BF16)
        dump["on"] = O("dump_on", [1024, 1024], BF16)
        dump["u"] = O("dump_u", [D, 256], BF16)

    with ExitStack() as st:
        P = Prog(nc, st)

        def v256(ap, R, n=1):
            if R % 256 == 0:
                return ap.rearrange("(a b) c -> a (b c)", a=n * 256)
            return ap.rearrange("r (a c) -> (r a) c", a=256 // R)

        def gather_w(names):
            for n in names:
                b = Buf()
                k_, f_ = spec_d[n]
                shard[n] = I(n + "_s", [k_ // 8, f_])
                bounce[n] = N(n + "_b", [k_ // 8, f_]); full[n] = N(n + "_f", [k_, f_])
                P.dma("sync", bounce[n], shard[n], writes=b)
                P.cc("AllGather", [v256(bounce[n], k_ // 8)], [v256(full[n], k_ // 8, 8)], ALL8, reads=b, writes=WB[n])

        def exchange(src, dst):
            R_ = src.shape[0]
            ev = P.cc("AllGather", [v256(src, R_)], [v256(dst, R_, 2)], PAIRS)
            P.wait_all(ev)

        def exchange_nsa(j):
            for k, (r0, r1) in enumerate(NSA_PIECES):
                exchange(proj_nsa[j][r0:r1, :], projX_nsa[j][k])
            exchange(gate_nsa[j], gateX_nsa[j])

        def nsa_pieces(j):
            return [(r0, r1, r2(projX_nsa[j][k], r1 - r0)) for k, (r0, r1) in enumerate(NSA_PIECES)]

        def r2(ap, n):
            return ap.rearrange("(r p) t -> r p t", r=2)

        def finish():
            if dumps:
                P.barrier()
                P.dma("sync", dump["att0"][:, 0:512], mix_loc[0][:, 0:512]); P.dma("sync", dump["att0"][:, 512:1024], mix_loc[0][:, 3584:4096])
                P.dma("sync", dump["on"][:, 0:512], mix_loc[1][:, 0:512]); P.dma("sync", dump["on"][:, 512:1024], mix_loc[1][:, 3584:4096])
                P.dma("sync", dump["u"], uT[:, 0:256])
                P.dma("sync", dump["hB"], hBt[:, 0:256]); P.dma("sync", dump["hA"], hA[:, 0:256])
            print("fused ninst", P.ninst, "stop", stop, flush=True)
            P.emit()

        gather_w(["nsa_w_in0", "cmp_w1k0", "cmp_w1v0"] + (["nsa_w_out0", "ffn_w1_0", "ffn_w2_0", "ple_wg_0", "ple_wp_0", "conv_w_in"] if stop >= 3 else []))
        emit_T(P, nc, dict(hT=xT, sel=selD, gains=gainsD[0], w_in=full["nsa_w_in0"], w_inB=WB["nsa_w_in0"], projT=proj_nsa[0], tail32=gate_nsa[0], hT_out=hA),
               None, False, "proj", F=NSA_F, F32TAIL=48)
        exchange_nsa(0)
        if stop <= 1:
            finish()
            return nc
        gather_w(["conv_w_out", "ffn_w1_1", "ffn_w2_1", "ple_wg_1", "ple_wp_1", "gla_w_in"] if stop >= 4 else [])
        cw = {"w1k": full["cmp_w1k0"], "w1kB": WB["cmp_w1k0"], "w1v": full["cmp_w1v0"], "w1vB": WB["cmp_w1v0"],
              "w2k": small["cmp_w2k0"], "w2v": small["cmp_w2v0"], "posTk": small["posTk0"], "posTv": small["posTv0"]}
        emit_nsa(P, nc, nsa_pieces(0), r2(gateX_nsa[0], 64), cw, mix_loc[0], selD)
        exchange(mix_loc[0], mixX[0])
        if stop <= 2:
            finish()
            return nc
        emit_T(P, nc, dict(hT=hA, sel=selD, gains=gainsD[1], mixX=r2(mixX[0], 1024), w_out=full["nsa_w_out0"], w_outB=WB["nsa_w_out0"],
                           w1=full["ffn_w1_0"], w1B=WB["ffn_w1_0"], w2=full["ffn_w2_0"], w2B=WB["ffn_w2_0"], wg=full["ple_wg_0"], wgB=WB["ple_wg_0"],
                           wp=full["ple_wp_0"], wpB=WB["ple_wp_0"], pT=pT[0], w_in=full["conv_w_in"], w_inB=WB["conv_w_in"], projT=uT, hT_out=hBt),
               "linear", True, "glu", F=4096)
        b = Buf()
        P.dma("sync", halo_loc, uT[:, NTOK - HST:NTOK], writes=b)
        P.barrier()
        exchange(halo_loc, haloX)
        if stop <= 3:
            finish()
            return nc
        gather_w(["gla_w_out", "ffn_w1_2", "ffn_w2_2", "ple_wg_2", "ple_wp_2", "nsa_w_in1", "cmp_w1k1", "cmp_w1v1"] if stop >= 6 else [])
        emit_T(P, nc, dict(hT=hBt, sel=selD, gains=gainsD[2], uT=uT, haloX=r2(haloX, D), dwT=dwT, cvec=cvec, w_out=full["conv_w_out"], w_outB=WB["conv_w_out"],
                           w1=full["ffn_w1_1"], w1B=WB["ffn_w1_1"], w2=full["ffn_w2_1"], w2B=WB["ffn_w2_1"], wg=full["ple_wg_1"], wgB=WB["ple_wg_1"],
                           wp=full["ple_wp_1"], wpB=WB["ple_wp_1"], pT=pT[1], w_in=full["gla_w_in"], w_inB=WB["gla_w_in"], projT=proj_gla, tail32=gz_loc, hT_out=hA),
               "conv", True, "proj", F=GLA_F, F32TAIL=16)
        exchange(proj_gla[0:2048, :], projX_gla[0]); exchange(proj_gla[2048:4096, :], projX_gla[1]); exchange(gz_loc, gzX)
        if stop <= 4:
            finish()
            return nc
        gather_w(["nsa_w_out1", "ffn_w1_3", "ffn_w2_3", "ple_wg_3", "ple_wp_3"] if stop >= 8 else [])
        emit_gla(P, nc, [r2(projX_gla[0], 2048), r2(projX_gla[1], 2048)], r2(gzX, 16), gla_wg, gla_ng, mix_loc[1], selD)
        exchange(mix_loc[1], mixX[1])
        if stop <= 5:
            finish()
            return nc
        emit_T(P, nc, dict(hT=hA, sel=selD, gains=gainsD[3], mixX=r2(mixX[1], 1024), rT=proj_gla[4096:6144, :], w_out=full["gla_w_out"], w_outB=WB["gla_w_out"],
                           w1=full["ffn_w1_2"], w1B=WB["ffn_w1_2"], w2=full["ffn_w2_2"], w2B=WB["ffn_w2_2"], wg=full["ple_wg_2"], wgB=WB["ple_wg_2"],
                           wp=full["ple_wp_2"], wpB=WB["ple_wp_2"], pT=pT[2], w_in=full["nsa_w_in1"], w_inB=WB["nsa_w_in1"], projT=proj_nsa[1], tail32=gate_nsa[1], hT_out=hBt),
               "gla", True, "proj", F=NSA_F, F32TAIL=48)
        exchange_nsa(1)
        if stop <= 6:
            finish()
            return nc
        cw = {"w1k": full["cmp_w1k1"], "w1kB": WB["cmp_w1k1"], "w1v": full["cmp_w1v1"], "w1vB": WB["cmp_w1v1"],
              "w2k": small["cmp_w2k1"], "w2v": small["cmp_w2v1"], "posTk": small["posTk1"], "posTv": small["posTv1"]}
        emit_nsa(P, nc, nsa_pieces(1), r2(gateX_nsa[1], 64), cw, mix_loc[2], selD)
        exchange(mix_loc[2], mixX[2])
        if stop <= 7:
            finish()
            return nc
        emit_T(P, nc, dict(hT=hBt, sel=selD, gains=gainsD[4], mixX=r2(mixX[2], 1024), w_out=full["nsa_w_out1"], w_outB=WB["nsa_w_out1"],
                           w1=full["ffn_w1_3"], w1B=WB["ffn_w1_3"], w2=full["ffn_w2_3"], w2B=WB["ffn_w2_3"], wg=full["ple_wg_3"], wgB=WB["ple_wg_3"],
                           wp=full["ple_wp_3"], wpB=WB["ple_wp_3"], pT=pT[3], outT=outT),
               "linear", True, "final")
        finish()
    return nc


def fm(v):
    return np.ascontiguousarray(v.reshape(-1, 128).T)


def nsa_col_perm():
    cols = []
    for s in range(2):
        for gl in range(2):
            g = 2 * s + gl
            cols += list(range(g * 512, (g + 1) * 512))
        for k in range(6):
            for gl in range(2):
                g = 2 * s + gl
                cols += list(range(2048 + k * 512 + g * 128, 2048 + k * 512 + (g + 1) * 128))
    cols += list(range(5120, 5168))
    return np.array(cols)


def gla_col_perm():
    cols = []
    for s in range(2):
        for hl in range(2):
            h = 2 * s + hl
            cols += list(range(h * 256, (h + 1) * 256))
        for hl in range(2):
            h = 2 * s + hl
            cols += list(range(1024 + h * 256, 1024 + (h + 1) * 256))
        for hl in range(2):
            h = 2 * s + hl
            cols += list(range(2048 + h * 512, 2048 + (h + 1) * 512))
    cols += list(range(4096, 6160))
    return np.array(cols)


def conv_col_perm():
    cols = []
    for fc in range(16):
        cols += list(range(2048 + fc * 128, 2048 + (fc + 1) * 128))
        cols += list(range(fc * 128, (fc + 1) * 128))
    return np.array(cols)


def make_inputs(inp, nc=None):
    f32 = np.float32
    W = {}
    pn, pg, pc = nsa_col_perm(), gla_col_perm(), conv_col_perm()
    for j in range(2):
        W["nsa_w_in%d" % j] = np.concatenate([inp["nsa_w_in"][j][:, pn], np.zeros((2048, NSA_FP - NSA_F), f32)], 1)
        W["nsa_w_out%d" % j] = inp["nsa_w_out"][j]
        W["cmp_w1k%d" % j] = inp["nsa_cmp_w1_k"][j]
        W["cmp_w1v%d" % j] = inp["nsa_cmp_w1_v"][j]
    W["conv_w_in"] = inp["conv_w_in"][0][:, pc]
    W["conv_w_out"] = inp["conv_w_out"][0]
    W["gla_w_in"] = np.concatenate([inp["gla_w_in"][0][:, pg], np.zeros((2048, GLA_FP - GLA_F), f32)], 1)
    W["gla_w_out"] = inp["gla_w_out"][0]
    for l in range(4):
        W["ffn_w1_%d" % l] = inp["ffn_w1"][l]; W["ffn_w2_%d" % l] = inp["ffn_w2"][l]
        W["ple_wg_%d" % l] = inp["ple_w_gate"][l]; W["ple_wp_%d" % l] = inp["ple_w_proj"][l]
    nm, nf, npl = inp["norm_mix"], inp["norm_ffn"], inp["norm_ple"]
    g = np.zeros((5, 128, 3, 16), f32)
    g[0, :, 2] = fm(nm[0])
    for k in range(1, 5):
        g[k, :, 0] = fm(nf[k - 1]); g[k, :, 1] = fm(npl[k - 1]); g[k, :, 2] = fm(nm[k]) if k < 4 else fm(inp["norm_final"])
    common = {"gains": g,
              "dwT": np.ascontiguousarray(inp["conv_dw"][0].T.reshape(16, 128, 31).transpose(1, 0, 2)).astype(f32),
              "cvec": np.ascontiguousarray(np.stack([fm(inp["conv_db"][0]), fm(inp["conv_ln_g"][0]), fm(inp["conv_ln_b"][0])], 1)).astype(f32),
              "gla_ng": np.ascontiguousarray(np.tile(inp["gla_norm_g"][0][None, :], (64, 1))).astype(f32)}
    for j in range(2):
        common["cmp_w2k%d" % j] = inp["nsa_cmp_w2_k"][j]; common["cmp_w2v%d" % j] = inp["nsa_cmp_w2_v"][j]
        common["posTk%d" % j] = np.ascontiguousarray(inp["nsa_cmp_pos_k"][j].T); common["posTv%d" % j] = np.ascontiguousarray(inp["nsa_cmp_pos_v"][j].T)
    maps = []
    for c in range(NCORES):
        b, r = c // 2, c % 2
        ts = slice(r * NTOK, (r + 1) * NTOK)
        m = dict(common)
        m["xT"] = np.ascontiguousarray(inp["x"][b, ts, :].T)
        m["pT"] = np.ascontiguousarray(inp["p"][:, b, ts, :].transpose(0, 2, 1))
        sel = np.zeros((128, 2), f32); sel[:, r] = 1.0
        m["sel"] = sel
        wgc = inp["gla_w_gate_up"][0][:, r * 512:(r + 1) * 512]
        bgc = inp["gla_b_gate"][0][r * 512:(r + 1) * 512]
        m["gla_wg"] = np.ascontiguousarray(np.concatenate([wgc, np.tile(bgc[None, :], (64, 1))], 0)).astype(f32)
        for n, k, f in weight_specs():
            m[n + "_s"] = np.ascontiguousarray(W[n][c * (k // 8):(c + 1) * (k // 8)])
        if nc is not None:
            names = set()
            import concourse.mybir as mybir_
            for alloc in nc.allocations:
                if isinstance(alloc, mybir_.MemoryLocationSet) and alloc.kind == "ExternalInput":
                    names.add(alloc.memorylocations[0].name)
            m = {k_: v_ for k_, v_ in m.items() if k_ in names}
        maps.append(m)
    return maps


def assemble(results):
    out = np.zeros((4, 4096, D), np.float32)
    for c in range(NCORES):
        b, r = c // 2, c % 2
        out[b, r * NTOK:(r + 1) * NTOK, :] = np.asarray(results[c]["outT"]).T
    return out

import os
import numpy as np
import ml_dtypes


BF = ml_dtypes.bfloat16
_cache = {}


def _run(nc, maps):
    return run_bass_kernel_spmd(nc, maps, core_ids=list(range(NCORES))).results


def build_T_prog(pre, tail, post, F=0, F32TAIL=0, FP=0):
    key = ("T", pre, tail, post, F)
    if key in _cache:
        return _cache[key]
    nc = bass.Bass("TRN2", target_bir_lowering=False)
    I = lambda n, s, dt=F32: nc.dram_tensor(n, list(s), dt, kind="ExternalInput").ap()
    O = lambda n, s, dt=F32: nc.dram_tensor(n, list(s), dt, kind="ExternalOutput").ap()
    io = dict(hT=I("hT", [D, NTOK]), sel=I("sel", [128, 2]), gains=I("gains", [128, 3, 16]))
    if pre in ("linear", "gla"):
        io["mixX"] = I("mixX", [2, 1024, 4096], BF16); io["w_out"] = I("w_out", [D, D])
        if pre == "gla":
            io["rT"] = I("rT", [D, NTOK], BF16)
    elif pre == "conv":
        io["uT"] = I("uT", [D, NTOK], BF16); io["haloX"] = I("haloX", [2, D, 32], BF16)
        io["dwT"] = I("dwT", [128, 16, 31]); io["cvec"] = I("cvec", [128, 3, 16]); io["w_out"] = I("w_out", [D, D])
    if tail:
        io["w1"] = I("w1", [D, 8192]); io["w2"] = I("w2", [8192, D]); io["wg"] = I("wg", [D, D]); io["wp"] = I("wp", [256, D])
        io["pT"] = I("pT", [256, NTOK])
    if post in ("proj", "glu"):
        io["w_in"] = I("w_in", [D, FP or F])
        FO = F // 2 if post == "glu" else (F // 128) * 128
        io["projT"] = O("projT", [FO, NTOK], BF16)
        if F32TAIL:
            io["tail32"] = O("tail32", [F32TAIL, NTOK])
        io["hT_out"] = O("hT_out", [D, NTOK])
    else:
        io["outT"] = O("outT", [D, NTOK])
    with ExitStack() as st:
        P = Prog(nc, st)
        emit_T(P, nc, io, pre, tail, post, F=F, F32TAIL=F32TAIL)
        print("T prog", pre, tail, post, "ninst", P.ninst, flush=True)
        P.emit()
    _cache[key] = nc
    return nc


def build_nsa_prog():
    if "nsa" in _cache:
        return _cache["nsa"]
    nc = bass.Bass("TRN2", target_bir_lowering=False)
    I = lambda n, s, dt=F32: nc.dram_tensor(n, list(s), dt, kind="ExternalInput").ap()
    projX = I("projX", [2, 5120, NTOK], BF16)
    gateX = I("gateX", [2, 64, NTOK])
    cw = {k: I(k, s) for k, s in (("w1k", [4096, 128]), ("w2k", [128, 128]), ("w1v", [4096, 128]), ("w2v", [128, 128]), ("posTk", [128, 32]), ("posTv", [128, 32]))}
    selD = I("sel", [128, 2])
    attnT = nc.dram_tensor("attnT", [1024, 4096], BF16, kind="ExternalOutput").ap()
    with ExitStack() as st:
        P = Prog(nc, st)
        emit_nsa(P, nc, [(0, 5120, projX)], gateX, cw, attnT, selD)
        print("nsa prog ninst", P.ninst, flush=True)
        P.emit()
    _cache["nsa"] = nc
    return nc


def build_gla_prog():
    if "gla" in _cache:
        return _cache["gla"]
    nc = bass.Bass("TRN2", target_bir_lowering=False)
    I = lambda n, s, dt=F32: nc.dram_tensor(n, list(s), dt, kind="ExternalInput").ap()
    p0 = I("projX0", [2, 2048, NTOK], BF16); p1 = I("projX1", [2, 2048, NTOK], BF16)
    gzX = I("gzX", [2, 16, NTOK]); wg = I("wg", [80, 512]); ng = I("ng", [64, 512]); selD = I("sel", [128, 2])
    onT = nc.dram_tensor("onT", [1024, 4096], BF16, kind="ExternalOutput").ap()
    with ExitStack() as st:
        P = Prog(nc, st)
        emit_gla(P, nc, [p0, p1], gzX, wg, ng, onT, selD)
        print("gla prog ninst", P.ninst, flush=True)
        P.emit()
    _cache["gla"] = nc
    return nc


def kernel_multi(inp):
    f32 = np.float32
    pn, pg, pc = nsa_col_perm(), gla_col_perm(), conv_col_perm()
    nm, nf, npl = inp["norm_mix"], inp["norm_ffn"], inp["norm_ple"]

    def gains(k):
        g = np.zeros((128, 3, 16), f32)
        if k > 0:
            g[:, 0] = fm(nf[k - 1]); g[:, 1] = fm(npl[k - 1])
        g[:, 2] = fm(nm[k]) if k < 4 else fm(inp["norm_final"])
        return g

    sels = []
    for c in range(NCORES):
        s = np.zeros((128, 2), f32); s[:, c % 2] = 1.0
        sels.append(s)

    def tok(c):
        return c // 2, slice((c % 2) * NTOK, (c % 2 + 1) * NTOK)

    def tail_w(l):
        return dict(w1=inp["ffn_w1"][l], w2=inp["ffn_w2"][l], wg=inp["ple_w_gate"][l], wp=inp["ple_w_proj"][l])

    def pT(l, c):
        b, ts = tok(c)
        return np.ascontiguousarray(inp["p"][l, b, ts, :].T)

    def pairstack(res, key, c):
        b0 = c // 2 * 2
        return np.ascontiguousarray(np.stack([np.asarray(res[b0][key]), np.asarray(res[b0 + 1][key])], 0))

    def nsa_launch(j, rT0):
        maps = []
        for c in range(NCORES):
            gx = np.zeros((2, 64, NTOK), f32)
            gx[:, :48] = pairstack(rT0, "tail32", c)
            maps.append(dict(projX=pairstack(rT0, "projT", c), gateX=gx, sel=sels[c],
                             w1k=inp["nsa_cmp_w1_k"][j], w2k=inp["nsa_cmp_w2_k"][j], w1v=inp["nsa_cmp_w1_v"][j], w2v=inp["nsa_cmp_w2_v"][j],
                             posTk=np.ascontiguousarray(inp["nsa_cmp_pos_k"][j].T), posTv=np.ascontiguousarray(inp["nsa_cmp_pos_v"][j].T)))
        return _run(build_nsa_prog(), maps)

    w_nsa = [np.ascontiguousarray(inp["nsa_w_in"][j][:, pn]) for j in range(2)]
    maps = []
    for c in range(NCORES):
        b, ts = tok(c)
        maps.append(dict(hT=np.ascontiguousarray(inp["x"][b, ts, :].T), sel=sels[c], gains=gains(0), w_in=w_nsa[0]))
    r0 = _run(build_T_prog(None, False, "proj", F=NSA_F, F32TAIL=48), maps)
    ra = nsa_launch(0, r0)
    maps = []
    for c in range(NCORES):
        m = dict(hT=np.asarray(r0[c]["hT_out"]), sel=sels[c], gains=gains(1), mixX=pairstack(ra, "attnT", c), w_out=inp["nsa_w_out"][0],
                 pT=pT(0, c), w_in=np.ascontiguousarray(inp["conv_w_in"][0][:, pc]))
        m.update(tail_w(0)); maps.append(m)
    r1 = _run(build_T_prog("linear", True, "glu", F=4096), maps)
    w_gla = np.ascontiguousarray(inp["gla_w_in"][0][:, pg])
    maps = []
    for c in range(NCORES):
        u = [np.asarray(r1[c // 2 * 2 + k]["projT"]) for k in range(2)]
        halo = np.ascontiguousarray(np.stack([u[0][:, NTOK - 32:], u[1][:, NTOK - 32:]], 0))
        m = dict(hT=np.asarray(r1[c]["hT_out"]), sel=sels[c], gains=gains(2), uT=u[c % 2], haloX=halo,
                 dwT=np.ascontiguousarray(inp["conv_dw"][0].T.reshape(16, 128, 31).transpose(1, 0, 2)).astype(f32),
                 cvec=np.ascontiguousarray(np.stack([fm(inp["conv_db"][0]), fm(inp["conv_ln_g"][0]), fm(inp["conv_ln_b"][0])], 1)).astype(f32),
                 w_out=inp["conv_w_out"][0], pT=pT(1, c), w_in=w_gla)
        m.update(tail_w(1)); maps.append(m)
    r2_ = _run(build_T_prog("conv", True, "proj", F=GLA_F, F32TAIL=16), maps)
    maps = []
    for c in range(NCORES):
        r = c % 2
        px = pairstack(r2_, "projT", c)
        wgc = inp["gla_w_gate_up"][0][:, r * 512:(r + 1) * 512]
        bgc = inp["gla_b_gate"][0][r * 512:(r + 1) * 512]
        maps.append(dict(projX0=np.ascontiguousarray(px[:, 0:2048]), projX1=np.ascontiguousarray(px[:, 2048:4096]), gzX=pairstack(r2_, "tail32", c),
                         wg=np.ascontiguousarray(np.concatenate([wgc, np.tile(bgc[None, :], (64, 1))], 0)).astype(f32),
                         ng=np.ascontiguousarray(np.tile(inp["gla_norm_g"][0][None, :], (64, 1))).astype(f32), sel=sels[c]))
    rg = _run(build_gla_prog(), maps)
    maps = []
    for c in range(NCORES):
        m = dict(hT=np.asarray(r2_[c]["hT_out"]), sel=sels[c], gains=gains(3), mixX=pairstack(rg, "onT", c),
                 rT=np.ascontiguousarray(np.asarray(r2_[c]["projT"])[4096:6144]), w_out=inp["gla_w_out"][0], pT=pT(2, c), w_in=w_nsa[1])
        m.update(tail_w(2)); maps.append(m)
    r3 = _run(build_T_prog("gla", True, "proj", F=NSA_F, F32TAIL=48), maps)
    rb = nsa_launch(1, r3)
    maps = []
    for c in range(NCORES):
        m = dict(hT=np.asarray(r3[c]["hT_out"]), sel=sels[c], gains=gains(4), mixX=pairstack(rb, "attnT", c), w_out=inp["nsa_w_out"][1], pT=pT(3, c))
        m.update(tail_w(3)); maps.append(m)
    r4 = _run(build_T_prog("linear", True, "final"), maps)
    out = np.zeros((4, 4096, D), f32)
    for c in range(NCORES):
        b, ts = tok(c)
        out[b, ts, :] = np.asarray(r4[c]["outT"]).T
    return out


_NC_CACHE = {}


def kernel(**inputs):
    inp = {k: np.asarray(v) for k, v in inputs.items()}
    return kernel_multi(inp).astype(np.float32)
```

```python
import numpy as np
from contextlib import ExitStack
import concourse.bass as bass
import concourse.mybir as mybir
from concourse.bass_utils import run_bass_kernel_spmd

F32 = mybir.dt.float32
BF16 = mybir.dt.bfloat16
AF = mybir.ActivationFunctionType
ALU = mybir.AluOpType
AX = mybir.AxisListType


class Ev:
    __slots__ = ("sem", "val")

    def __init__(self, sem, val):
        self.sem = sem
        self.val = val


class Buf:
    __slots__ = ("name", "w", "r", "excl")

    def __init__(self, name="", excl=False):
        self.name = name
        self.w = None
        self.r = {}
        self.excl = excl


def PBuf():
    return Buf(excl=True)


def _flat(x, out=None):
    if out is None:
        out = []
    if x is None:
        return out
    if isinstance(x, Buf):
        out.append(x)
    else:
        for y in x:
            _flat(y, out)
    return out


class _Rec:
    def __getattr__(self, name):
        def f(*a, **k):
            return (name, a, k)
        return f


R = _Rec()


def _call(fn):
    if callable(fn):
        return fn
    name, a, k = fn

    def call(e):
        try:
            return getattr(e, name)(*a, **k)
        except Exception:
            print("FAILED OP:", name, [str(x)[:200] for x in a], {kk: str(v)[:200] for kk, v in k.items()}, flush=True)
            raise
    return call


class Prog:
    def __init__(self, nc, stack, ndma=8):
        self.nc = nc
        self._stack = stack
        self.streams = {e: [] for e in ("sync", "act", "pool", "dve", "pe")}
        self.esem = {e: stack.enter_context(nc.semaphore("sem_" + e))
                     for e in ("act", "pool", "dve", "pe")}
        self.ecnt = {e: 0 for e in self.esem}
        self.nd = ndma
        self.dsem = {q: [stack.enter_context(nc.semaphore("dma_%s_%d" % (q, i)))
                         for i in range(ndma)] for q in ("sync", "pool")}
        self.dcnt = {q: [0] * ndma for q in self.dsem}
        self.dnext = {q: 0 for q in self.dsem}
        self.waited = {}
        self.pe_pending = []
        self.ninst = 0

    def _wait(self, eng, ev):
        if ev is None:
            return
        assert ev.val is not None, "wait on unresolved PE event"
        key = (eng, ev.sem.num)
        if self.waited.get(key, 0) >= ev.val:
            return
        self.waited[key] = ev.val
        sem, val = ev.sem, ev.val
        self.streams[eng].append(lambda e: e.wait_ge(sem, val))

    def _deps(self, eng, R, W):
        pesem = self.esem["pe"]
        for b in R:
            ev = b.w
            if ev is not None and not (eng == "pe" and ev.sem is pesem):
                self._wait(eng, ev)
        for b in W:
            ev = b.w
            if ev is not None and not (eng == "pe" and ev.sem is pesem):
                self._wait(eng, ev)
            for ev in b.r.values():
                if not (eng == "pe" and ev.sem is pesem):
                    self._wait(eng, ev)

    def _post(self, ev, R, W):
        k = ev.sem.num
        for b in R:
            b.r[k] = ev
        for b in W:
            b.w = ev
            b.r = {}

    def sync_to(self, eng, bufs):
        self._deps(eng, [], _flat(bufs))

    def op(self, eng, fn, reads=(), writes=(), inc=True):
        Rd = _flat(reads)
        W = _flat(writes)
        if any(b.excl for b in Rd):
            W = W + [b for b in Rd if b.excl]
            Rd = [b for b in Rd if not b.excl]
        fn = _call(fn)
        self._deps(eng, Rd, W)
        sem = self.esem[eng]
        self.ninst += 1
        if inc:
            self.ecnt[eng] += 1
            ev = Ev(sem, self.ecnt[eng])
            self.streams[eng].append(lambda e: fn(e).then_inc(sem, 1))
            if eng == "pe":
                for p in self.pe_pending:
                    p.val = ev.val
                self.pe_pending = []
        else:
            assert eng == "pe"
            ev = Ev(sem, None)
            self.pe_pending.append(ev)
            self.streams[eng].append(lambda e: fn(e))
        self._post(ev, Rd, W)
        return ev

    def dma(self, q, out, in_, reads=(), writes=(), **kw):
        Rd = _flat(reads)
        W = _flat(writes)
        i = self.dnext[q]
        self.dnext[q] = (i + 1) % self.nd
        sem = self.dsem[q][i]
        prev = self.dcnt[q][i]
        if prev > 0:
            self._wait(q, Ev(sem, 16 * prev))
        self._deps(q, Rd, W)
        self.dcnt[q][i] += 1
        ev = Ev(sem, 16 * self.dcnt[q][i])
        self.ninst += 1
        self.streams[q].append(
            lambda e: e.dma_start(out=out, in_=in_, **kw).then_inc(sem, 16))
        self._post(ev, Rd, W)
        return ev

    def finish(self):
        for q in self.dsem:
            for i in range(self.nd):
                if self.dcnt[q][i] > 0:
                    self._wait("sync", Ev(self.dsem[q][i], 16 * self.dcnt[q][i]))
        for e in self.esem:
            if self.ecnt[e] > 0:
                self._wait("sync", Ev(self.esem[e], self.ecnt[e]))
        if hasattr(self, "ccsem") and self.cccnt > 0:
            self._wait("sync", Ev(self.ccsem, self.cccnt))

    def emit(self):
        self.finish()
        st = self.streams
        with self.nc.Block() as block:
            @block.sync
            def _(e):
                for f in st["sync"]:
                    f(e)

            @block.scalar
            def _(e):
                for f in st["act"]:
                    f(e)

            @block.gpsimd
            def _(e):
                for f in st["pool"]:
                    f(e)

            @block.vector
            def _(e):
                for f in st["dve"]:
                    f(e)

            @block.tensor
            def _(e):
                for f in st["pe"]:
                    f(e)


def _prog_cc(self, kind, ins, outs, groups, reads=(), writes=(), stack=None):
    if not hasattr(self, "ccsem"):
        self.ccsem = self._stack.enter_context(self.nc.semaphore("cc_sem"))
        self.cccnt = 0
    Rd = _flat(reads)
    W = _flat(writes)
    sem = self.ccsem
    if self.cccnt > 0:
        self._wait("pool", Ev(sem, self.cccnt))
    self._deps("pool", Rd, W)
    self.cccnt += self.CC_INC
    ev = Ev(sem, self.cccnt)
    inc = self.CC_INC
    self.streams["pool"].append(lambda e: e.collective_compute(
        kind, ALU.bypass, replica_groups=groups, ins=ins, outs=outs).then_inc(sem, inc))
    self._post(ev, Rd, W)
    return ev


Prog.cc = _prog_cc
Prog.CC_INC = 1


def _prog_barrier(self):
    evs = []
    for e in self.esem:
        if self.ecnt[e] > 0:
            evs.append(Ev(self.esem[e], self.ecnt[e]))
    for q in self.dsem:
        for i in range(self.nd):
            if self.dcnt[q][i] > 0:
                evs.append(Ev(self.dsem[q][i], 16 * self.dcnt[q][i]))
    assert not self.pe_pending
    for eng in self.streams:
        for ev in evs:
            self._wait(eng, ev)


Prog.barrier = _prog_barrier


def _prog_wait_all(self, ev):
    for eng in self.streams:
        self._wait(eng, ev)


Prog.wait_all = _prog_wait_all


_uid = [0]

D = 2048
DFF = 8192
T = 1024
TT = T // 512
NTOK = 2048
NG = NTOK // T
CONV_W = 31
HALO = 30


class Ctx:
    pass


def _dram(nc, name, shape, dt, kind):
    return nc.dram_tensor(name, list(shape), dt, kind=kind).ap()


def emit_T(P, nc, io, pre, tail, post, F=0, F32TAIL=0):
    hT_in = io["hT"]
    if pre in ("linear", "gla"):
        mixX = io["mixX"]; w_out = io["w_out"]
        if pre == "gla":
            rT = io["rT"]
    elif pre == "conv":
        uT = io["uT"]; haloX = io["haloX"]; dwT = io["dwT"]; cvec = io["cvec"]; w_out = io["w_out"]
    if tail:
        w1 = io["w1"]; w2 = io["w2"]; wg = io["wg"]; wp = io["wp"]; pT = io["pT"]
    gains = io["gains"]
    if post in ("proj", "glu"):
        w_in = io["w_in"]; projT = io["projT"]; hT_out = io["hT_out"]
        if F32TAIL:
            tail32 = io["tail32"]
    else:
        outT = io["outT"]

    _uid[0] += 1
    st = ExitStack()
    with st:
        def A(name, shape, dt):
            return st.enter_context(nc.sbuf_tensor("T_%d_" % _uid[0] + name, shape, dt))

        def PSA(name, shape, dt=F32):
            return st.enter_context(nc.psum_tensor("T_%d_" % _uid[0] + name, shape, dt))
        h32 = A("h32", [128, 16, T], F32); hB = [Buf() for _ in range(16)]
        xb = A("xb", [128, 16, T], BF16); xbB = [Buf() for _ in range(16)]
        ab = A("ab", [128, 16, T], BF16); abB = [Buf() for _ in range(16)]
        NW = 2
        wt32 = [A("wt32_%d" % i, [128, 16, 256], F32) for i in range(NW)]; wt32B = [Buf() for _ in range(NW)]
        wtb = [A("wtb_%d" % i, [128, 16, 256], BF16) for i in range(NW)]; wtbB = [Buf() for _ in range(NW)]
        scr = [wt32[i][:].rearrange("p a b -> p (a b)")[:, 0:T] for i in range(NW)]; scrB = wt32B
        rstd = A("rstd", [128, T], F32); rstdB = Buf()
        tmpa = A("tmpa", [128, 4, 512], F32); tmp = [tmpa[:, i, :] for i in range(4)]; tmpB = [Buf() for _ in range(4)]
        obf = [A("obf%d" % i, [128, T], BF16) for i in range(2)]; obfB = [Buf(), Buf()]
        gn = A("gn", [128, 3, 16], F32); gnB = Buf()
        sel = A("sel", [128, 2], F32); selB = Buf()
        ones = A("ones", [128, 128], F32); onesB = Buf()
        NPS = 2
        ps_mm = [[PSA("psmm%d_%d" % (i, t), [128, 512]) for t in range(TT)] for i in range(NPS)]
        ps_mmB = [[PBuf() for t in range(TT)] for i in range(NPS)]
        ps_st = [PSA("psst%d" % t, [128, 512]) for t in range(2 * TT)]
        ps_stB = [PBuf() for t in range(2 * TT)]
        if tail:
            pb = A("pb", [128, 2, T], BF16); pbB = [Buf(), Buf()]
        if pre == "conv":
            identb = A("identb", [128, 128], BF16); identB = Buf()
            ident32 = A("ident32", [128, 128], F32)
            dw_sb = A("dw_sb", [128, 16, CONV_W], F32); dwB = Buf()
            cv_sb = A("cv_sb", [128, 3, 16], F32); cvB = Buf()
            abf = ab[:].rearrange("p a b -> p (a b)")
            NDJ = CONV_W * 128
            Dj = [abf[:, i * 8192: i * 8192 + NDJ].rearrange("p (j m) -> p j m", j=CONV_W) for i in range(2)]
            ub = [abf[:, i * 8192 + NDJ: i * 8192 + NDJ + HALO + T] for i in range(2)]
            DjB = [abB[0:8], abB[8:16]]
            DjJ = [[Buf() for _ in range(CONV_W)] for _ in range(2)]; ubB = [Buf(), Buf()]
            mean = tmpa[:, 0:2, :].rearrange("p a t -> p (a t)"); meanB = tmpB[0:2]

        C = Ctx(); C.wi = 0; C.pi = 0; C.ci = 0; C.ti = 0; C.oi = 0

        P.dma("sync", gn[:], gains, writes=gnB)
        P.dma("sync", sel[:], io["sel"], writes=selB)
        P.op("pool", R.memset(ones[:], 1.0), writes=onesB)
        if pre == "conv":
            P.dma("sync", dw_sb[:], dwT, writes=dwB)
            P.dma("sync", cv_sb[:], cvec, writes=cvB)
            P.op("pool", R.memset(ident32[:], 1.0), writes=identB)
            P.op("pool", R.affine_select(out=ident32[:], in_=ident32[:], pattern=[[-1, 128]],
                                         compare_op=ALU.is_equal, fill=0.0, base=0, channel_multiplier=1),
                 reads=identB, writes=identB)
            P.op("dve", R.tensor_copy(out=identb[:], in_=ident32[:]), reads=identB, writes=identB)

        cast_rot = ["act", "dve"]

        def cast(eng, out, in_, reads, writes):
            if eng == "act":
                P.op("act", R.copy(out=out, in_=in_), reads=reads, writes=writes)
            else:
                P.op(eng, R.tensor_copy(out=out, in_=in_), reads=reads, writes=writes)

        def linear(xt, xB, KC, Wd, r0, c0, ncols, epi, WB=None):
            nblk = (ncols + 255) // 256
            for blk in range(nblk):
                cw = min(256, ncols - blk * 256)
                wi = C.wi % NW; C.wi += 1
                w32, wb_ = wt32[wi], wtb[wi]
                src = Wd[r0:r0 + KC * 128, c0 + blk * 256: c0 + blk * 256 + cw].rearrange("(kc p) c -> p kc c", p=128)
                P.dma("sync", w32[:, 0:KC, 0:cw], src, reads=WB, writes=wt32B[wi])
                ce = cast_rot[C.ci % len(cast_rot)]; C.ci += 1
                cast(ce, wb_[:, 0:KC, 0:cw], w32[:, 0:KC, 0:cw], wt32B[wi], wtbB[wi])
                for j in range((cw + 127) // 128):
                    m = min(128, cw - j * 128)
                    pi = C.pi % NPS; C.pi += 1
                    for kc in range(KC):
                        for tt in range(TT):
                            P.op("pe", R.matmul(ps_mm[pi][tt][0:m, :], lhsT=wb_[:, kc, j * 128:j * 128 + m],
                                                rhs=xt[:, kc, tt * 512:(tt + 1) * 512], start=(kc == 0), stop=(kc == KC - 1)),
                                 reads=[wtbB[wi], xB[kc]], writes=ps_mmB[pi][tt], inc=(kc == KC - 1))
                    epi(blk * 2 + j, m, ps_mm[pi], ps_mmB[pi])

        def sl_(tt):
            return slice(tt * 512, (tt + 1) * 512)

        def norm(gi, out_bf=True, eps=1e-6):
            for kc in range(16):
                s = kc % 2
                P.op("act", R.activation(out=scr[s], in_=h32[:, kc, :], func=AF.Square), reads=hB[kc], writes=scrB[s])
                for tt in range(TT):
                    P.op("pe", R.matmul(ps_st[tt][:], lhsT=ones[:], rhs=scr[s][:, sl_(tt)], start=(kc == 0), stop=(kc == 15)),
                         reads=[onesB, scrB[s]], writes=ps_stB[tt])
            for tt in range(TT):
                P.op("dve", R.tensor_scalar(out=rstd[:, sl_(tt)], in0=ps_st[tt][:], scalar1=1.0 / D, scalar2=eps,
                                            op0=ALU.mult, op1=ALU.add), reads=ps_stB[tt], writes=rstdB)
            P.op("act", R.activation(out=rstd[:], in_=rstd[:], func=AF.Sqrt), reads=rstdB, writes=rstdB)
            P.op("dve", R.reciprocal(out=rstd[:], in_=rstd[:]), reads=rstdB, writes=rstdB)
            if out_bf:
                for kc in range(16):
                    P.op("dve", R.scalar_tensor_tensor(out=xb[:, kc, :], in0=h32[:, kc, :], scalar=gn[:, gi, kc:kc + 1],
                                                       in1=rstd[:], op0=ALU.mult, op1=ALU.mult),
                         reads=[hB[kc], gnB, rstdB], writes=xbB[kc])

        def epi_add_h(j, m, ps, psB):
            for tt in range(TT):
                P.op("dve", R.tensor_tensor(out=h32[:, j, sl_(tt)], in0=h32[:, j, sl_(tt)], in1=ps[tt][:], op=ALU.add),
                     reads=[hB[j], psB[tt]], writes=hB[j])

        for g in range(NG):
            tok = slice(g * T, (g + 1) * T)
            hview = hT_in.rearrange("(c p) t -> p c t", p=128)

            if pre == "conv":
                for cc in range(16):
                    di = cc % 2
                    for en in ("dve", "act", "sync"):
                        P.sync_to(en, DjB[di])
                    for j in range(CONV_W):
                        if j % 2 == 0:
                            P.op("dve", R.tensor_scalar(out=Dj[di][:, j, :], in0=identb[:], scalar1=dw_sb[:, cc, j:j + 1],
                                                        scalar2=None, op0=ALU.mult), reads=[identB, dwB], writes=DjJ[di][j])
                        else:
                            P.op("act", R.activation(out=Dj[di][:, j, :], in_=identb[:], func=AF.Identity,
                                                     scale=dw_sb[:, cc, j:j + 1]), reads=[identB, dwB], writes=DjJ[di][j])
                    if g == 0:
                        P.dma("sync", ub[di][:, 0:HALO], haloX[0, cc * 128:(cc + 1) * 128, 2:2 + HALO], writes=ubB[di])
                        P.dma("sync", ub[di][:, HALO:HALO + T], uT[cc * 128:(cc + 1) * 128, 0:T], writes=ubB[di])
                        P.op("dve", R.tensor_scalar(out=ub[di][:, 0:HALO], in0=ub[di][:, 0:HALO], scalar1=sel[:, 1:2], scalar2=None, op0=ALU.mult),
                             reads=[ubB[di], selB], writes=ubB[di])
                    else:
                        P.dma("sync", ub[di], uT[cc * 128:(cc + 1) * 128, g * T - HALO: g * T + T], writes=ubB[di])
                    pi = C.pi % NPS; C.pi += 1
                    for j in range(CONV_W):
                        for tt in range(TT):
                            lastj = (j == CONV_W - 1)
                            P.op("pe", R.matmul(ps_mm[pi][tt][:], lhsT=Dj[di][:, j, :],
                                                rhs=ub[di][:, j + tt * 512: j + tt * 512 + 512],
                                                start=(j == 0), stop=lastj),
                                 reads=[DjJ[di][j], ubB[di]] + (DjB[di] if lastj else []), writes=ps_mmB[pi][tt], inc=lastj)
                    for tt in range(TT):
                        P.op("act", R.activation(out=h32[:, cc, sl_(tt)], in_=ps_mm[pi][tt][:], func=AF.Identity,
                                                 bias=cv_sb[:, 0, cc:cc + 1]), reads=[ps_mmB[pi][tt], cvB], writes=hB[cc])
                for kc in range(16):
                    s = kc % 2
                    P.op("act", R.activation(out=scr[s], in_=h32[:, kc, :], func=AF.Square), reads=hB[kc], writes=scrB[s])
                    for tt in range(TT):
                        P.op("pe", R.matmul(ps_st[tt][:], lhsT=ones[:], rhs=scr[s][:, sl_(tt)], start=(kc == 0), stop=(kc == 15)),
                             reads=[onesB, scrB[s]], writes=ps_stB[tt])
                        P.op("pe", R.matmul(ps_st[TT + tt][:], lhsT=ones[:], rhs=h32[:, kc, sl_(tt)], start=(kc == 0), stop=(kc == 15)),
                             reads=[onesB, hB[kc]], writes=ps_stB[TT + tt])
                for tt in range(TT):
                    P.op("dve", R.tensor_scalar(out=mean[:, sl_(tt)], in0=ps_st[TT + tt][:], scalar1=1.0 / D, scalar2=None, op0=ALU.mult),
                         reads=ps_stB[TT + tt], writes=meanB)
                    P.op("dve", R.tensor_tensor(out=scr[0][:, sl_(tt)], in0=mean[:, sl_(tt)], in1=mean[:, sl_(tt)], op=ALU.mult), reads=meanB, writes=scrB[0])
                    P.op("dve", R.scalar_tensor_tensor(out=rstd[:, sl_(tt)], in0=ps_st[tt][:], scalar=1.0 / D, in1=scr[0][:, sl_(tt)],
                                                       op0=ALU.mult, op1=ALU.subtract), reads=[ps_stB[tt], scrB[0]], writes=rstdB)
                P.op("dve", R.tensor_scalar(out=rstd[:], in0=rstd[:], scalar1=1e-5, scalar2=None, op0=ALU.add), reads=rstdB, writes=rstdB)
                P.op("act", R.activation(out=rstd[:], in_=rstd[:], func=AF.Sqrt), reads=rstdB, writes=rstdB)
                P.op("dve", R.reciprocal(out=rstd[:], in_=rstd[:]), reads=rstdB, writes=rstdB)
                for kc in range(16):
                    s = kc % 2
                    P.op("dve", R.tensor_tensor(out=scr[s], in0=h32[:, kc, :], in1=mean, op=ALU.subtract),
                         reads=[hB[kc], meanB], writes=scrB[s])
                    P.op("dve", R.tensor_tensor(out=scr[s], in0=scr[s], in1=rstd[:], op=ALU.mult), reads=[scrB[s], rstdB], writes=scrB[s])
                    P.op("act", R.activation(out=xb[:, kc, :], in_=scr[s], func=AF.Silu, scale=cv_sb[:, 1, kc:kc + 1],
                                             bias=cv_sb[:, 2, kc:kc + 1]), reads=[scrB[s], cvB], writes=xbB[kc])
            elif pre in ("linear", "gla"):
                for c4 in range(4):
                    rk, f0 = divmod(c4 * 512, 1024)
                    for cand, (dst, dstB) in enumerate(((xb, xbB), (ab, abB))):
                        src = mixX[rk, f0:f0 + 512, cand * NTOK + g * T: cand * NTOK + (g + 1) * T].rearrange("(c p) t -> p c t", p=128)
                        P.dma("sync", dst[:, c4 * 4:(c4 + 1) * 4, :], src, writes=dstB[c4 * 4:(c4 + 1) * 4])
                for kc in range(16):
                    P.op("dve", R.tensor_scalar(out=xb[:, kc, :], in0=xb[:, kc, :], scalar1=sel[:, 0:1], scalar2=None, op0=ALU.mult),
                         reads=[xbB[kc], selB], writes=xbB[kc])
                    P.op("dve", R.scalar_tensor_tensor(out=xb[:, kc, :], in0=ab[:, kc, :], scalar=sel[:, 1:2], in1=xb[:, kc, :], op0=ALU.mult, op1=ALU.add),
                         reads=[abB[kc], xbB[kc], selB], writes=xbB[kc])
                if pre == "gla":
                    rview = rT.rearrange("(c p) t -> p c t", p=128)
                    for c4 in range(4):
                        P.dma("sync", ab[:, c4 * 4:(c4 + 1) * 4, :], rview[:, c4 * 4:(c4 + 1) * 4, tok], writes=abB[c4 * 4:(c4 + 1) * 4])
                    for kc in range(16):
                        P.op("act", R.activation(out=ab[:, kc, :], in_=ab[:, kc, :], func=AF.Silu), reads=abB[kc], writes=abB[kc])
                        P.op("dve", R.tensor_tensor(out=xb[:, kc, :], in0=xb[:, kc, :], in1=ab[:, kc, :], op=ALU.mult),
                             reads=[xbB[kc], abB[kc]], writes=xbB[kc])

            for c4 in range(4):
                P.dma("sync", h32[:, c4 * 4:(c4 + 1) * 4, :], hview[:, c4 * 4:(c4 + 1) * 4, tok], writes=hB[c4 * 4:(c4 + 1) * 4])

            if pre is not None:
                linear(xb, xbB, 16, w_out, 0, 0, D, epi_add_h, io.get('w_outB'))

            if tail:
                norm(0)
                for qd in range(4):
                    def epi_ffn1(j, m, ps, psB):
                        for tt in range(TT):
                            ti = C.ti % 4; C.ti += 1
                            P.op("act", R.activation(out=tmp[ti], in_=ps[tt][:], func=AF.Relu), reads=psB[tt], writes=tmpB[ti])
                            P.op("act", R.activation(out=ab[:, j, sl_(tt)], in_=tmp[ti], func=AF.Square), reads=tmpB[ti], writes=abB[j])
                    linear(xb, xbB, 16, w1, 0, qd * 2048, 2048, epi_ffn1, io.get('w1B'))
                    linear(ab, abB, 16, w2, qd * 2048, 0, D, epi_add_h, io.get('w2B'))
                norm(1)

                def epi_gate(j, m, ps, psB):
                    for tt in range(TT):
                        P.op("act", R.activation(out=ab[:, j, sl_(tt)], in_=ps[tt][:], func=AF.Sigmoid), reads=psB[tt], writes=abB[j])
                linear(xb, xbB, 16, wg, 0, 0, D, epi_gate, io.get('wgB'))
                wi = C.wi % NW; C.wi += 1
                p32 = wt32[wi][:].rearrange("p a b -> p (a b)")[:, 0:2 * T].rearrange("p (c t) -> p c t", c=2)
                P.dma("sync", p32, pT.rearrange("(c p) t -> p c t", p=128)[:, :, tok], writes=wt32B[wi])
                for c in range(2):
                    P.op("pool", R.tensor_copy(out=pb[:, c, :], in_=p32[:, c, :]), reads=wt32B[wi], writes=pbB[c])

                def epi_ple(j, m, ps, psB):
                    for tt in range(TT):
                        ti = C.ti % 4; C.ti += 1
                        P.op("dve", R.tensor_tensor(out=tmp[ti], in0=ps[tt][:], in1=ab[:, j, sl_(tt)], op=ALU.mult),
                             reads=[psB[tt], abB[j]], writes=tmpB[ti])
                        P.op("dve", R.tensor_tensor(out=h32[:, j, sl_(tt)], in0=h32[:, j, sl_(tt)], in1=tmp[ti], op=ALU.add),
                             reads=[hB[j], tmpB[ti]], writes=hB[j])
                linear(pb, pbB, 2, wp, 0, 0, D, epi_ple, io.get('wpB'))

            if post in ("proj", "glu"):
                oview = hT_out.rearrange("(c p) t -> p c t", p=128)
                for c4 in range(4):
                    P.dma("pool", oview[:, c4 * 4:(c4 + 1) * 4, tok], h32[:, c4 * 4:(c4 + 1) * 4, :], reads=hB[c4 * 4:(c4 + 1) * 4])
                norm(2)
                if post == "proj":
                    nfull = F // 128

                    def epi_store(j, m, ps, psB):
                        if j == nfull:
                            wi = C.wi % NW; C.wi += 1
                            dst_t = scr[wi]; dB = scrB[wi]
                        else:
                            oi = C.oi % 2; C.oi += 1
                            dst_t = obf[oi][:]; dB = obfB[oi]
                        for tt in range(TT):
                            if tt % 2 == 0:
                                P.op("act", R.copy(out=dst_t[0:m, sl_(tt)], in_=ps[tt][0:m, :]), reads=psB[tt], writes=dB)
                            else:
                                P.op("dve", R.tensor_copy(out=dst_t[0:m, sl_(tt)], in_=ps[tt][0:m, :]), reads=psB[tt], writes=dB)
                        if j == nfull:
                            P.dma("pool", tail32[0:m, tok], dst_t[0:m, :], reads=dB)
                        else:
                            P.dma("pool", projT[j * 128:j * 128 + m, tok], dst_t[0:m, :], reads=dB)
                    linear(xb, xbB, 16, w_in, 0, 0, F, epi_store, io.get('w_inB'))
                else:
                    def epi_glu(j, m, ps, psB):
                        if j % 2 == 0:
                            C.glu_t = [(C.ti + k) % 4 for k in range(TT)]; C.ti += TT
                            for tt in range(TT):
                                ti = C.glu_t[tt]
                                P.op("act", R.activation(out=tmp[ti], in_=ps[tt][:], func=AF.Sigmoid), reads=psB[tt], writes=tmpB[ti])
                        else:
                            oi = C.oi % 2; C.oi += 1
                            for tt in range(TT):
                                ti = C.glu_t[tt]
                                P.op("dve", R.tensor_tensor(out=obf[oi][:, sl_(tt)], in0=ps[tt][:], in1=tmp[ti], op=ALU.mult),
                                     reads=[psB[tt], tmpB[ti]], writes=obfB[oi])
                            fc = j // 2
                            P.dma("pool", projT[fc * 128:(fc + 1) * 128, tok], obf[oi][:], reads=obfB[oi])
                    linear(xb, xbB, 16, w_in, 0, 0, F, epi_glu, io.get('w_inB'))
            else:
                norm(2, out_bf=False)
                oview = outT.rearrange("(c p) t -> p c t", p=128)
                for kc in range(16):
                    wi = C.wi % NW; C.wi += 1
                    P.op("dve", R.scalar_tensor_tensor(out=scr[wi], in0=h32[:, kc, :], scalar=gn[:, 2, kc:kc + 1], in1=rstd[:],
                                                       op0=ALU.mult, op1=ALU.mult), reads=[hB[kc], gnB, rstdB], writes=scrB[wi])
                    P.dma("pool", oview[:, kc, tok], scr[wi], reads=scrB[wi])
    P.barrier()


_uid = [0]

S = 4096
NQT = 32
SCALE = 128 ** -0.5
NEGB = 30000.0


def emit_nsa(P, nc, projX, gateX, cw, attnT, selD):
    _uid[0] += 1
    st = ExitStack()
    with st:
        def A(name, shape, dt):
            return st.enter_context(nc.sbuf_tensor("nsa_%d_" % _uid[0] + name, shape, dt))

        def PS(name, shape, dt=F32):
            return st.enter_context(nc.psum_tensor("nsa_%d_" % _uid[0] + name, shape, dt))

        q_sb = A("q", [128, 4, S], BF16); qB = Buf()
        kc_sb = A("kc", [128, S], BF16); kcB = Buf()
        vc_sb = A("vc", [128, S], BF16); vcB = Buf()
        ks_sb = A("ks", [128, S], BF16); ksB = Buf()
        kw_sb = A("kw", [128, S], BF16); kwB = Buf()
        vT_sb = A("vT", [128, S], BF16); vTB = Buf()
        vs_e = A("vse", [128, NQT, 129], BF16); vsB = Buf()
        vw_e = A("vwe", [128, NQT, 129], BF16); vwB = Buf()
        gT_sb = A("gT", [12, S], F32); gTB = Buf()
        gstg = A("gstg", [12, S], F32); gstgB = Buf()
        stg = A("stg", [128, S], BF16); stgB = Buf()
        sel = A("sel", [128, 2], F32); selB = Buf()
        g_sb = A("g", [128, NQT, 12], F32); gB = Buf()
        w1s = A("w1s", [128, 32, 128], F32); w1sB = Buf()
        w1b = [A("w1b%d" % i, [128, 32, 128], BF16) for i in range(2)]; w1bB = [Buf(), Buf()]
        w2s = A("w2s", [128, 128], F32); w2sB = Buf()
        w2b = [A("w2b%d" % i, [128, 128], BF16) for i in range(2)]; w2bB = [Buf(), Buf()]
        pos_s = A("poss", [128, 32], F32); possB = Buf()
        posb = [A("posb%d" % i, [128, 32], BF16) for i in range(2)]; posbB = [Buf(), Buf()]
        cb = [A("cb%d" % i, [128, 1], F32) for i in range(2)]; cbB = [Buf(), Buf()]
        hid = A("hid", [128, 256], BF16); hidB = Buf()
        kcmpT = A("kcmpT", [128, 256], BF16); kcmpB = Buf()
        vcmp = A("vcmp", [128, 2, 193], BF16); vcmpB = Buf()
        ident32 = A("ident32", [128, 128], F32); idB = Buf()
        identb = A("identb", [128, 128], BF16)
        E32 = A("E32", [64, S], F32)
        Eb = A("Eb", [64, S], BF16); EB = Buf()
        ov32 = A("ov32", [128, 2, 64], F32); ovB = Buf()
        pt = [A("pt%d" % i, [128, 512], BF16) for i in range(3)]; ptB = [Buf() for _ in range(3)]
        imp = A("imp", [128, 64], F32); impB = Buf()
        wk = A("wk", [128, 64], F32); wkB = Buf()
        m8 = A("m8", [128, 8], F32); m8B = Buf()
        negm = A("negm", [128, 64], F32); negmB = Buf()
        ind = A("ind", [128, 3, 64], F32); indB = [Buf(), Buf(), Buf()]
        addm = A("addm", [128, 64], F32); addB = Buf()
        ones64 = A("ones64", [128, 64], F32); onesB = Buf()
        negT4 = A("negT4", [64, 4, 128], BF16); negTB = Buf()
        den = A("den", [128, 3, 4], F32); denB = [Buf(), Buf(), Buf()]
        coef = A("coef", [128, 3, 4], F32); coefB = [Buf(), Buf(), Buf()]
        o32 = A("o32", [128, 4, 128], F32); o32B = Buf()
        oT = [A("oT%d" % i, [128, 4, 128], BF16) for i in range(2)]; oTB = [Buf(), Buf()]

        ps_s = [PS("s%d" % i, [128, 512]) for i in range(2)]; ps_sB = [PBuf(), PBuf()]
        ps_c = [PS("c%d" % i, [128, 512]) for i in range(2)]; ps_cB = [PBuf(), PBuf()]
        ps_e = [PS("e%d" % i, [128, 512]) for i in range(2)]; ps_eB = [PBuf(), PBuf()]
        ps_w = [PS("w%d" % i, [128, 512]) for i in range(2)]; ps_wB = [PBuf(), PBuf()]

        C = type("C", (), {})(); C.si = 0; C.pi = 0; C.oi = 0

        P.dma("sync", sel[:], selD, writes=selB)
        P.op("pool", R.memset(ident32[:], 1.0), writes=idB)
        P.op("pool", R.affine_select(out=ident32[:], in_=ident32[:], pattern=[[-1, 128]], compare_op=ALU.is_equal,
                                     fill=0.0, base=0, channel_multiplier=1), reads=idB, writes=idB)
        P.op("dve", R.tensor_copy(out=identb[:], in_=ident32[:]), reads=idB, writes=idB)
        P.op("pool", R.memset(ones64[:], 1.0), writes=onesB)
        P.op("pool", R.memset(E32[:], 1.0), writes=EB)
        P.op("pool", R.affine_select(out=E32[:], in_=E32[:], pattern=[[1, S]], compare_op=ALU.is_ge, fill=0.0,
                                     base=0, channel_multiplier=-64), reads=EB, writes=EB)
        P.op("pool", R.affine_select(out=E32[:], in_=E32[:], pattern=[[-1, S]], compare_op=ALU.is_ge, fill=0.0,
                                     base=63, channel_multiplier=64), reads=EB, writes=EB)
        P.op("dve", R.tensor_copy(out=Eb[:], in_=E32[:]), reads=EB, writes=EB)
        P.op("pool", R.memset(ov32[:], 1.0), writes=ovB)
        for c in range(2):
            P.op("pool", R.affine_select(out=ov32[:, c, :], in_=ov32[:, c, :], pattern=[[-4, 64]], compare_op=ALU.is_ge, fill=0.0,
                                         base=128 * c + 1, channel_multiplier=1), reads=ovB, writes=ovB)
            P.op("pool", R.affine_select(out=ov32[:, c, :], in_=ov32[:, c, :], pattern=[[4, 64]], compare_op=ALU.is_ge, fill=0.0,
                                         base=3 - 128 * c, channel_multiplier=-1), reads=ovB, writes=ovB)
        P.op("pool", R.memset(vcmp[:], 0.0), writes=vcmpB)
        P.op("dve", R.tensor_copy(out=vcmp[:, :, 129:193], in_=ov32[:]), reads=ovB, writes=vcmpB)
        P.op("pool", R.memset(vcmp[:, :, 128:129], 1.0), writes=vcmpB)
        P.op("pool", R.memset(vs_e[:, :, 128:129], 1.0), writes=vsB)
        P.op("pool", R.memset(vw_e[:, :, 128:129], 1.0), writes=vwB)
        P.op("pool", R.memset(hid[:], 0.0), writes=hidB)
        P.op("pool", R.memset(kcmpT[:], 0.0), writes=kcmpB)
        for i, (w1n, w2n, pn) in enumerate((("w1k", "w2k", "posTk"), ("w1v", "w2v", "posTv"))):
            P.dma("sync", w1s[:], cw[w1n].rearrange("(l d) j -> d l j", d=128), reads=cw.get(w1n + "B"), writes=w1sB)
            P.op("dve", R.tensor_copy(out=w1b[i][:], in_=w1s[:]), reads=w1sB, writes=w1bB[i])
            P.dma("sync", w2s[:], cw[w2n], writes=w2sB)
            P.op("dve", R.tensor_copy(out=w2b[i][:], in_=w2s[:]), reads=w2sB, writes=w2bB[i])
            P.dma("sync", pos_s[:], cw[pn], writes=possB)
            P.op("dve", R.tensor_copy(out=posb[i][:], in_=pos_s[:]), reads=possB, writes=posbB[i])
            for l in range(32):
                P.op("pe", R.matmul(ps_s[0][:, 0:1], lhsT=w1b[i][:, l, :], rhs=posb[i][:, l:l + 1], start=(l == 0), stop=(l == 31)),
                     reads=[w1bB[i], posbB[i]], writes=ps_sB[0], inc=(l == 31))
            P.op("dve", R.tensor_copy(out=cb[i][:], in_=ps_s[0][:, 0:1]), reads=ps_sB[0], writes=cbB[i])

        def sview(t):
            return t.rearrange("p (r t) -> p r t", r=2)

        def load_sel(dst, dstB, src, base, np_=128, stage=None, stageB=None):
            stage = stg if stage is None else stage
            stageB = stgB if stageB is None else stageB
            for cand, (tgt, tgtB) in enumerate(((dst, dstB), (stage[0:np_, :], stageB))):
                if isinstance(src, list):
                    g_ = cand * 2560 + base
                    r0_, r1_, ap_ = [p_ for p_ in src if p_[0] <= g_ < p_[1]][0]
                    sap = ap_[:, g_ - r0_:g_ - r0_ + np_, :]
                else:
                    sap = src[:, cand * 24 + base:cand * 24 + base + np_, :]
                P.dma("sync", sview(tgt), sap.rearrange("r p t -> p r t"), writes=tgtB)
            P.op("dve", R.tensor_scalar(out=dst, in0=dst, scalar1=sel[0:np_, 0:1], scalar2=None, op0=ALU.mult), reads=[dstB, selB], writes=dstB)
            P.op("dve", R.scalar_tensor_tensor(out=dst, in0=stage[0:np_, :], scalar=sel[0:np_, 1:2], in1=dst, op0=ALU.mult, op1=ALU.add),
                 reads=[stageB, dstB, selB], writes=dstB)

        for gl in range(2):
            for hh in range(4):
                load_sel(q_sb[:, hh, :], qB, projX, (gl * 4 + hh) * 128)
            load_sel(kc_sb[:], kcB, projX, 1024 + gl * 128)
            load_sel(vc_sb[:], vcB, projX, 1280 + gl * 128)
            load_sel(ks_sb[:], ksB, projX, 1536 + gl * 128)
            load_sel(kw_sb[:], kwB, projX, 2048 + gl * 128)
            load_sel(gT_sb[:], gTB, gateX, gl * 12, np_=12, stage=gstg, stageB=gstgB)
            for (ve, veB, base) in ((vs_e, vsB, 1792), (vw_e, vwB, 2304)):
                load_sel(vT_sb[:], vTB, projX, base + gl * 128)
                for k4 in range(NQT // 4):
                    pst = ps_c[k4 % 2]
                    pv = pst[:].bitcast(BF16)[:, 0:512].rearrange("p (a d) -> p a d", a=4)
                    for a in range(4):
                        kt = k4 * 4 + a
                        P.op("pe", R.transpose(pv[:, a, :], vT_sb[:, kt * 128:(kt + 1) * 128], identb[:]),
                             reads=[vTB, idB], writes=ps_cB[k4 % 2], inc=(a == 3))
                    P.op("dve" if k4 % 2 == 0 else "act",
                         (R.tensor_copy if k4 % 2 == 0 else R.copy)(out=ve[:, k4 * 4:(k4 + 1) * 4, 0:128], in_=pv),
                         reads=ps_cB[k4 % 2], writes=veB)
            for k4 in range(NQT // 4):
                pst = ps_e[k4 % 2]
                for a in range(4):
                    kt = k4 * 4 + a
                    P.op("pe", R.transpose(pst[:, a * 12:(a + 1) * 12], gT_sb[0:12, kt * 128:(kt + 1) * 128], ident32[0:12, 0:12]),
                         reads=[gTB, idB], writes=ps_eB[k4 % 2], inc=(a == 3))
                P.op("act", R.activation(out=g_sb[:, k4 * 4:(k4 + 1) * 4, :], in_=pst[:, 0:48].rearrange("p (a c) -> p a c", a=4), func=AF.Sigmoid),
                     reads=ps_eB[k4 % 2], writes=gB)
            for i, (src, srcB) in enumerate(((kc_sb, kcB), (vc_sb, vcB))):
                v3 = src[:].rearrange("p (n s) -> p n s", s=16)
                for l in range(32):
                    rhs = v3[:, 0:255, l] if l < 16 else v3[:, 1:256, l - 16]
                    P.op("pe", R.matmul(ps_s[0][:, 0:255], lhsT=w1b[i][:, l, :], rhs=rhs, start=(l == 0), stop=(l == 31)),
                         reads=[w1bB[i], srcB], writes=ps_sB[0], inc=(l == 31))
                P.op("act", R.activation(out=hid[:, 0:255], in_=ps_s[0][:, 0:255], func=AF.Silu, bias=cb[i][:, 0:1]),
                     reads=[ps_sB[0], cbB[i]], writes=hidB)
                if i == 0:
                    P.op("pe", R.matmul(ps_s[1][:, 0:255], lhsT=w2b[0][:], rhs=hid[:, 0:255], start=True, stop=True),
                         reads=[w2bB[0], hidB], writes=ps_sB[1])
                    P.op("dve", R.tensor_copy(out=kcmpT[:, 0:255], in_=ps_s[1][:, 0:255]), reads=ps_sB[1], writes=kcmpB)
                else:
                    for c in range(2):
                        nk = 128 if c == 0 else 127
                        P.op("pe", R.matmul(ps_s[1][0:nk, c * 128:(c + 1) * 128], lhsT=hid[:, c * 128:c * 128 + nk], rhs=w2b[1][:], start=True, stop=True),
                             reads=[w2bB[1], hidB], writes=ps_sB[1])
                        P.op("dve", R.tensor_copy(out=vcmp[0:nk, c, 0:128], in_=ps_s[1][0:nk, c * 128:(c + 1) * 128]), reads=ps_sB[1], writes=vcmpB)

            def att_tile(qt, kT, kTB, nk, vrhs, vB, pso, psoB, width, first, last, mask, bias_kt=None):
                si = C.si % 2; C.si += 1
                pi = C.pi % 3; C.pi += 1
                qrhs = q_sb[:, :, qt * 128:(qt + 1) * 128]
                P.op("pe", R.matmul(ps_s[si][0:nk, :], lhsT=kT, rhs=qrhs, start=True, stop=(bias_kt is None)),
                     reads=[kTB, qB], writes=ps_sB[si], inc=(bias_kt is None))
                if bias_kt is not None:
                    P.op("pe", R.matmul(ps_s[si][0:nk, :], lhsT=Eb[0:64, bias_kt * 128:(bias_kt + 1) * 128], rhs=negT4[:].rearrange("p a t -> p (a t)"),
                                        start=False, stop=True), reads=[EB, negTB], writes=ps_sB[si])
                P.op("act", R.activation(out=pt[pi][0:nk, :], in_=ps_s[si][0:nk, :], func=AF.Exp, scale=SCALE), reads=ps_sB[si], writes=ptB[pi])
                if mask is not None:
                    pat, base, cm = mask
                    p3 = pt[pi][0:nk, :].rearrange("p (a t) -> p a t", a=4)
                    P.op("pool", R.affine_select(out=p3, in_=p3, pattern=pat, compare_op=ALU.is_ge, fill=0.0, base=base, channel_multiplier=cm),
                         reads=ptB[pi], writes=ptB[pi])
                for hh in range(4):
                    P.op("pe", R.matmul(pso[hh // 2][:, (hh % 2) * width:(hh % 2 + 1) * width], lhsT=pt[pi][0:nk, hh * 128:(hh + 1) * 128], rhs=vrhs,
                                        start=(first and hh % 2 == 0), stop=last, skip_group_check=True), reads=[ptB[pi], vB], writes=psoB[hh // 2], inc=(hh == 3))

            def dens(x, pso, psoB, width):
                for b2 in range(2):
                    P.op("dve", R.tensor_scalar(out=den[:, x, b2 * 2:(b2 + 1) * 2], in0=pso[b2][:, 0:2 * width].rearrange("p (a w) -> p a w", a=2)[:, :, 128],
                                                scalar1=1e-30, scalar2=None, op0=ALU.max), reads=psoB[b2], writes=denB[x])
                P.op("dve", R.reciprocal(out=den[:, x, :], in_=den[:, x, :]), reads=denB[x], writes=denB[x])

            for qt in range(NQT):
                t0 = qt * 128
                chunks = [0] + ([1] if qt >= 16 else [])
                for ci, c in enumerate(chunks):
                    nk = 128 if c == 0 else 127
                    full = (c == 0 and qt >= 17)
                    mask = None if full else ([[0, 4], [1, 128]], t0 - 31 - 2048 * c, -16)
                    att_tile(qt, kcmpT[:, c * 128:c * 128 + nk], kcmpB, nk, vcmp[0:nk, c, :], vcmpB, ps_c, ps_cB, 193,
                             ci == 0, ci == len(chunks) - 1, mask)
                dens(0, ps_c, ps_cB, 193)
                gq = g_sb[:, qt, :].rearrange("p (h x) -> p h x", x=3)
                P.op("dve", R.tensor_tensor(out=coef[:, 0, :], in0=den[:, 0, :], in1=gq[:, :, 0], op=ALU.mult), reads=[denB[0], gB], writes=coefB[0])
                for hh in range(4):
                    P.op("dve", R.tensor_scalar(out=o32[:, hh, :], in0=ps_c[hh // 2][:, (hh % 2) * 193:(hh % 2) * 193 + 128], scalar1=coef[:, 0, hh:hh + 1],
                                                scalar2=None, op0=ALU.mult), reads=[ps_cB[hh // 2], coefB[0]], writes=o32B)
                need_sel = qt >= 8
                if need_sel:
                    for hh in range(4):
                        src = ps_c[hh // 2][:, (hh % 2) * 193 + 129:(hh % 2) * 193 + 193]
                        if hh == 0:
                            P.op("dve", R.tensor_scalar(out=imp[:], in0=src, scalar1=den[:, 0, 0:1], scalar2=None, op0=ALU.mult),
                                 reads=[ps_cB[0], denB[0]], writes=impB)
                        else:
                            P.op("dve", R.scalar_tensor_tensor(out=imp[:], in0=src, scalar=den[:, 0, hh:hh + 1], in1=imp[:], op0=ALU.mult, op1=ALU.add),
                                 reads=[ps_cB[hh // 2], denB[0], impB], writes=impB)
                    for k, off in enumerate((128, 64, 0)):
                        P.op("pool", R.affine_select(out=ind[:, k, :], in_=ones64[:], pattern=[[-64, 64]], compare_op=ALU.is_ge, fill=0.0,
                                                     base=t0 - off, channel_multiplier=1), reads=onesB, writes=indB[k])
                    P.op("dve", R.tensor_tensor(out=addm[:], in0=ind[:, 0, :], in1=ind[:, 1, :], op=ALU.add), reads=[indB[0], indB[1]], writes=addB)
                    P.op("dve", R.tensor_scalar(out=addm[:], in0=addm[:], scalar1=-1e9, scalar2=-1e9, op0=ALU.mult, op1=ALU.add), reads=addB, writes=addB)
                    P.op("dve", R.scalar_tensor_tensor(out=addm[:], in0=ind[:, 2, :], scalar=3e9, in1=addm[:], op0=ALU.mult, op1=ALU.add),
                         reads=[indB[2], addB], writes=addB)
                    P.op("dve", R.tensor_tensor(out=imp[:], in0=imp[:], in1=ind[:, 0, :], op=ALU.mult), reads=[impB, indB[0]], writes=impB)
                    P.op("dve", R.tensor_tensor(out=imp[:], in0=imp[:], in1=addm[:], op=ALU.add), reads=[impB, addB], writes=impB)
                    P.op("dve", R.tensor_scalar(out=imp[:, 0:1], in0=imp[:, 0:1], scalar1=0.0, scalar2=3e9, op0=ALU.mult, op1=ALU.add), reads=impB, writes=impB)
                    P.op("dve", R.max(out=m8[:], in_=imp[:]), reads=impB, writes=m8B)
                    P.op("dve", R.match_replace(out=wk[:], in_to_replace=m8[:], in_values=imp[:], imm_value=-3e9), reads=[m8B, impB], writes=wkB)
                    P.op("dve", R.max(out=m8[:], in_=wk[:]), reads=wkB, writes=m8B)
                    P.op("dve", R.tensor_scalar(out=negm[:], in0=imp[:], scalar1=m8[:, 7:8], scalar2=None, op0=ALU.is_ge), reads=[impB, m8B], writes=negmB)
                    P.op("dve", R.tensor_scalar(out=negm[:], in0=negm[:], scalar1=-1.0, scalar2=NEGB, op0=ALU.add, op1=ALU.mult), reads=negmB, writes=negmB)
                    P.op("pe", R.transpose(ps_c[0][0:64, 0:128], negm[:], ident32[:]), reads=[negmB, idB], writes=ps_cB[0])
                    for a in range(4):
                        P.op("dve", R.tensor_copy(out=negT4[:, a, :], in_=ps_c[0][0:64, 0:128]), reads=ps_cB[0], writes=negTB)
                kts = list(range(max(0, qt - 4), qt + 1))
                for i, kt in enumerate(kts):
                    if kt == qt:
                        mask = ([[0, 4], [1, 128]], 0, -1)
                    elif kt == qt - 4:
                        mask = ([[0, 4], [-1, 128]], -1, 1)
                    else:
                        mask = None
                    att_tile(qt, kw_sb[:, kt * 128:(kt + 1) * 128], kwB, 128, vw_e[:, kt, :], vwB, ps_w, ps_wB, 129, i == 0, i == len(kts) - 1, mask)
                for kt in range(qt + 1):
                    mask = ([[0, 4], [1, 128]], 0, -1) if kt == qt else None
                    bias_kt = kt if (need_sel and kt < qt) else None
                    att_tile(qt, ks_sb[:, kt * 128:(kt + 1) * 128], ksB, 128, vs_e[:, kt, :], vsB, ps_e, ps_eB, 129, kt == 0, kt == qt, mask, bias_kt)
                for x, (pso, psoB) in ((1, (ps_e, ps_eB)), (2, (ps_w, ps_wB))):
                    dens(x, pso, psoB, 129)
                    P.op("dve", R.tensor_tensor(out=coef[:, x, :], in0=den[:, x, :], in1=gq[:, :, x], op=ALU.mult), reads=[denB[x], gB], writes=coefB[x])
                    for hh in range(4):
                        P.op("dve", R.scalar_tensor_tensor(out=o32[:, hh, :], in0=pso[hh // 2][:, (hh % 2) * 129:(hh % 2) * 129 + 128],
                                                           scalar=coef[:, x, hh:hh + 1], in1=o32[:, hh, :], op0=ALU.mult, op1=ALU.add),
                             reads=[psoB[hh // 2], coefB[x], o32B], writes=o32B)
                oi = C.oi % 2; C.oi += 1
                for hh in range(4):
                    P.op("pe", R.transpose(ps_c[1][:, hh * 128:(hh + 1) * 128], o32[:, hh, :], ident32[:]), reads=[o32B, idB], writes=ps_cB[1], inc=(hh == 3))
                P.op("act", R.copy(out=oT[oi][:].rearrange("p a t -> p (a t)"), in_=ps_c[1][:]), reads=ps_cB[1], writes=oTB[oi])
                dst = attnT[gl * 512:(gl + 1) * 512, qt * 128:(qt + 1) * 128].rearrange("(a d) t -> d a t", a=4)
                P.dma("sync", dst, oT[oi][:], reads=oTB[oi])
    P.barrier()


_uid = [0]

S = 4096
DK = 256
DV = 512
CH = 64
BLK = 512


def emit_gla(P, nc, projX, gzX, wg, ng, onT, selD, dbg_chunks=None, dbg_stage=9):
    _uid[0] += 1
    st = ExitStack()
    with st:
        def A(name, shape, dt):
            return st.enter_context(nc.sbuf_tensor("gla_%d_" % _uid[0] + name, shape, dt))

        def PS(name, shape, dt=F32):
            return st.enter_context(nc.psum_tensor("gla_%d_" % _uid[0] + name, shape, dt))

        ident32 = A("ident32", [128, 128], F32); idB = Buf()
        identb = A("identb", [128, 128], BF16)
        Um32 = A("Um32", [64, 64], F32); UmB = Buf()
        Ucs = A("Ucs", [64, 64], F32)
        onec = A("onec", [128, 1], F32); onecB = Buf()
        wg_sb = A("wg", [16, 512], F32); wgB = Buf()
        bgbc = A("bgbc", [64, 512], F32); bgB = Buf()
        ngbc = A("ngbc", [64, 512], F32); ngbcB = Buf()
        zb = [A("zb%d" % i, [64, DK], F32) for i in range(2)]; zbB = [Buf(), Buf()]
        qT = [A("qT%d" % i, [128, 2, BLK], BF16) for i in range(2)]; qTB = [Buf(), Buf()]
        kT = [A("kT%d" % i, [128, 2, BLK], BF16) for i in range(2)]; kTB = [Buf(), Buf()]
        vT = [A("vT%d" % i, [128, 4, BLK], BF16) for i in range(2)]; vTB = [Buf(), Buf()]
        gz = [A("gz%d" % i, [16, BLK], F32) for i in range(2)]; gzB = [Buf(), Buf()]
        qS = A("qS", [128, 2, BLK], BF16); kS = A("kS", [128, 2, BLK], BF16); vS = A("vS", [128, 4, BLK], BF16); stgB = [Buf(), Buf(), Buf()]
        sel = A("sel", [128, 2], F32); selB = Buf()
        lsp = [A("lsp%d" % i, [64, DK], F32) for i in range(2)]; lspB = [Buf(), Buf()]
        emt = [A("emt%d" % i, [64, DK], F32) for i in range(2)]; emtB = [Buf(), Buf()]
        kit = [A("kit%d" % i, [64, DK], BF16) for i in range(2)]; kitB = [Buf(), Buf()]
        vtok = [A("vtok%d" % i, [64, DV], BF16) for i in range(2)]; vtokB = [Buf(), Buf()]
        epT = [A("epT%d" % i, [128, 2, CH], F32) for i in range(2)]; epTB = [Buf(), Buf()]
        emT = [A("emT%d" % i, [128, 2, CH], F32) for i in range(2)]; emTB = [Buf(), Buf()]
        qd = [A("qd%d" % i, [128, 2, CH], BF16) for i in range(2)]; qdB = [Buf(), Buf()]
        kiT = [A("kiT%d" % i, [128, 2, CH], BF16) for i in range(2)]; kiTB = [Buf(), Buf()]
        Abf = [A("Abf%d" % i, [64, CH], BF16) for i in range(2)]; AbfB = [Buf(), Buf()]
        S32 = A("S32", [128, 2, DV], F32); S32B = [Buf(), Buf()]
        S16 = A("S16", [128, 2, DV], BF16); S16B = [Buf(), Buf()]
        tS = [A("tS%d" % i, [128, DV], F32) for i in range(2)]; tSB = [Buf(), Buf()]
        junk = A("junk", [64, DV], F32); junkB = Buf()
        ssq = [A("ssq%d" % i, [64, 1], F32) for i in range(2)]; ssqB = [Buf(), Buf()]
        on = [A("on%d" % i, [64, DV], BF16) for i in range(2)]; onB = [Buf(), Buf()]
        onst = [A("onst%d" % i, [128, 4, BLK], BF16) for i in range(2)]; onstB = [Buf(), Buf()]

        ps_z = PS("z", [128, 512]); ps_zB = PBuf()
        ps_bT = PS("bT", [128, 512]); ps_bTB = PBuf()
        ps_tr = PS("tr", [128, 512]); ps_trB = PBuf()
        ps_A = PS("A", [128, 512]); ps_AB = PBuf()
        ps_o = PS("o", [128, 512]); ps_oB = PBuf()
        ps_S = [PS("S%d" % i, [128, 512]) for i in range(2)]; ps_SB = [PBuf(), PBuf()]
        ps_ot = PS("ot", [128, 512]); ps_otB = PBuf()

        P.op("pool", R.memset(ident32[:], 1.0), writes=idB)
        P.op("pool", R.affine_select(out=ident32[:], in_=ident32[:], pattern=[[-1, 128]], compare_op=ALU.is_equal,
                                     fill=0.0, base=0, channel_multiplier=1), reads=idB, writes=idB)
        P.op("dve", R.tensor_copy(out=identb[:], in_=ident32[:]), reads=idB, writes=idB)
        P.op("pool", R.memset(Um32[:], 1.0), writes=UmB)
        P.op("pool", R.affine_select(out=Um32[:], in_=Um32[:], pattern=[[1, 64]], compare_op=ALU.is_ge, fill=0.0,
                                     base=0, channel_multiplier=-1), reads=UmB, writes=UmB)
        P.op("dve", R.tensor_scalar(out=Ucs[:], in0=Um32[:], scalar1=-1.0 / 16.0, scalar2=None, op0=ALU.mult), reads=UmB, writes=UmB)
        P.op("pool", R.memset(onec[:], 1.0), writes=onecB)
        P.dma("sync", sel[:], selD, writes=selB)
        P.dma("sync", wg_sb[:], wg[0:16, :], writes=wgB)
        P.dma("sync", bgbc[:], wg[16:80, :], writes=bgB)
        P.dma("sync", ngbc[:], ng, writes=ngbcB)

        def rows(base, n, blk, cand):
            r, o = divmod(blk * BLK, 2048)
            return projX[cand][r, base:base + n, o:o + BLK]

        def blend(dst, dstB, stage, stageB):
            P.op("dve", R.tensor_scalar(out=dst, in0=dst, scalar1=sel[:, 0:1], scalar2=None, op0=ALU.mult), reads=[dstB, selB], writes=dstB)
            P.op("dve", R.scalar_tensor_tensor(out=dst, in0=stage, scalar=sel[:, 1:2], in1=dst, op0=ALU.mult, op1=ALU.add),
                 reads=[stageB, dstB, selB], writes=dstB)

        cnt = 0
        for hl in range(2):
            for ac in range(2):
                P.op("pool", R.memset(S32[:, ac, :], 0.0), writes=S32B[ac])
                P.op("pool", R.memset(S16[:, ac, :], 0.0), writes=S16B[ac])
            for blk in range(S // BLK):
                bi = (hl * (S // BLK) + blk) % 2
                for ac in range(2):
                    P.dma("sync", qT[bi][:, ac, :], rows(hl * 256 + ac * 128, 128, blk, 0), writes=qTB[bi])
                    P.dma("sync", qS[:, ac, :], rows(hl * 256 + ac * 128, 128, blk, 1), writes=stgB[0])
                    P.dma("sync", kT[bi][:, ac, :], rows(512 + hl * 256 + ac * 128, 128, blk, 0), writes=kTB[bi])
                    P.dma("sync", kS[:, ac, :], rows(512 + hl * 256 + ac * 128, 128, blk, 1), writes=stgB[1])
                for dc in range(4):
                    P.dma("sync", vT[bi][:, dc, :], rows(1024 + hl * 512 + dc * 128, 128, blk, 0), writes=vTB[bi])
                    P.dma("sync", vS[:, dc, :], rows(1024 + hl * 512 + dc * 128, 128, blk, 1), writes=stgB[2])
                blend(qT[bi][:], qTB[bi], qS[:], stgB[0])
                blend(kT[bi][:], kTB[bi], kS[:], stgB[1])
                blend(vT[bi][:], vTB[bi], vS[:], stgB[2])
                r_, o_ = divmod(blk * BLK, 2048)
                P.dma("sync", gz[bi][0:16, :], gzX[r_, :, o_:o_ + BLK], writes=gzB[bi])
                for cc in range(BLK // CH):
                    if dbg_chunks is not None and cnt >= dbg_chunks:
                        continue
                    i2 = cnt % 2; cnt += 1
                    ts = slice(cc * CH, (cc + 1) * CH)
                    P.op("pe", R.matmul(ps_z[0:64, 0:DK], lhsT=gz[bi][:, ts], rhs=wg_sb[:, hl * DK:(hl + 1) * DK], start=True, stop=True),
                         reads=[gzB[bi], wgB], writes=ps_zB)
                    P.op("dve", R.tensor_tensor(out=zb[i2][:], in0=ps_z[0:64, 0:DK], in1=bgbc[:, hl * DK:(hl + 1) * DK], op=ALU.add), reads=[ps_zB, bgB], writes=zbB[i2])
                    P.op("act", R.activation(out=lsp[i2][:], in_=zb[i2][:], func=AF.Exp, scale=-1.0), reads=zbB[i2], writes=lspB[i2])
                    P.op("act", R.activation(out=lsp[i2][:], in_=lsp[i2][:], func=AF.Ln, bias=onec[0:64, :]), reads=[lspB[i2], onecB], writes=lspB[i2])
                    if dbg_stage < 2:
                        continue
                    P.op("pe", R.matmul(ps_z[0:64, 0:DK], lhsT=Ucs[:], rhs=lsp[i2][:], start=True, stop=True), reads=[UmB, lspB[i2]], writes=ps_zB)
                    for ac in range(2):
                        P.op("pe", R.matmul(ps_bT[:, ac * CH:(ac + 1) * CH], lhsT=lsp[i2][:, ac * 128:(ac + 1) * 128], rhs=Ucs[:], start=True, stop=True),
                             reads=[UmB, lspB[i2]], writes=ps_bTB)
                    P.op("act", R.activation(out=emt[i2][:], in_=ps_z[0:64, 0:DK], func=AF.Exp, scale=-1.0), reads=ps_zB, writes=emtB[i2])
                    bT3 = ps_bT[:, 0:2 * CH].rearrange("p (a c) -> p a c", a=2)
                    P.op("act", R.activation(out=epT[i2][:], in_=bT3, func=AF.Exp), reads=ps_bTB, writes=epTB[i2])
                    P.op("act", R.activation(out=emT[i2][:], in_=bT3, func=AF.Exp, scale=-1.0), reads=ps_bTB, writes=emTB[i2])
                    if dbg_stage < 3:
                        continue
                    trb = ps_tr[:].bitcast(BF16)
                    for ac in range(2):
                        P.op("pe", R.transpose(trb[0:64, ac * 128:(ac + 1) * 128], kT[bi][:, ac, ts], identb[:]), reads=[kTB[bi], idB], writes=ps_trB, inc=False)
                    for dc in range(4):
                        P.op("pe", R.transpose(trb[0:64, 256 + dc * 128:256 + (dc + 1) * 128], vT[bi][:, dc, ts], identb[:]), reads=[vTB[bi], idB], writes=ps_trB,
                             inc=(dc == 3))
                    P.op("dve", R.tensor_tensor(out=kit[i2][:], in0=trb[0:64, 0:256], in1=emt[i2][:], op=ALU.mult), reads=[ps_trB, emtB[i2]], writes=kitB[i2])
                    P.op("act", R.copy(out=vtok[i2][:], in_=trb[0:64, 256:768]), reads=ps_trB, writes=vtokB[i2])
                    if dbg_stage < 4:
                        continue
                    P.op("dve", R.scalar_tensor_tensor(out=qd[i2][:], in0=epT[i2][:], scalar=DK ** -0.5, in1=qT[bi][:, :, ts], op0=ALU.mult, op1=ALU.mult),
                         reads=[epTB[i2], qTB[bi]], writes=qdB[i2])
                    P.op("dve", R.tensor_tensor(out=kiT[i2][:], in0=emT[i2][:], in1=kT[bi][:, :, ts], op=ALU.mult), reads=[emTB[i2], kTB[bi]], writes=kiTB[i2])
                    for ac in range(2):
                        P.op("pe", R.matmul(ps_A[0:64, 0:CH], lhsT=kiT[i2][:, ac, :], rhs=qd[i2][:, ac, :], start=(ac == 0), stop=(ac == 1)),
                             reads=[kiTB[i2], qdB[i2]], writes=ps_AB, inc=(ac == 1))
                    P.op("dve", R.tensor_tensor(out=Abf[i2][:], in0=ps_A[0:64, 0:CH], in1=Um32[:], op=ALU.mult), reads=[ps_AB, UmB], writes=AbfB[i2])
                    if dbg_stage < 5:
                        continue
                    P.op("pe", R.matmul(ps_o[0:64, :], lhsT=Abf[i2][:], rhs=vtok[i2][:], start=True, stop=False), reads=[AbfB[i2], vtokB[i2]], writes=ps_oB, inc=False)
                    for ac in range(2):
                        P.op("pe", R.matmul(ps_o[0:64, :], lhsT=qd[i2][:, ac, :], rhs=S16[:, ac, :], start=False, stop=(ac == 1)),
                             reads=[qdB[i2], S16B[ac]], writes=ps_oB, inc=(ac == 1))
                    for ac in range(2):
                        P.op("pe", R.matmul(ps_S[ac][:], lhsT=kit[i2][:, ac * 128:(ac + 1) * 128], rhs=vtok[i2][:], start=True, stop=True),
                             reads=[kitB[i2], vtokB[i2]], writes=ps_SB[ac])
                        P.op("dve", R.tensor_tensor(out=tS[ac][:], in0=S32[:, ac, :], in1=ps_S[ac][:], op=ALU.add), reads=[S32B[ac], ps_SB[ac]], writes=tSB[ac])
                        eb = epT[i2][:, ac, CH - 1:CH]
                        P.op("act", R.activation(out=S32[:, ac, :], in_=tS[ac][:], func=AF.Identity, scale=eb), reads=[tSB[ac], epTB[i2]], writes=S32B[ac])
                        P.op("dve", R.tensor_scalar(out=S16[:, ac, :], in0=tS[ac][:], scalar1=eb, scalar2=None, op0=ALU.mult), reads=[tSB[ac], epTB[i2]], writes=S16B[ac])
                    if dbg_stage < 6:
                        continue
                    P.op("act", R.activation(out=junk[:], in_=ps_o[0:64, :], func=AF.Square), reads=ps_oB, writes=junkB)
                    P.op("dve", R.reduce_sum(out=ssq[i2][:], in_=junk[:], axis=AX.X), reads=junkB, writes=ssqB[i2])
                    P.op("dve", R.tensor_scalar(out=ssq[i2][:], in0=ssq[i2][:], scalar1=1.0 / DV, scalar2=1e-6, op0=ALU.mult, op1=ALU.add), reads=ssqB[i2], writes=ssqB[i2])
                    P.op("act", R.activation(out=ssq[i2][:], in_=ssq[i2][:], func=AF.Sqrt), reads=ssqB[i2], writes=ssqB[i2])
                    P.op("dve", R.reciprocal(out=ssq[i2][:], in_=ssq[i2][:]), reads=ssqB[i2], writes=ssqB[i2])
                    P.op("dve", R.scalar_tensor_tensor(out=on[i2][:], in0=ps_o[0:64, :], scalar=ssq[i2][:, 0:1], in1=ngbc[:], op0=ALU.mult, op1=ALU.mult),
                         reads=[ps_oB, ssqB[i2], ngbcB], writes=onB[i2])
                    otb = ps_ot[:].bitcast(BF16)
                    for dc in range(4):
                        P.op("pe", R.transpose(otb[:, dc * CH:(dc + 1) * CH], on[i2][:, dc * 128:(dc + 1) * 128], identb[0:64, 0:64]), reads=[onB[i2], idB], writes=ps_otB,
                             inc=(dc == 3))
                    P.op("dve", R.tensor_copy(out=onst[bi][:, :, ts], in_=otb[:, 0:4 * CH].rearrange("p (a c) -> p a c", a=4)), reads=ps_otB, writes=onstB[bi])
                r, o = divmod(blk * BLK, 2048)
                dst = onT[hl * 512:(hl + 1) * 512, blk * BLK:(blk + 1) * BLK].rearrange("(a p) t -> p a t", p=128)
                P.dma("sync", dst, onst[bi][:], reads=onstB[bi])
    P.barrier()

import numpy as np
import ml_dtypes


BF = ml_dtypes.bfloat16
NCORES = 8
PAIRS = [[0, 1], [2, 3], [4, 5], [6, 7]]
ALL8 = [list(range(8))]
NSA_F = 5168
GLA_F = 6160
NSA_FP = 5376
GLA_FP = 6400
HST = 32

def weight_specs():
    sp = []
    for j in range(2):
        sp += [("nsa_w_in%d" % j, 2048, NSA_FP), ("nsa_w_out%d" % j, 2048, 2048),
               ("cmp_w1k%d" % j, 4096, 128), ("cmp_w1v%d" % j, 4096, 128)]
    sp += [("conv_w_in", 2048, 4096), ("conv_w_out", 2048, 2048), ("gla_w_in", 2048, GLA_FP), ("gla_w_out", 2048, 2048)]
    for l in range(4):
        sp += [("ffn_w1_%d" % l, 2048, 8192), ("ffn_w2_%d" % l, 8192, 2048), ("ple_wg_%d" % l, 2048, 2048), ("ple_wp_%d" % l, 256, 2048)]
    return sp


def build_fused(dumps=False, stop=99):
    nc = bass.Bass("TRN2", target_bir_lowering=False)
    I = lambda n, s, dt=F32: nc.dram_tensor(n, list(s), dt, kind="ExternalInput").ap()
    O = lambda n, s, dt=F32: nc.dram_tensor(n, list(s), dt, kind="ExternalOutput").ap()
    N = lambda n, s, dt=F32: nc.dram_tensor(n, list(s), dt).ap()
    xT = I("xT", [D, NTOK])
    pT = I("pT", [4, 256, NTOK])
    selD = I("sel", [128, 2])
    gainsD = I("gains", [5, 128, 3, 16])
    small = {}
    for j in range(2):
        small["cmp_w2k%d" % j] = I("cmp_w2k%d" % j, [128, 128]); small["cmp_w2v%d" % j] = I("cmp_w2v%d" % j, [128, 128])
        small["posTk%d" % j] = I("posTk%d" % j, [128, 32]); small["posTv%d" % j] = I("posTv%d" % j, [128, 32])
    dwT = I("dwT", [128, 16, 31]); cvec = I("cvec", [128, 3, 16])
    gla_wg = I("gla_wg", [80, 512]); gla_ng = I("gla_ng", [64, 512])
    outT = O("outT", [D, NTOK])
    specs = weight_specs()
    shard = {}
    spec_d = {n: (k, f) for n, k, f in specs}
    bounce = {}
    full = {}
    WB = {n: Buf() for n, _, _ in specs}
    hA = N("hA", [D, NTOK]); hBt = N("hBt", [D, NTOK])
    proj_nsa = [N("proj_nsa%d" % j, [5120, NTOK], BF16) for j in range(2)]
    gate_nsa = [N("gate_nsa%d" % j, [64, NTOK]) for j in range(2)]
    NSA_PIECES = [(0, 2048), (2048, 4096), (4096, 5120)]
    projX_nsa = [[N("projX_nsa%d_%d" % (j, k), [2 * (r1 - r0), NTOK], BF16) for k, (r0, r1) in enumerate(NSA_PIECES)] for j in range(2)]
    gateX_nsa = [N("gateX_nsa%d" % j, [2 * 64, NTOK]) for j in range(2)]
    mix_loc = [N("mix_loc%d" % j, [1024, 4096], BF16) for j in range(3)]
    mixX = [N("mixX%d" % j, [2 * 1024, 4096], BF16) for j in range(3)]
    uT = N("uT", [D, NTOK], BF16)
    halo_loc = N("halo_loc", [D, HST], BF16); haloX = N("haloX", [2 * D, HST], BF16)
    proj_gla = N("proj_gla", [6144, NTOK], BF16); gz_loc = N("gz_loc", [16, NTOK])
    projX_gla = [N("projX_gla%d" % k, [2 * 2048, NTOK], BF16) for k in range(2)]; gzX = N("gzX", [2 * 16, NTOK])
    dump = {}
    if dumps:
        dump["hA"] = O("dump_hA", [D, 256]); dump["hB"] = O("dump_hB", [D, 256])
        dump["att0"] = O("dump_att0", [1024, 1024], BF16)
        dump["on"] = O("dump_on", [1024, 1024], BF16)
        dump["u"] = O("dump_u", [D, 256], BF16)

    with ExitStack() as st:
        P = Prog(nc, st)

        def v256(ap, R, n=1):
            if R % 256 == 0:
                return ap.rearrange("(a b) c -> a (b c)", a=n * 256)
            return ap.rearrange("r (a c) -> (r a) c", a=256 // R)

        def gather_w(names):
            for n in names:
                b = Buf()
                k_, f_ = spec_d[n]
                shard[n] = I(n + "_s", [k_ // 8, f_])
                bounce[n] = N(n + "_b", [k_ // 8, f_]); full[n] = N(n + "_f", [k_, f_])
                P.dma("sync", bounce[n], shard[n], writes=b)
                P.cc("AllGather", [v256(bounce[n], k_ // 8)], [v256(full[n], k_ // 8, 8)], ALL8, reads=b, writes=WB[n])

        def exchange(src, dst):
            R_ = src.shape[0]
            ev = P.cc("AllGather", [v256(src, R_)], [v256(dst, R_, 2)], PAIRS)
            P.wait_all(ev)

        def exchange_nsa(j):
            for k, (r0, r1) in enumerate(NSA_PIECES):
                exchange(proj_nsa[j][r0:r1, :], projX_nsa[j][k])
            exchange(gate_nsa[j], gateX_nsa[j])

        def nsa_pieces(j):
            return [(r0, r1, r2(projX_nsa[j][k], r1 - r0)) for k, (r0, r1) in enumerate(NSA_PIECES)]

        def r2(ap, n):
            return ap.rearrange("(r p) t -> r p t", r=2)

        def finish():
            if dumps:
                P.barrier()
                P.dma("sync", dump["att0"][:, 0:512], mix_loc[0][:, 0:512]); P.dma("sync", dump["att0"][:, 512:1024], mix_loc[0][:, 3584:4096])
                P.dma("sync", dump["on"][:, 0:512], mix_loc[1][:, 0:512]); P.dma("sync", dump["on"][:, 512:1024], mix_loc[1][:, 3584:4096])
                P.dma("sync", dump["u"], uT[:, 0:256])
                P.dma("sync", dump["hB"], hBt[:, 0:256]); P.dma("sync", dump["hA"], hA[:, 0:256])
            print("fused ninst", P.ninst, "stop", stop, flush=True)
            P.emit()

        gather_w(["nsa_w_in0", "cmp_w1k0", "cmp_w1v0"] + (["nsa_w_out0", "ffn_w1_0", "ffn_w2_0", "ple_wg_0", "ple_wp_0", "conv_w_in"] if stop >= 3 else []))
        emit_T(P, nc, dict(hT=xT, sel=selD, gains=gainsD[0], w_in=full["nsa_w_in0"], w_inB=WB["nsa_w_in0"], projT=proj_nsa[0], tail32=gate_nsa[0], hT_out=hA),
               None, False, "proj", F=NSA_F, F32TAIL=48)
        exchange_nsa(0)
        if stop <= 1:
            finish()
            return nc
        gather_w(["conv_w_out", "ffn_w1_1", "ffn_w2_1", "ple_wg_1", "ple_wp_1", "gla_w_in"] if stop >= 4 else [])
        cw = {"w1k": full["cmp_w1k0"], "w1kB": WB["cmp_w1k0"], "w1v": full["cmp_w1v0"], "w1vB": WB["cmp_w1v0"],
              "w2k": small["cmp_w2k0"], "w2v": small["cmp_w2v0"], "posTk": small["posTk0"], "posTv": small["posTv0"]}
        emit_nsa(P, nc, nsa_pieces(0), r2(gateX_nsa[0], 64), cw, mix_loc[0], selD)
        exchange(mix_loc[0], mixX[0])
        if stop <= 2:
            finish()
            return nc
        emit_T(P, nc, dict(hT=hA, sel=selD, gains=gainsD[1], mixX=r2(mixX[0], 1024), w_out=full["nsa_w_out0"], w_outB=WB["nsa_w_out0"],
                           w1=full["ffn_w1_0"], w1B=WB["ffn_w1_0"], w2=full["ffn_w2_0"], w2B=WB["ffn_w2_0"], wg=full["ple_wg_0"], wgB=WB["ple_wg_0"],
                           wp=full["ple_wp_0"], wpB=WB["ple_wp_0"], pT=pT[0], w_in=full["conv_w_in"], w_inB=WB["conv_w_in"], projT=uT, hT_out=hBt),
               "linear", True, "glu", F=4096)
        b = Buf()
        P.dma("sync", halo_loc, uT[:, NTOK - HST:NTOK], writes=b)
        P.barrier()
        exchange(halo_loc, haloX)
        if stop <= 3:
            finish()
            return nc
        gather_w(["gla_w_out", "ffn_w1_2", "ffn_w2_2", "ple_wg_2", "ple_wp_2", "nsa_w_in1", "cmp_w1k1", "cmp_w1v1"] if stop >= 6 else [])
        emit_T(P, nc, dict(hT=hBt, sel=selD, gains=gainsD[2], uT=uT, haloX=r2(haloX, D), dwT=dwT, cvec=cvec, w_out=full["conv_w_out"], w_outB=WB["conv_w_out"],
                           w1=full["ffn_w1_1"], w1B=WB["ffn_w1_1"], w2=full["ffn_w2_1"], w2B=WB["ffn_w2_1"], wg=full["ple_wg_1"], wgB=WB["ple_wg_1"],
                           wp=full["ple_wp_1"], wpB=WB["ple_wp_1"], pT=pT[1], w_in=full["gla_w_in"], w_inB=WB["gla_w_in"], projT=proj_gla, tail32=gz_loc, hT_out=hA),
               "conv", True, "proj", F=GLA_F, F32TAIL=16)
        exchange(proj_gla[0:2048, :], projX_gla[0]); exchange(proj_gla[2048:4096, :], projX_gla[1]); exchange(gz_loc, gzX)
        if stop <= 4:
            finish()
            return nc
        gather_w(["nsa_w_out1", "ffn_w1_3", "ffn_w2_3", "ple_wg_3", "ple_wp_3"] if stop >= 8 else [])
        emit_gla(P, nc, [r2(projX_gla[0], 2048), r2(projX_gla[1], 2048)], r2(gzX, 16), gla_wg, gla_ng, mix_loc[1], selD)
        exchange(mix_loc[1], mixX[1])
        if stop <= 5:
            finish()
            return nc
        emit_T(P, nc, dict(hT=hA, sel=selD, gains=gainsD[3], mixX=r2(mixX[1], 1024), rT=proj_gla[4096:6144, :], w_out=full["gla_w_out"], w_outB=WB["gla_w_out"],
                           w1=full["ffn_w1_2"], w1B=WB["ffn_w1_2"], w2=full["ffn_w2_2"], w2B=WB["ffn_w2_2"], wg=full["ple_wg_2"], wgB=WB["ple_wg_2"],
                           wp=full["ple_wp_2"], wpB=WB["ple_wp_2"], pT=pT[2], w_in=full["nsa_w_in1"], w_inB=WB["nsa_w_in1"], projT=proj_nsa[1], tail32=gate_nsa[1], hT_out=hBt),
               "gla", True, "proj", F=NSA_F, F32TAIL=48)
        exchange_nsa(1)
        if stop <= 6:
            finish()
            return nc
        cw = {"w1k": full["cmp_w1k1"], "w1kB": WB["cmp_w1k1"], "w1v": full["cmp_w1v1"], "w1vB": WB["cmp_w1v1"],
              "w2k": small["cmp_w2k1"], "w2v": small["cmp_w2v1"], "posTk": small["posTk1"], "posTv": small["posTv1"]}
        emit_nsa(P, nc, nsa_pieces(1), r2(gateX_nsa[1], 64), cw, mix_loc[2], selD)
        exchange(mix_loc[2], mixX[2])
        if stop <= 7:
            finish()
            return nc
        emit_T(P, nc, dict(hT=hBt, sel=selD, gains=gainsD[4], mixX=r2(mixX[2], 1024), w_out=full["nsa_w_out1"], w_outB=WB["nsa_w_out1"],
                           w1=full["ffn_w1_3"], w1B=WB["ffn_w1_3"], w2=full["ffn_w2_3"], w2B=WB["ffn_w2_3"], wg=full["ple_wg_3"], wgB=WB["ple_wg_3"],
                           wp=full["ple_wp_3"], wpB=WB["ple_wp_3"], pT=pT[3], outT=outT),
               "linear", True, "final")
        finish()
    return nc


def fm(v):
    return np.ascontiguousarray(v.reshape(-1, 128).T)


def nsa_col_perm():
    cols = []
    for s in range(2):
        for gl in range(2):
            g = 2 * s + gl
            cols += list(range(g * 512, (g + 1) * 512))
        for k in range(6):
            for gl in range(2):
                g = 2 * s + gl
                cols += list(range(2048 + k * 512 + g * 128, 2048 + k * 512 + (g + 1) * 128))
    cols += list(range(5120, 5168))
    return np.array(cols)


def gla_col_perm():
    cols = []
    for s in range(2):
        for hl in range(2):
            h = 2 * s + hl
            cols += list(range(h * 256, (h + 1) * 256))
        for hl in range(2):
            h = 2 * s + hl
            cols += list(range(1024 + h * 256, 1024 + (h + 1) * 256))
        for hl in range(2):
            h = 2 * s + hl
            cols += list(range(2048 + h * 512, 2048 + (h + 1) * 512))
    cols += list(range(4096, 6160))
    return np.array(cols)


def conv_col_perm():
    cols = []
    for fc in range(16):
        cols += list(range(2048 + fc * 128, 2048 + (fc + 1) * 128))
        cols += list(range(fc * 128, (fc + 1) * 128))
    return np.array(cols)


def make_inputs(inp, nc=None):
    f32 = np.float32
    W = {}
    pn, pg, pc = nsa_col_perm(), gla_col_perm(), conv_col_perm()
    for j in range(2):
        W["nsa_w_in%d" % j] = np.concatenate([inp["nsa_w_in"][j][:, pn], np.zeros((2048, NSA_FP - NSA_F), f32)], 1)
        W["nsa_w_out%d" % j] = inp["nsa_w_out"][j]
        W["cmp_w1k%d" % j] = inp["nsa_cmp_w1_k"][j]
        W["cmp_w1v%d" % j] = inp["nsa_cmp_w1_v"][j]
    W["conv_w_in"] = inp["conv_w_in"][0][:, pc]
    W["conv_w_out"] = inp["conv_w_out"][0]
    W["gla_w_in"] = np.concatenate([inp["gla_w_in"][0][:, pg], np.zeros((2048, GLA_FP - GLA_F), f32)], 1)
    W["gla_w_out"] = inp["gla_w_out"][0]
    for l in range(4):
        W["ffn_w1_%d" % l] = inp["ffn_w1"][l]; W["ffn_w2_%d" % l] = inp["ffn_w2"][l]
        W["ple_wg_%d" % l] = inp["ple_w_gate"][l]; W["ple_wp_%d" % l] = inp["ple_w_proj"][l]
    nm, nf, npl = inp["norm_mix"], inp["norm_ffn"], inp["norm_ple"]
    g = np.zeros((5, 128, 3, 16), f32)
    g[0, :, 2] = fm(nm[0])
    for k in range(1, 5):
        g[k, :, 0] = fm(nf[k - 1]); g[k, :, 1] = fm(npl[k - 1]); g[k, :, 2] = fm(nm[k]) if k < 4 else fm(inp["norm_final"])
    common = {"gains": g,
              "dwT": np.ascontiguousarray(inp["conv_dw"][0].T.reshape(16, 128, 31).transpose(1, 0, 2)).astype(f32),
              "cvec": np.ascontiguousarray(np.stack([fm(inp["conv_db"][0]), fm(inp["conv_ln_g"][0]), fm(inp["conv_ln_b"][0])], 1)).astype(f32),
              "gla_ng": np.ascontiguousarray(np.tile(inp["gla_norm_g"][0][None, :], (64, 1))).astype(f32)}
    for j in range(2):
        common["cmp_w2k%d" % j] = inp["nsa_cmp_w2_k"][j]; common["cmp_w2v%d" % j] = inp["nsa_cmp_w2_v"][j]
        common["posTk%d" % j] = np.ascontiguousarray(inp["nsa_cmp_pos_k"][j].T); common["posTv%d" % j] = np.ascontiguousarray(inp["nsa_cmp_pos_v"][j].T)
    maps = []
    for c in range(NCORES):
        b, r = c // 2, c % 2
        ts = slice(r * NTOK, (r + 1) * NTOK)
        m = dict(common)
        m["xT"] = np.ascontiguousarray(inp["x"][b, ts, :].T)
        m["pT"] = np.ascontiguousarray(inp["p"][:, b, ts, :].transpose(0, 2, 1))
        sel = np.zeros((128, 2), f32); sel[:, r] = 1.0
        m["sel"] = sel
        wgc = inp["gla_w_gate_up"][0][:, r * 512:(r + 1) * 512]
        bgc = inp["gla_b_gate"][0][r * 512:(r + 1) * 512]
        m["gla_wg"] = np.ascontiguousarray(np.concatenate([wgc, np.tile(bgc[None, :], (64, 1))], 0)).astype(f32)
        for n, k, f in weight_specs():
            m[n + "_s"] = np.ascontiguousarray(W[n][c * (k // 8):(c + 1) * (k // 8)])
        if nc is not None:
            names = set()
            import concourse.mybir as mybir_
            for alloc in nc.allocations:
                if isinstance(alloc, mybir_.MemoryLocationSet) and alloc.kind == "ExternalInput":
                    names.add(alloc.memorylocations[0].name)
            m = {k_: v_ for k_, v_ in m.items() if k_ in names}
        maps.append(m)
    return maps


def assemble(results):
    out = np.zeros((4, 4096, D), np.float32)
    for c in range(NCORES):
        b, r = c // 2, c % 2
        out[b, r * NTOK:(r + 1) * NTOK, :] = np.asarray(results[c]["outT"]).T
    return out

import os
import numpy as np
import ml_dtypes


BF = ml_dtypes.bfloat16
_cache = {}


def _run(nc, maps):
    return run_bass_kernel_spmd(nc, maps, core_ids=list(range(NCORES))).results


def build_T_prog(pre, tail, post, F=0, F32TAIL=0, FP=0):
    key = ("T", pre, tail, post, F)
    if key in _cache:
        return _cache[key]
    nc = bass.Bass("TRN2", target_bir_lowering=False)
    I = lambda n, s, dt=F32: nc.dram_tensor(n, list(s), dt, kind="ExternalInput").ap()
    O = lambda n, s, dt=F32: nc.dram_tensor(n, list(s), dt, kind="ExternalOutput").ap()
    io = dict(hT=I("hT", [D, NTOK]), sel=I("sel", [128, 2]), gains=I("gains", [128, 3, 16]))
    if pre in ("linear", "gla"):
        io["mixX"] = I("mixX", [2, 1024, 4096], BF16); io["w_out"] = I("w_out", [D, D])
        if pre == "gla":
            io["rT"] = I("rT", [D, NTOK], BF16)
    elif pre == "conv":
        io["uT"] = I("uT", [D, NTOK], BF16); io["haloX"] = I("haloX", [2, D, 32], BF16)
        io["dwT"] = I("dwT", [128, 16, 31]); io["cvec"] = I("cvec", [128, 3, 16]); io["w_out"] = I("w_out", [D, D])
    if tail:
        io["w1"] = I("w1", [D, 8192]); io["w2"] = I("w2", [8192, D]); io["wg"] = I("wg", [D, D]); io["wp"] = I("wp", [256, D])
        io["pT"] = I("pT", [256, NTOK])
    if post in ("proj", "glu"):
        io["w_in"] = I("w_in", [D, FP or F])
        FO = F // 2 if post == "glu" else (F // 128) * 128
        io["projT"] = O("projT", [FO, NTOK], BF16)
        if F32TAIL:
            io["tail32"] = O("tail32", [F32TAIL, NTOK])
        io["hT_out"] = O("hT_out", [D, NTOK])
    else:
        io["outT"] = O("outT", [D, NTOK])
    with ExitStack() as st:
        P = Prog(nc, st)
        emit_T(P, nc, io, pre, tail, post, F=F, F32TAIL=F32TAIL)
        print("T prog", pre, tail, post, "ninst", P.ninst, flush=True)
        P.emit()
    _cache[key] = nc
    return nc


def build_nsa_prog():
    if "nsa" in _cache:
        return _cache["nsa"]
    nc = bass.Bass("TRN2", target_bir_lowering=False)
    I = lambda n, s, dt=F32: nc.dram_tensor(n, list(s), dt, kind="ExternalInput").ap()
    projX = I("projX", [2, 5120, NTOK], BF16)
    gateX = I("gateX", [2, 64, NTOK])
    cw = {k: I(k, s) for k, s in (("w1k", [4096, 128]), ("w2k", [128, 128]), ("w1v", [4096, 128]), ("w2v", [128, 128]), ("posTk", [128, 32]), ("posTv", [128, 32]))}
    selD = I("sel", [128, 2])
    attnT = nc.dram_tensor("attnT", [1024, 4096], BF16, kind="ExternalOutput").ap()
    with ExitStack() as st:
        P = Prog(nc, st)
        emit_nsa(P, nc, [(0, 5120, projX)], gateX, cw, attnT, selD)
        print("nsa prog ninst", P.ninst, flush=True)
        P.emit()
    _cache["nsa"] = nc
    return nc


def build_gla_prog():
    if "gla" in _cache:
        return _cache["gla"]
    nc = bass.Bass("TRN2", target_bir_lowering=False)
    I = lambda n, s, dt=F32: nc.dram_tensor(n, list(s), dt, kind="ExternalInput").ap()
    p0 = I("projX0", [2, 2048, NTOK], BF16); p1 = I("projX1", [2, 2048, NTOK], BF16)
    gzX = I("gzX", [2, 16, NTOK]); wg = I("wg", [80, 512]); ng = I("ng", [64, 512]); selD = I("sel", [128, 2])
    onT = nc.dram_tensor("onT", [1024, 4096], BF16, kind="ExternalOutput").ap()
    with ExitStack() as st:
        P = Prog(nc, st)
        emit_gla(P, nc, [p0, p1], gzX, wg, ng, onT, selD)
        print("gla prog ninst", P.ninst, flush=True)
        P.emit()
    _cache["gla"] = nc
    return nc


def kernel_multi(inp):
    f32 = np.float32
    pn, pg, pc = nsa_col_perm(), gla_col_perm(), conv_col_perm()
    nm, nf, npl = inp["norm_mix"], inp["norm_ffn"], inp["norm_ple"]

    def gains(k):
        g = np.zeros((128, 3, 16), f32)
        if k > 0:
            g[:, 0] = fm(nf[k - 1]); g[:, 1] = fm(npl[k - 1])
        g[:, 2] = fm(nm[k]) if k < 4 else fm(inp["norm_final"])
        return g

    sels = []
    for c in range(NCORES):
        s = np.zeros((128, 2), f32); s[:, c % 2] = 1.0
        sels.append(s)

    def tok(c):
        return c // 2, slice((c % 2) * NTOK, (c % 2 + 1) * NTOK)

    def tail_w(l):
        return dict(w1=inp["ffn_w1"][l], w2=inp["ffn_w2"][l], wg=inp["ple_w_gate"][l], wp=inp["ple_w_proj"][l])

    def pT(l, c):
        b, ts = tok(c)
        return np.ascontiguousarray(inp["p"][l, b, ts, :].T)

    def pairstack(res, key, c):
        b0 = c // 2 * 2
        return np.ascontiguousarray(np.stack([np.asarray(res[b0][key]), np.asarray(res[b0 + 1][key])], 0))

    def nsa_launch(j, rT0):
        maps = []
        for c in range(NCORES):
            gx = np.zeros((2, 64, NTOK), f32)
            gx[:, :48] = pairstack(rT0, "tail32", c)
            maps.append(dict(projX=pairstack(rT0, "projT", c), gateX=gx, sel=sels[c],
                             w1k=inp["nsa_cmp_w1_k"][j], w2k=inp["nsa_cmp_w2_k"][j], w1v=inp["nsa_cmp_w1_v"][j], w2v=inp["nsa_cmp_w2_v"][j],
                             posTk=np.ascontiguousarray(inp["nsa_cmp_pos_k"][j].T), posTv=np.ascontiguousarray(inp["nsa_cmp_pos_v"][j].T)))
        return _run(build_nsa_prog(), maps)

    w_nsa = [np.ascontiguousarray(inp["nsa_w_in"][j][:, pn]) for j in range(2)]
    maps = []
    for c in range(NCORES):
        b, ts = tok(c)
        maps.append(dict(hT=np.ascontiguousarray(inp["x"][b, ts, :].T), sel=sels[c], gains=gains(0), w_in=w_nsa[0]))
    r0 = _run(build_T_prog(None, False, "proj", F=NSA_F, F32TAIL=48), maps)
    ra = nsa_launch(0, r0)
    maps = []
    for c in range(NCORES):
        m = dict(hT=np.asarray(r0[c]["hT_out"]), sel=sels[c], gains=gains(1), mixX=pairstack(ra, "attnT", c), w_out=inp["nsa_w_out"][0],
                 pT=pT(0, c), w_in=np.ascontiguousarray(inp["conv_w_in"][0][:, pc]))
        m.update(tail_w(0)); maps.append(m)
    r1 = _run(build_T_prog("linear", True, "glu", F=4096), maps)
    w_gla = np.ascontiguousarray(inp["gla_w_in"][0][:, pg])
    maps = []
    for c in range(NCORES):
        u = [np.asarray(r1[c // 2 * 2 + k]["projT"]) for k in range(2)]
        halo = np.ascontiguousarray(np.stack([u[0][:, NTOK - 32:], u[1][:, NTOK - 32:]], 0))
        m = dict(hT=np.asarray(r1[c]["hT_out"]), sel=sels[c], gains=gains(2), uT=u[c % 2], haloX=halo,
                 dwT=np.ascontiguousarray(inp["conv_dw"][0].T.reshape(16, 128, 31).transpose(1, 0, 2)).astype(f32),
                 cvec=np.ascontiguousarray(np.stack([fm(inp["conv_db"][0]), fm(inp["conv_ln_g"][0]), fm(inp["conv_ln_b"][0])], 1)).astype(f32),
                 w_out=inp["conv_w_out"][0], pT=pT(1, c), w_in=w_gla)
        m.update(tail_w(1)); maps.append(m)
    r2_ = _run(build_T_prog("conv", True, "proj", F=GLA_F, F32TAIL=16), maps)
    maps = []
    for c in range(NCORES):
        r = c % 2
        px = pairstack(r2_, "projT", c)
        wgc = inp["gla_w_gate_up"][0][:, r * 512:(r + 1) * 512]
        bgc = inp["gla_b_gate"][0][r * 512:(r + 1) * 512]
        maps.append(dict(projX0=np.ascontiguousarray(px[:, 0:2048]), projX1=np.ascontiguousarray(px[:, 2048:4096]), gzX=pairstack(r2_, "tail32", c),
                         wg=np.ascontiguousarray(np.concatenate([wgc, np.tile(bgc[None, :], (64, 1))], 0)).astype(f32),
                         ng=np.ascontiguousarray(np.tile(inp["gla_norm_g"][0][None, :], (64, 1))).astype(f32), sel=sels[c]))
    rg = _run(build_gla_prog(), maps)
    maps = []
    for c in range(NCORES):
        m = dict(hT=np.asarray(r2_[c]["hT_out"]), sel=sels[c], gains=gains(3), mixX=pairstack(rg, "onT", c),
                 rT=np.ascontiguousarray(np.asarray(r2_[c]["projT"])[4096:6144]), w_out=inp["gla_w_out"][0], pT=pT(2, c), w_in=w_nsa[1])
        m.update(tail_w(2)); maps.append(m)
    r3 = _run(build_T_prog("gla", True, "proj", F=NSA_F, F32TAIL=48), maps)
    rb = nsa_launch(1, r3)
    maps = []
    for c in range(NCORES):
        m = dict(hT=np.asarray(r3[c]["hT_out"]), sel=sels[c], gains=gains(4), mixX=pairstack(rb, "attnT", c), w_out=inp["nsa_w_out"][1], pT=pT(3, c))
        m.update(tail_w(3)); maps.append(m)
    r4 = _run(build_T_prog("linear", True, "final"), maps)
    out = np.zeros((4, 4096, D), f32)
    for c in range(NCORES):
        b, ts = tok(c)
        out[b, ts, :] = np.asarray(r4[c]["outT"]).T
    return out


_NC_CACHE = {}


def kernel(**inputs):
    inp = {k: np.asarray(v) for k, v in inputs.items()}
    return kernel_multi(inp).astype(np.float32)
```
